# Optimizing a Trainium2 kernel written in Bass

```python
import math
import jax
import jax.numpy as jnp
from jax import lax
import numpy as np

D_MODEL = 1024
BATCH = 8
SEQ = 4096
DEPTH = 4

CTX_LEN = 256
GRID_W = 64
N_MIXERS = 4
N_MOD = 6
NORM_EPS = 1e-6

S5_GROUP = 16
S5_GROUPS = D_MODEL // S5_GROUP
S5_STATE = 64
S5_CHUNK = 128

LRU_WIDTH = D_MODEL
LRU_BLOCKS = 4
LRU_BW = LRU_WIDTH // LRU_BLOCKS
LRU_CONV = 4
LRU_C = 8.0

RET_HEADS = 4
RET_DK = 256
RET_DV = 512
RET_CHUNK = 128
RET_HK = RET_HEADS * RET_DK
RET_HV = RET_HEADS * RET_DV
RET_IN = 2 * (RET_HK + RET_HV)
ROPE_BASE = 10000.0

GDN_K_HEADS = 8
GDN_V_HEADS = 16
GDN_DK = 128
GDN_DV = 128
GDN_CONV = 4
GDN_CHUNK = 64
GDN_QK = GDN_K_HEADS * GDN_DK
GDN_VV = GDN_V_HEADS * GDN_DV
GDN_NG = 2 * 2 * GDN_V_HEADS
GDN_STATE_COLS = GDN_QK + GDN_VV + GDN_NG
GDN_IN = GDN_STATE_COLS + GDN_QK + GDN_VV

D_FF = 2816
FFN_CONV = 3

kernel_name = 'hybrid_s5_rglru_retention_gdn_diffusion_trunk'


def _flip(t, d):
    return t if (t is None or d == 0) else jnp.flip(t, axis=1)


def rmsnorm(x, g):
    x32 = x.astype(jnp.float32)
    y = x32 * lax.rsqrt(jnp.mean(x32 * x32, axis=-1, keepdims=True) + NORM_EPS)
    return (y * g.astype(jnp.float32)).astype(x.dtype)


def centred_dwconv(x, w):
    k = w.shape[0]
    return lax.conv_general_dilated(x, w[:, None, :].astype(x.dtype), window_strides=(1,),
                                    padding=[(k // 2, k - 1 - k // 2)],
                                    dimension_numbers=('NWC', 'WIO', 'NWC'),
                                    feature_group_count=x.shape[-1])


def _real_combine(e1, e2):
    a1, b1 = e1
    a2, b2 = e2
    return a1 * a2, a2 * b1 + b2


def real_linear_scan(a, b, h0):
    a_cum, b_cum = lax.associative_scan(_real_combine, (a, b), axis=1)
    return b_cum + a_cum * h0[:, None]


def _complex_combine(e1, e2):
    a1r, a1i, b1r, b1i = e1
    a2r, a2i, b2r, b2i = e2
    return (a1r * a2r - a1i * a2i, a1r * a2i + a1i * a2r,
            a2r * b1r - a2i * b1i + b2r, a2r * b1i + a2i * b1r + b2i)


def s5_discretise(lam_re, lam_im, log_step, b_re, b_im):
    lr = jnp.minimum(lam_re.astype(jnp.float32), -1e-4)
    li = lam_im.astype(jnp.float32)
    step = jnp.exp(log_step.astype(jnp.float32))[:, None]
    mag = jnp.exp(lr * step)
    ar, ai = mag * jnp.cos(li * step), mag * jnp.sin(li * step)
    den = lr * lr + li * li
    kr = ((ar - 1.0) * lr + ai * li) / den
    ki = (ai * lr - (ar - 1.0) * li) / den
    br32, bi32 = b_re.astype(jnp.float32), b_im.astype(jnp.float32)
    br = kr[..., None] * br32 - ki[..., None] * bi32
    bi = kr[..., None] * bi32 + ki[..., None] * br32
    return (ar, ai), (br, bi)


def s5_scan(u, lam_bar, b_bar, c_mat, h0, emit):
    ar, ai = lam_bar
    br, bi = b_bar
    cr, ci = c_mat
    bn, length = u.shape[:2]
    t_blk = min(S5_CHUNK, length)
    n_blk = length // t_blk
    a_shape = (bn, t_blk, S5_GROUPS, S5_STATE)
    a_r = jnp.broadcast_to(ar, a_shape)
    a_i = jnp.broadcast_to(ai, a_shape)
    u_blocks = u.reshape(bn, n_blk, t_blk, S5_GROUPS, S5_GROUP).swapaxes(0, 1)

    def step(carry, u_blk):
        hr, hi = carry
        bur = jnp.einsum('btgc,gpc->btgp', u_blk, br)
        bui = jnp.einsum('btgc,gpc->btgp', u_blk, bi)
        cum_r, cum_i, sr, si = lax.associative_scan(_complex_combine, (a_r, a_i, bur, bui), axis=1)
        st_r = sr + cum_r * hr[:, None] - cum_i * hi[:, None]
        st_i = si + cum_r * hi[:, None] + cum_i * hr[:, None]
        y = None
        if emit:
            y = jnp.einsum('btgp,gcp->btgc', st_r, cr) - jnp.einsum('btgp,gcp->btgc', st_i, ci)
        return (st_r[:, -1], st_i[:, -1]), y

    h_last, ys = lax.scan(step, h0, u_blocks)
    y = ys.swapaxes(0, 1).reshape(bn, length, S5_GROUPS, S5_GROUP) if emit else None
    return y, h_last


def s5_mixer(h_lat, h_ctx, lam_re, lam_im, log_step, b_re, b_im, c_re, c_im, d_skip, w_glu, b_glu, ctx_out):
    dt = h_lat.dtype
    bn = h_lat.shape[0]

    def groups(h):
        return h.astype(jnp.float32).reshape(h.shape[0], h.shape[1], S5_GROUPS, S5_GROUP)

    u_lat, u_ctx = groups(h_lat), groups(h_ctx)
    zero = jnp.zeros((bn, S5_GROUPS, S5_STATE), jnp.float32)
    ys_lat, ys_ctx = [], []
    for d in range(2):
        lam_bar, b_bar = s5_discretise(lam_re[d], lam_im[d], log_step[d], b_re[d], b_im[d])
        c_mat = (c_re[d].astype(jnp.float32), c_im[d].astype(jnp.float32))
        y_c, h_c = s5_scan(_flip(u_ctx, d), lam_bar, b_bar, c_mat, (zero, zero), ctx_out)
        y_l, _ = s5_scan(_flip(u_lat, d), lam_bar, b_bar, c_mat, h_c, True)
        ys_lat.append(_flip(y_l, d))
        ys_ctx.append(_flip(y_c, d))
    d_g = d_skip.astype(jnp.float32).reshape(S5_GROUPS, S5_GROUP)

    def glu(y_f, y_b, u):
        y = (y_f + y_b + d_g * u).reshape(u.shape[0], u.shape[1], D_MODEL)
        z = jax.nn.gelu(y).astype(dt)
        val, gate = jnp.split(z @ w_glu + b_glu, 2, axis=-1)
        return val * jax.nn.sigmoid(gate)

    out_ctx = glu(ys_ctx[0], ys_ctx[1], u_ctx) if ctx_out else None
    return glu(ys_lat[0], ys_lat[1], u_lat), out_ctx


def blockdiag(x, w):
    bn, length, _ = x.shape
    y = jnp.einsum('blnc,ncd->blnd', x.reshape(bn, length, LRU_BLOCKS, LRU_BW), w)
    return y.reshape(bn, length, LRU_WIDTH)


def rglru_mixer(h_lat, h_ctx, w_in, conv_w, conv_b, w_a, b_a, w_x, b_x, lam, w_out, ctx_out):
    dt = h_lat.dtype
    wd = LRU_WIDTH

    def prep(h, need_out):
        p = h @ w_in[:, :(2 * wd if need_out else wd)]
        xc = centred_dwconv(p[..., :wd], conv_w) + conv_b
        y = jax.nn.gelu(p[..., wd:]) if need_out else None
        return xc, y

    xc_l, y_l = prep(h_lat, True)
    xc_c, y_c = prep(h_ctx, ctx_out)

    def gates(xc, d):
        r = jax.nn.sigmoid(blockdiag(xc, w_a[d]) + b_a[d]).astype(jnp.float32)
        i = jax.nn.sigmoid(blockdiag(xc, w_x[d]) + b_x[d]).astype(jnp.float32)
        log_a = -LRU_C * r * jax.nn.softplus(-lam[d].astype(jnp.float32))
        return jnp.exp(log_a), jnp.sqrt(-jnp.expm1(2.0 * log_a)) * (i * xc.astype(jnp.float32))

    h0 = jnp.zeros((h_lat.shape[0], wd), jnp.float32)
    hs_lat, hs_ctx = [], []
    for d in range(2):
        a_c, b_c = gates(xc_c, d)
        hc = real_linear_scan(_flip(a_c, d), _flip(b_c, d), h0)
        a_l, b_l = gates(xc_l, d)
        hl = real_linear_scan(_flip(a_l, d), _flip(b_l, d), hc[:, -1])
        hs_lat.append(_flip(hl, d))
        hs_ctx.append(_flip(hc, d))
    out_lat = (y_l * (hs_lat[0] + hs_lat[1]).astype(dt)) @ w_out
    out_ctx = (y_c * (hs_ctx[0] + hs_ctx[1]).astype(dt)) @ w_out if ctx_out else None
    return out_lat, out_ctx


def rope_rotate(x, pos):
    half = x.shape[-1] // 2
    inv_freq = ROPE_BASE ** (-jnp.arange(half, dtype=jnp.float32) / half)
    ang = pos[:, None] * inv_freq[None, :]
    cos = jnp.cos(ang)[None, :, None, :]
    sin = jnp.sin(ang)[None, :, None, :]
    x1 = x[..., :half].astype(jnp.float32)
    x2 = x[..., half:].astype(jnp.float32)
    return jnp.concatenate([x1 * cos - x2 * sin, x1 * sin + x2 * cos], axis=-1).astype(x.dtype)


def grid_rope(x):
    length = x.shape[1]
    rows = length // GRID_W
    t = jnp.arange(length, dtype=jnp.int32)
    row = (t // GRID_W).astype(jnp.float32) - (rows - 1) / 2.0
    col = (t % GRID_W).astype(jnp.float32) - (GRID_W - 1) / 2.0
    half = x.shape[-1] // 2
    return jnp.concatenate([rope_rotate(x[..., :half], row), rope_rotate(x[..., half:], col)], axis=-1)


def retention_scan(q, k, v, r0, strict):
    emit = q is not None
    bn, length = k.shape[:2]
    c = min(RET_CHUNK, length)
    n = length // c
    log_g = jnp.log1p(-jnp.power(2.0, -5.0 - jnp.arange(RET_HEADS, dtype=jnp.float32)))
    idx = jnp.arange(c, dtype=jnp.float32)
    diff = idx[:, None] - idx[None, :]
    mask = (diff > 0) if strict else (diff >= 0)
    dmask = jnp.where(mask[None], jnp.exp(jnp.where(mask, diff, 0.0)[None] * log_g[:, None, None]), 0.0)
    xi = jnp.exp((idx + 1.0)[:, None] * log_g[None, :])
    zeta = jnp.exp((c - 1.0 - idx)[:, None] * log_g[None, :])
    g_blk = jnp.exp(c * log_g)

    def blocks(t):
        return t.reshape(bn, n, c, *t.shape[2:]).swapaxes(0, 1)

    xs = (blocks(k), blocks(v)) + ((blocks(q),) if emit else ())

    def step(r, blk):
        kc, vc = blk[0], blk[1]
        r_new = g_blk[None, :, None, None] * r + jnp.einsum('bjhd,bjhe->bhde', kc * zeta[None, :, :, None], vc)
        o = None
        if emit:
            qc = blk[2]
            s = jnp.einsum('bihd,bjhd->bhij', qc, kc) * dmask[None]
            o = (jnp.einsum('bhij,bjhe->bihe', s, vc)
                 + jnp.einsum('bihd,bhde->bihe', qc, r) * xi[None, :, :, None])
        return r_new, o

    r_last, os = lax.scan(step, r0, xs)
    out = os.swapaxes(0, 1).reshape(bn, length, RET_HEADS, RET_DV) if emit else None
    return out, r_last


def retention_mixer(h_lat, h_ctx, w_in, norm_g, w_out, ctx_out):
    dt = h_lat.dtype

    def prep(h, need_out, rotate):
        bn, length, _ = h.shape
        p = h @ w_in[:, :(RET_IN if need_out else RET_HK + RET_HV)]
        k = p[..., :RET_HK].reshape(bn, length, RET_HEADS, RET_DK)
        v = p[..., RET_HK:RET_HK + RET_HV].reshape(bn, length, RET_HEADS, RET_DV).astype(jnp.float32)
        if rotate:
            k = grid_rope(k)
        q = None
        gate = None
        if need_out:
            q = p[..., RET_HK + RET_HV:2 * RET_HK + RET_HV].reshape(bn, length, RET_HEADS, RET_DK)
            if rotate:
                q = grid_rope(q)
            q = q.astype(jnp.float32) * (RET_DK ** -0.5)
            gate = p[..., 2 * RET_HK + RET_HV:]
        return q, k.astype(jnp.float32), v, gate

    q_l, k_l, v_l, gate_l = prep(h_lat, True, True)
    q_c, k_c, v_c, gate_c = prep(h_ctx, ctx_out, False)
    r0 = jnp.zeros((h_lat.shape[0], RET_HEADS, RET_DK, RET_DV), jnp.float32)
    o_lat, o_ctx = [], []
    for d in range(2):
        strict = d == 1
        oc, r_c = retention_scan(_flip(q_c, d), _flip(k_c, d), _flip(v_c, d), r0, strict)
        ol, _ = retention_scan(_flip(q_l, d), _flip(k_l, d), _flip(v_l, d), r_c, strict)
        o_lat.append(_flip(ol, d))
        o_ctx.append(_flip(oc, d))

    def head_out(o_f, o_b, gate):
        o = o_f + o_b
        o = o * lax.rsqrt(jnp.mean(o * o, axis=-1, keepdims=True) + NORM_EPS)
        o = (o.reshape(o.shape[0], o.shape[1], RET_HV) * norm_g.astype(jnp.float32)).astype(dt)
        return (jax.nn.silu(gate) * o) @ w_out

    out_ctx = head_out(o_ctx[0], o_ctx[1], gate_c) if ctx_out else None
    return head_out(o_lat[0], o_lat[1], gate_l), out_ctx


def l2norm(x):
    return x * lax.rsqrt(jnp.sum(x * x, axis=-1, keepdims=True) + 1e-6)


def gated_delta_chunked(q, k, v, g, beta, s0):
    emit = q is not None
    bn, length, nh = g.shape
    c = min(GDN_CHUNK, length)
    n = length // c

    def blocks(t):
        return jnp.moveaxis(t.reshape(bn, n, c, nh, *t.shape[3:]), 3, 1)

    k, v, beta = blocks(k), blocks(v), blocks(beta)
    g = jnp.cumsum(blocks(g), axis=-1)
    k_beta = k * beta[..., None]
    idx = jnp.arange(c)
    incl = idx[:, None] >= idx[None, :]
    strict = idx[:, None] > idx[None, :]
    gdiff = g[..., :, None] - g[..., None, :]
    decay = jnp.where(incl, jnp.exp(jnp.where(incl, gdiff, 0.0)), 0.0)
    m = jnp.where(strict, jnp.einsum('bhnid,bhnjd->bhnij', k_beta, k) * decay, 0.0)
    a_mat = m + jnp.eye(c, dtype=jnp.float32)
    u = lax.linalg.triangular_solve(a_mat, v * beta[..., None], left_side=True, lower=True, unit_diagonal=True)
    w = lax.linalg.triangular_solve(a_mat, k_beta * jnp.exp(g)[..., None], left_side=True, lower=True,
                                    unit_diagonal=True)
    xs = (k, u, w, g)
    if emit:
        xs = xs + (blocks(q) * (GDN_DK ** -0.5), decay)
    xs = tuple(jnp.moveaxis(t, 2, 0) for t in xs)

    def step(s, blk):
        kc, uc, wc, gc = blk[0], blk[1], blk[2], blk[3]
        v_new = uc - jnp.einsum('bhcd,bhde->bhce', wc, s)
        g_last = gc[..., -1]
        s_new = (s * jnp.exp(g_last)[..., None, None]
                 + jnp.einsum('bhcd,bhce->bhde', kc * jnp.exp(g_last[..., None] - gc)[..., None], v_new))
        o = None
        if emit:
            qc, dc = blk[4], blk[5]
            attn = jnp.einsum('bhid,bhjd->bhij', qc, kc) * dc
            o = (jnp.einsum('bhid,bhde->bhie', qc * jnp.exp(gc)[..., None], s)
                 + jnp.einsum('bhij,bhje->bhie', attn, v_new))
        return s_new, o

    s_last, os = lax.scan(step, s0, xs)
    out = os.transpose(1, 0, 3, 2, 4).reshape(bn, length, nh, GDN_DV) if emit else None
    return out, s_last


def gdn_mixer(h_lat, h_ctx, w_in, conv_w, a_log, dt_bias, norm_g, w_out, ctx_out):
    dt = h_lat.dtype
    rep = GDN_V_HEADS // GDN_K_HEADS

    def prep(h, need_out):
        bn, length, _ = h.shape
        p = h @ w_in[:, :(GDN_IN if need_out else GDN_STATE_COLS)]
        kv = jax.nn.silu(centred_dwconv(p[..., :GDN_QK + GDN_VV], conv_w[:, :GDN_QK + GDN_VV]))
        k = l2norm(kv[..., :GDN_QK].reshape(bn, length, GDN_K_HEADS, GDN_DK).astype(jnp.float32))
        k = jnp.repeat(k, rep, axis=2)
        v = kv[..., GDN_QK:].reshape(bn, length, GDN_V_HEADS, GDN_DV).astype(jnp.float32)
        ba = p[..., GDN_QK + GDN_VV:GDN_STATE_COLS].astype(jnp.float32).reshape(bn, length, 2, 2, GDN_V_HEADS)
        q = None
        z = None
        if need_out:
            q = jax.nn.silu(centred_dwconv(p[..., GDN_STATE_COLS:GDN_STATE_COLS + GDN_QK],
                                           conv_w[:, GDN_QK + GDN_VV:]))
            q = jnp.repeat(l2norm(q.reshape(bn, length, GDN_K_HEADS, GDN_DK).astype(jnp.float32)), rep, axis=2)
            z = p[..., GDN_STATE_COLS + GDN_QK:]
        return q, k, v, ba, z

    q_l, k_l, v_l, ba_l, z_l = prep(h_lat, True)
    q_c, k_c, v_c, ba_c, z_c = prep(h_ctx, ctx_out)
    s0 = jnp.zeros((h_lat.shape[0], GDN_V_HEADS, GDN_DK, GDN_DV), jnp.float32)
    o_lat, o_ctx = [], []
    for d in range(2):
        a_d = jnp.exp(a_log[d].astype(jnp.float32))
        dtb = dt_bias[d].astype(jnp.float32)

        def gates(ba):
            return (-a_d * jax.nn.softplus(ba[:, :, d, 1] + dtb), jax.nn.sigmoid(ba[:, :, d, 0]))

        g_c, b_c = gates(ba_c)
        g_l, b_l = gates(ba_l)
        oc, s_c = gated_delta_chunked(_flip(q_c, d), _flip(k_c, d), _flip(v_c, d), _flip(g_c, d), _flip(b_c, d), s0)
        ol, _ = gated_delta_chunked(_flip(q_l, d), _flip(k_l, d), _flip(v_l, d), _flip(g_l, d), _flip(b_l, d), s_c)
        o_lat.append(_flip(ol, d))
        o_ctx.append(_flip(oc, d))

    def head_out(o_f, o_b, z):
        o = o_f + o_b
        o = o * lax.rsqrt(jnp.mean(o * o, axis=-1, keepdims=True) + NORM_EPS) * norm_g.astype(jnp.float32)
        o = o.astype(dt) * jax.nn.silu(z).reshape(o.shape)
        return o.reshape(o.shape[0], o.shape[1], GDN_VV) @ w_out

    out_ctx = head_out(o_ctx[0], o_ctx[1], z_c) if ctx_out else None
    return head_out(o_lat[0], o_lat[1], z_l), out_ctx


def conv_ffn(h, w_in, conv_w, conv_b, w_out):
    gate, up = jnp.split(h @ w_in, 2, axis=-1)
    return (jax.nn.gelu(centred_dwconv(gate, conv_w) + conv_b) * up) @ w_out


def setup_inputs(seed: int = 0) -> dict:
    key = jax.random.key(seed)
    keys = iter(jax.random.split(key, 64))
    f32 = jnp.float32

    def nrm(shape, scale):
        return scale * jax.random.normal(next(keys), shape, f32)

    def unif(shape, lo, hi):
        return jax.random.uniform(next(keys), shape, f32, lo, hi)

    n_a, n_b, n_c, n_d = [len(range(m, DEPTH, N_MIXERS)) for m in range(N_MIXERS)]
    d = D_MODEL
    g, p, gc = S5_GROUPS, S5_STATE, S5_GROUP
    wd = LRU_WIDTH
    s5_n = jnp.arange(p, dtype=f32)
    lru_a0 = unif((n_b, 2, wd), 0.9, 0.999) ** (1.0 / LRU_C)
    gdn_dt = jnp.exp(unif((n_d, 2, GDN_V_HEADS), math.log(1e-3), math.log(1e-1)))
    return {
        'x': nrm((BATCH, SEQ, d), 1.0),
        'c': nrm((BATCH, d), 1.0),
        'ctx': nrm((BATCH, CTX_LEN, d), 1.0),
        'c_ctx': nrm((d,), 1.0),
        'norm1_g': 1.0 + nrm((DEPTH, d), 0.02),
        'norm2_g': 1.0 + nrm((DEPTH, d), 0.02),
        'mod_w': nrm((DEPTH, d, N_MOD * d), 0.5 * d ** -0.5),
        'mod_b': nrm((DEPTH, N_MOD * d), 0.02),
        'ffn_w_in': nrm((DEPTH, d, 2 * D_FF), d ** -0.5),
        'ffn_conv_w': nrm((DEPTH, FFN_CONV, D_FF), FFN_CONV ** -0.5),
        'ffn_conv_b': nrm((DEPTH, D_FF), 0.02),
        'ffn_w_out': nrm((DEPTH, D_FF, d), D_FF ** -0.5),
        's5_lam_re': -0.5 + nrm((n_a, 2, g, p), 0.01),
        's5_lam_im': math.pi * s5_n + nrm((n_a, 2, g, p), 0.01),
        's5_log_step': unif((n_a, 2, g), math.log(1e-3), math.log(1e-1)),
        's5_b_re': nrm((n_a, 2, g, p, gc), (2 * gc) ** -0.5),
        's5_b_im': nrm((n_a, 2, g, p, gc), (2 * gc) ** -0.5),
        's5_c_re': nrm((n_a, 2, g, gc, p), p ** -0.5),
        's5_c_im': nrm((n_a, 2, g, gc, p), p ** -0.5),
        's5_d': nrm((n_a, d), 1.0),
        's5_w_glu': nrm((n_a, d, 2 * d), d ** -0.5),
        's5_b_glu': nrm((n_a, 2 * d), 0.02),
        'lru_w_in': nrm((n_b, d, 2 * wd), d ** -0.5),
        'lru_conv_w': nrm((n_b, LRU_CONV, wd), LRU_CONV ** -0.5),
        'lru_conv_b': nrm((n_b, wd), 0.02),
        'lru_w_a': nrm((n_b, 2, LRU_BLOCKS, LRU_BW, LRU_BW), LRU_BW ** -0.5),
        'lru_b_a': nrm((n_b, 2, wd), 0.02),
        'lru_w_x': nrm((n_b, 2, LRU_BLOCKS, LRU_BW, LRU_BW), LRU_BW ** -0.5),
        'lru_b_x': nrm((n_b, 2, wd), 0.02),
        'lru_lam': jnp.log(lru_a0) - jnp.log1p(-lru_a0),
        'lru_w_out': nrm((n_b, wd, d), wd ** -0.5),
        'ret_w_in': nrm((n_c, d, RET_IN), d ** -0.5),
        'ret_norm_g': 1.0 + nrm((n_c, RET_HV), 0.02),
        'ret_w_out': nrm((n_c, RET_HV, d), RET_HV ** -0.5),
        'gdn_w_in': nrm((n_d, d, GDN_IN), d ** -0.5),
        'gdn_conv_w': nrm((n_d, GDN_CONV, 2 * GDN_QK + GDN_VV), GDN_CONV ** -0.5),
        'gdn_a_log': jnp.log(unif((n_d, 2, GDN_V_HEADS), 1.0, 16.0)),
        'gdn_dt_bias': gdn_dt + jnp.log(-jnp.expm1(-gdn_dt)),
        'gdn_norm_g': 1.0 + nrm((n_d, GDN_DV), 0.02),
        'gdn_w_out': nrm((n_d, GDN_VV, d), GDN_VV ** -0.5),
        'final_norm_g': 1.0 + nrm((d,), 0.02),
    }


def reference(x, c, ctx, c_ctx, norm1_g, norm2_g, mod_w, mod_b, ffn_w_in, ffn_conv_w, ffn_conv_b, ffn_w_out,
              s5_lam_re, s5_lam_im, s5_log_step, s5_b_re, s5_b_im, s5_c_re, s5_c_im, s5_d, s5_w_glu, s5_b_glu,
              lru_w_in, lru_conv_w, lru_conv_b, lru_w_a, lru_b_a, lru_w_x, lru_b_x, lru_lam, lru_w_out,
              ret_w_in, ret_norm_g, ret_w_out,
              gdn_w_in, gdn_conv_w, gdn_a_log, gdn_dt_bias, gdn_norm_g, gdn_w_out,
              final_norm_g):
    for i in range(DEPTH):
        kind, j = i % N_MIXERS, i // N_MIXERS
        ctx_out = i < DEPTH - 1
        mod = jax.nn.silu(c) @ mod_w[i] + mod_b[i]
        mod_c = jax.nn.silu(c_ctx) @ mod_w[i] + mod_b[i]
        sh1, sc1, g1, sh2, sc2, g2 = jnp.split(mod[:, None, :], N_MOD, axis=-1)
        csh1, csc1, cg1, csh2, csc2, cg2 = jnp.split(mod_c[None, None, :], N_MOD, axis=-1)
        h_lat = rmsnorm(x, norm1_g[i]) * (1.0 + sc1) + sh1
        h_ctx = rmsnorm(ctx, norm1_g[i]) * (1.0 + csc1) + csh1
        if kind == 0:
            y_lat, y_ctx = s5_mixer(h_lat, h_ctx, s5_lam_re[j], s5_lam_im[j], s5_log_step[j], s5_b_re[j],
                                    s5_b_im[j], s5_c_re[j], s5_c_im[j], s5_d[j], s5_w_glu[j], s5_b_glu[j], ctx_out)
        elif kind == 1:
            y_lat, y_ctx = rglru_mixer(h_lat, h_ctx, lru_w_in[j], lru_conv_w[j], lru_conv_b[j], lru_w_a[j],
                                       lru_b_a[j], lru_w_x[j], lru_b_x[j], lru_lam[j], lru_w_out[j], ctx_out)
        elif kind == 2:
            y_lat, y_ctx = retention_mixer(h_lat, h_ctx, ret_w_in[j], ret_norm_g[j], ret_w_out[j], ctx_out)
        else:
            y_lat, y_ctx = gdn_mixer(h_lat, h_ctx, gdn_w_in[j], gdn_conv_w[j], gdn_a_log[j], gdn_dt_bias[j],
                                     gdn_norm_g[j], gdn_w_out[j], ctx_out)
        x = x + g1 * y_lat
        x = x + g2 * conv_ffn(rmsnorm(x, norm2_g[i]) * (1.0 + sc2) + sh2,
                              ffn_w_in[i], ffn_conv_w[i], ffn_conv_b[i], ffn_w_out[i])
        if ctx_out:
            ctx = ctx + cg1 * y_ctx
            ctx = ctx + cg2 * conv_ffn(rmsnorm(ctx, norm2_g[i]) * (1.0 + csc2) + csh2,
                                       ffn_w_in[i], ffn_conv_w[i], ffn_conv_b[i], ffn_w_out[i])
    return rmsnorm(x, final_norm_g)
```

```python
import math
from contextlib import ExitStack

import numpy as np
import concourse.bass as bass
import concourse.mybir as mybir
from concourse.bass_utils import run_bass_kernel_spmd

F32 = mybir.dt.float32
BF16 = mybir.dt.bfloat16
I32 = mybir.dt.int32
AF = mybir.ActivationFunctionType
ALU = mybir.AluOpType
AX = mybir.AxisListType

D = 1024
L = 4096
LC = 256
NT = L + LC
DEPTH = 4
DFF = 2816
NFC = DFF // 128
EPS = 1e-6
HT_CTX0 = 1
HT_LAT0 = 259
HT_COLS = 4360
NS_DMA = 24


def ht_col(tok):
    return HT_CTX0 + tok if tok < LC else HT_LAT0 + (tok - LC)


class Prog:
    def __init__(self, nc, stack):
        self.nc = nc
        self.streams = {e: [] for e in ("pe", "act", "dve", "pool", "sp")}
        self.csem = {e: stack.enter_context(nc.semaphore("c_" + e)) for e in ("pe", "act", "dve", "pool")}
        self.ccount = {e: 0 for e in self.csem}
        self.dsem = {q: [stack.enter_context(nc.semaphore(f"d_{q}{i}")) for i in range(NS_DMA)] for q in ("sp", "pool")}
        self.dcount = {"sp": 0, "pool": 0}
        self.semobj = {}
        for e, s in self.csem.items():
            self.semobj[("c", e)] = s
        for q, lst in self.dsem.items():
            for i, s in enumerate(lst):
                self.semobj[("d", q, i)] = s
        self.waited = {e: {} for e in self.streams}
        self.lastw = {}
        self.readers = {}
        self.n_ops = 0

    def _deps(self, r, w):
        deps = {}

        def add(tok):
            sk, v = tok
            if deps.get(sk, 0) < v:
                deps[sk] = v

        for k in list(r) + list(w):
            t = self.lastw.get(k)
            if t is not None:
                add(t)
        for k in w:
            for sk, v in self.readers.get(k, {}).items():
                add((sk, v))
        return deps

    def _record(self, r, w, tok):
        sk, v = tok
        for k in r:
            d = self.readers.setdefault(k, {})
            if d.get(sk, 0) < v:
                d[sk] = v
        for k in w:
            self.lastw[k] = tok
            self.readers[k] = {}

    def _waits(self, eng, deps):
        out = []
        wd = self.waited[eng]
        for sk, v in deps.items():
            if eng == "pe" and sk == ("c", "pe"):
                continue
            if wd.get(sk, 0) >= v:
                continue
            wd[sk] = v
            out.append((self.semobj[sk], v))
        return out

    def op(self, eng, fn, r=(), w=()):
        deps = self._deps(r, w)
        self.ccount[eng] += 1
        tok = (("c", eng), self.ccount[eng])
        self.streams[eng].append((self._waits(eng, deps), fn, (self.csem[eng], 1)))
        self._record(r, w, tok)
        self.n_ops += 1

    def dma(self, q, out, in_, r=(), w=()):
        k = self.dcount[q]
        self.dcount[q] += 1
        slot = k % NS_DMA
        val = 16 * (k // NS_DMA + 1)
        sk = ("d", q, slot)
        deps = self._deps(r, w)
        if k >= NS_DMA and deps.get(sk, 0) < val - 16:
            deps[sk] = val - 16
        self.streams[q].append((self._waits(q, deps), lambda e: e.dma_start(out=out, in_=in_), (self.semobj[sk], 16)))
        self._record(r, w, (sk, val))
        self.n_ops += 1

    def barrier(self):
        toks = {}
        for q in ("sp", "pool"):
            k = self.dcount[q]
            for slot in range(NS_DMA):
                n = (k - slot + NS_DMA - 1) // NS_DMA if k > slot else 0
                if n > 0:
                    toks[("d", q, slot)] = 16 * n
        for e, c in self.ccount.items():
            if c > 0:
                toks[("c", e)] = c
        for eng in self.streams:
            w = self._waits(eng, dict(toks)) if eng != "pe" else self._waits_pe_barrier(toks)
            if w:
                self.streams[eng].append((w, None, None))

    def _waits_pe_barrier(self, toks):
        t = dict(toks)
        t.pop(("c", "pe"), None)
        return self._waits("pe", t)

    def finish(self):
        waits = []
        for q in ("sp", "pool"):
            k = self.dcount[q]
            for slot in range(NS_DMA):
                n = (k - slot + NS_DMA - 1) // NS_DMA if k > slot else 0
                if n > 0:
                    waits.append((self.semobj[("d", q, slot)], 16 * n))
        for e, c in self.ccount.items():
            if c > 0:
                waits.append((self.csem[e], c))
        self.streams["sp"].append((waits, None, None))

    def emit(self):
        nc = self.nc
        with nc.Block() as block:
            for name, deco in (("pe", block.tensor), ("act", block.scalar), ("dve", block.vector),
                               ("pool", block.gpsimd), ("sp", block.sync)):
                lst = self.streams[name]
                if not lst:
                    continue

                def body(e, lst=lst):
                    for waits, fn, sig in lst:
                        for sem, val in waits:
                            e.wait_ge(sem, val)
                        if fn is not None:
                            ins = fn(e)
                            ins.then_inc(sig[0], sig[1])

                deco(body)


def bkeys(name, lo, hi, g=128):
    return [(name, b) for b in range(lo // g, (hi - 1) // g + 1)]


class K:
    def __init__(self, nc, stack, cfg):
        self.nc = nc
        self.stack = stack
        self.cfg = cfg
        self.P = Prog(nc, stack)
        self.dram = {}
        self.uid = 0

    def din(self, name, shape, dt=F32):
        t = self.nc.dram_tensor(name, list(shape), dt, kind="ExternalInput").ap()
        self.dram[name] = t
        return t

    def dout(self, name, shape, dt=F32):
        t = self.nc.dram_tensor(name, list(shape), dt, kind="ExternalOutput").ap()
        self.dram[name] = t
        return t

    def dscr(self, name, shape, dt=F32):
        t = self.nc.dram_tensor(name, list(shape), dt).ap()
        self.dram[name] = t
        return t

    def sb(self, name, shape, dt=F32, stack=None):
        self.uid += 1
        return (stack or self.stack).enter_context(self.nc.sbuf_tensor(f"{name}_{self.uid}", list(shape), dt))

    def ps(self, name, shape=(128, 512), dt=F32, stack=None):
        return (stack or self.stack).enter_context(self.nc.psum_tensor(name, list(shape), dt))

    def setup_common(self):
        P = self.P
        self.ones = self.sb("ones", [128, 128])
        P.op("dve", lambda e: e.memset(self.ones[:], 1.0), w=["ones"])
        self.psum = [self.ps(f"ps{i}") for i in range(8)]
        self.n1g = self.sb("n1g", [128, DEPTH, 8])
        self.n2g = self.sb("n2g", [128, DEPTH, 8])
        self.fng = self.sb("fng", [128, 8])
        self.modb = self.sb("modb", [128, DEPTH, 48])
        self.ccol = self.sb("ccol", [128, 8, 2])
        self.scol = self.sb("scol", [128, 8, 2])
        for nm, t in (("n1g", self.n1g), ("n2g", self.n2g), ("fng", self.fng), ("modb", self.modb), ("ccol", self.ccol)):
            src = self.din("i_" + nm, list(t[:].shape))
            P.dma("sp", t[:], src, w=[nm])
        P.op("act", lambda e: e.activation(out=self.scol[:], in_=self.ccol[:], func=AF.Silu), r=["ccol"], w=["scol"])
        self.modv = self.sb("modv", [128, 48, 2])
        self.A1 = self.sb("A1", [128, 8, 2])
        self.A2 = self.sb("A2", [128, 8, 2])
        self.epsc = self.sb("epsc", [128, 1])
        P.op("dve", lambda e: e.memset(self.epsc[:], EPS), w=["epsc"])

    def alloc_ht(self, st):
        P = self.P
        self.HT = self.sb("HT", [128, 8, HT_COLS], BF16, st)
        HT = self.HT
        for c in (0, 257, 258, 4355):
            P.op("dve", lambda e, c=c, HT=HT: e.memset(HT[:, :, c:c + 1], 0.0), w=bkeys("HT", c, c + 1))

    def mod_phase(self, li):
        P = self.P
        mw = self.dram["mod_w"][li].rearrange("(kc p) n -> p kc n", p=128)
        ps = self.psum[7]
        with ExitStack() as st:
            wb = [self.sb(f"modw{i}", [128, 8, 512], F32, st) for i in range(2)]
            for pc in range(12):
                b = wb[pc % 2]
                bk = f"modw{pc % 2}"
                P.dma("sp", b[:], mw[:, :, pc * 512:(pc + 1) * 512], w=[bk])
                for jj in range(4):
                    j = pc * 4 + jj

                    def mm(e, b=b, jj=jj, j=j):
                        ins = None
                        for kc in range(8):
                            ins = e.matmul(ps[:, 2 * j:2 * j + 2], lhsT=b[:, kc, jj * 128:(jj + 1) * 128],
                                           rhs=self.scol[:, kc, :], start=(kc == 0), stop=(kc == 7))
                        return ins

                    P.op("pe", mm, r=[bk, "scol"], w=["ps7"])
            P.op("dve", lambda e: e.tensor_tensor(
                out=self.modv[:], in0=ps[:, 0:96].rearrange("p (j s) -> p j s", s=2),
                in1=self.modb[:, li, :].unsqueeze(2).to_broadcast([128, 48, 2]), op=ALU.add),
                r=["ps7", "modb"], w=["modv"])
            P.op("dve", lambda e: e.scalar_tensor_tensor(
                out=self.A1[:], in0=self.modv[:, 8:16, :], scalar=1.0,
                in1=self.n1g[:, li, :].unsqueeze(2).to_broadcast([128, 8, 2]), op0=ALU.add, op1=ALU.mult),
                r=["modv", "n1g"], w=["A1"])
            P.op("dve", lambda e: e.scalar_tensor_tensor(
                out=self.A2[:], in0=self.modv[:, 32:40, :], scalar=1.0,
                in1=self.n2g[:, li, :].unsqueeze(2).to_broadcast([128, 8, 2]), op0=ALU.add, op1=ALU.mult),
                r=["modv", "n2g"], w=["A2"])
            P.barrier()

    def xview(self, seg):
        name = "XL" if seg == 1 else "XC"
        return self.dram[name].rearrange("(kc p) t -> p kc t", p=128), name

    def tiles(self, n_lat=512, n_ctx=256, ctx=True):
        out = []
        if ctx:
            for t0 in range(0, LC, n_ctx):
                out.append((0, t0, n_ctx))
        for t0 in range(0, L, n_lat):
            out.append((1, t0, n_lat))
        return out

    def rstd_tile(self, st_bufs, xt, n, tagk):
        P = self.P
        sq, rs, rinv = st_bufs
        ps = self.psum[6]
        P.op("act", lambda e: e.activation(out=sq[:, :, :n], in_=xt[:, :, :n], func=AF.Square), r=[tagk], w=["nsq"])

        def mm(e):
            ins = None
            for kc in range(8):
                ins = e.matmul(ps[:, :n], lhsT=self.ones[:], rhs=sq[:, kc, :n], start=(kc == 0), stop=(kc == 7))
            return ins

        P.op("pe", mm, r=["nsq", "ones"], w=["ps6"])
        P.op("act", lambda e: e.activation(out=rs[:, :n], in_=ps[:, :n], func=AF.Sqrt, scale=1.0 / D, bias=self.epsc[:]),
             r=["ps6", "epsc"], w=["nrs"])
        P.op("dve", lambda e: e.reciprocal(out=rinv[:, :n], in_=rs[:, :n]), r=["nrs"], w=["nrinv"])

    def norm_phase(self, A, Bsl, ctx=True, h32=None, write_ht=True):
        P = self.P
        HT = self.HT if write_ht else None
        with ExitStack() as st:
            xts = [self.sb(f"nx{i}", [128, 8, 512], F32, st) for i in range(2)]
            sq = self.sb("nsq", [128, 8, 512], F32, st)
            rs = self.sb("nrs", [128, 512], F32, st)
            rinv = self.sb("nrinv", [128, 512], F32, st)
            tmp = self.sb("ntmp", [128, 8, 512], F32, st)
            if h32 is not None:
                self.sb32 = self.sb("h32t", [128, 8, 512], F32, st)
            for it, (seg, t0, n) in enumerate(self.tiles(ctx=ctx)):
                xt = xts[it % 2]
                xk = f"nx{it % 2}"
                xv, xname = self.xview(seg)
                P.dma("sp", xt[:, :, :n], xv[:, :, t0:t0 + n], r=bkeys(xname, t0, t0 + n), w=[xk])
                self.rstd_tile((sq, rs, rinv), xt, n, xk)
                P.op("dve", lambda e, xt=xt, n=n: e.tensor_tensor(
                    out=tmp[:, :, :n], in0=xt[:, :, :n], in1=rinv[:, :n].unsqueeze(1).to_broadcast([128, 8, n]), op=ALU.mult),
                    r=[xk, "nrinv"], w=["ntmp"])
                tok0 = t0 if seg == 0 else LC + t0
                c0 = ht_col(tok0)
                s = 1 if seg == 0 else 0
                for kc in range(8 if write_ht else 0):
                    P.op("act" if kc % 2 else "dve",
                         (lambda e, kc=kc, n=n, c0=c0, s=s: e.activation(
                             out=HT[:, kc, c0:c0 + n], in_=tmp[:, kc, :n], func=AF.Identity,
                             scale=A[:, kc, s:s + 1], bias=self.modv[:, Bsl + kc, s:s + 1])) if kc % 2 else
                         (lambda e, kc=kc, n=n, c0=c0, s=s: e.tensor_scalar(
                             out=HT[:, kc, c0:c0 + n], in0=tmp[:, kc, :n],
                             scalar1=A[:, kc, s:s + 1], scalar2=self.modv[:, Bsl + kc, s:s + 1], op0=ALU.mult, op1=ALU.add)),
                         r=["ntmp", "A1", "A2", "modv"], w=bkeys("HT", c0, c0 + n))
                if h32 is not None:
                    h32t = self.sb32
                    for kc in range(8):
                        P.op("dve", lambda e, kc=kc, n=n, s=s: e.tensor_scalar(
                            out=h32t[:, kc, :n], in0=tmp[:, kc, :n], scalar1=A[:, kc, s:s + 1],
                            scalar2=self.modv[:, Bsl + kc, s:s + 1], op0=ALU.mult, op1=ALU.add),
                            r=["ntmp", "A1", "A2", "modv"], w=["h32t"])
                    P.dma("sp", h32.rearrange("(kc p) t -> p kc t", p=128)[:, :, tok0:tok0 + n], h32t[:, :, :n],
                          r=["h32t"], w=bkeys("H32", tok0, tok0 + n))
            P.barrier()


    def ffn_phase(self, li, ctx=True):
        P = self.P
        with ExitStack() as hst:
            self.alloc_ht(hst)
            self.norm_phase(self.A2, 24, ctx=ctx)
            self._ffn1(li, ctx)
        self._ffn2(li, ctx)

    def _ffn1(self, li, ctx):
        P = self.P
        HT = self.HT
        AT = self.dram["AT"]
        win = self.dram["ffn_w_in"][li].rearrange("(kc p) n -> p kc n", p=128)
        tiles = self.tiles(n_lat=256, n_ctx=256, ctx=ctx)
        with ExitStack() as st:
            wg = [self.sb(f"fwg{i}", [128, 8, 512], BF16, st) for i in range(2)]
            wu = [self.sb(f"fwu{i}", [128, 8, 512], BF16, st) for i in range(2)]
            cw = self.sb("fcw", [128, NFC, 3], F32, st)
            cb = self.sb("fcb", [128, NFC], F32, st)
            P.dma("sp", cw[:], self.dram["i_ffn_cw"][li], w=["fcw"])
            P.dma("sp", cb[:], self.dram["i_ffn_cb"][li], w=["fcb"])
            t1 = [self.sb(f"ft1{i}", [128, 256], F32, st) for i in range(2)]
            t2 = [self.sb(f"ft2{i}", [128, 256], F32, st) for i in range(2)]
            ge = [self.sb(f"fge{i}", [128, 256], F32, st) for i in range(2)]
            ast = [self.sb(f"fa{i}", [128, 256], BF16, st) for i in range(2)]
            cnt = 0
            for g in range((NFC + 3) // 4):
                nfc = min(4, NFC - 4 * g)
                gb = g % 2
                P.dma("pool", wg[gb][:, :, :nfc * 128], win[:, :, g * 512:g * 512 + nfc * 128], w=[f"fwg{gb}"])
                P.dma("pool", wu[gb][:, :, :nfc * 128], win[:, :, DFF + g * 512:DFF + g * 512 + nfc * 128], w=[f"fwu{gb}"])
                for fl in range(nfc):
                    fc = 4 * g + fl
                    for (seg, t0, n) in tiles:
                        tok0 = t0 if seg == 0 else LC + t0
                        c0 = ht_col(tok0)
                        b = cnt % 2
                        cnt += 1
                        psG, psU = self.psum[2 * b], self.psum[2 * b + 1]
                        kG, kU = f"ps{2 * b}", f"ps{2 * b + 1}"

                        def mmG(e, psG=psG, gb=gb, fl=fl, c0=c0, n=n):
                            ins = None
                            for kc in range(8):
                                ins = e.matmul(psG[:, :n + 2], lhsT=wg[gb][:, kc, fl * 128:(fl + 1) * 128],
                                               rhs=HT[:, kc, c0 - 1:c0 + n + 1], start=(kc == 0), stop=(kc == 7))
                            return ins

                        def mmU(e, psU=psU, gb=gb, fl=fl, c0=c0, n=n):
                            ins = None
                            for kc in range(8):
                                ins = e.matmul(psU[:, :n], lhsT=wu[gb][:, kc, fl * 128:(fl + 1) * 128],
                                               rhs=HT[:, kc, c0:c0 + n], start=(kc == 0), stop=(kc == 7))
                            return ins

                        hk = bkeys("HT", c0 - 1, c0 + n + 1)
                        P.op("pe", mmG, r=[f"fwg{gb}"] + hk, w=[kG])
                        P.op("pe", mmU, r=[f"fwu{gb}"] + hk, w=[kU])
                        P.op("dve", lambda e, psG=psG, b=b, fc=fc, n=n: e.tensor_scalar(
                            out=t1[b][:, :n], in0=psG[:, 0:n], scalar1=cw[:, fc, 0:1], scalar2=None, op0=ALU.mult),
                            r=[kG, "fcw"], w=[f"ft1{b}"])
                        P.op("dve", lambda e, psG=psG, b=b, fc=fc, n=n: e.scalar_tensor_tensor(
                            out=t2[b][:, :n], in0=psG[:, 1:n + 1], scalar=cw[:, fc, 1:2], in1=t1[b][:, :n],
                            op0=ALU.mult, op1=ALU.add), r=[kG, "fcw", f"ft1{b}"], w=[f"ft2{b}"])
                        P.op("dve", lambda e, psG=psG, b=b, fc=fc, n=n: e.scalar_tensor_tensor(
                            out=t1[b][:, :n], in0=psG[:, 2:n + 2], scalar=cw[:, fc, 2:3], in1=t2[b][:, :n],
                            op0=ALU.mult, op1=ALU.add), r=[kG, "fcw", f"ft2{b}"], w=[f"ft1{b}"])
                        P.op("act", lambda e, b=b, fc=fc, n=n: e.activation(
                            out=ge[b][:, :n], in_=t1[b][:, :n], func=AF.Gelu, bias=cb[:, fc:fc + 1]),
                            r=[f"ft1{b}", "fcb"], w=[f"fge{b}"])
                        P.op("dve", lambda e, psU=psU, b=b, n=n: e.tensor_tensor(
                            out=ast[b][:, :n], in0=ge[b][:, :n], in1=psU[:, :n], op=ALU.mult),
                            r=[f"fge{b}", kU], w=[f"fa{b}"])
                        P.dma("sp", AT[fc * 128:(fc + 1) * 128, tok0:tok0 + n], ast[b][:, :n],
                              r=[f"fa{b}"], w=bkeys("AT", tok0, tok0 + n))
            P.barrier()

    def _ffn2(self, li, ctx):
        P = self.P
        AT = self.dram["AT"]
        tiles = self.tiles(n_lat=256, n_ctx=256, ctx=ctx)
        with ExitStack() as st:
            wo = self.sb("fwo", [128, NFC, D], BF16, st)
            wsrc = self.dram["ffn_w_out"][li].rearrange("(fc p) n -> p fc n", p=128)
            for h in range(2):
                P.dma("pool", wo[:, h * 11:(h + 1) * 11, :], wsrc[:, h * 11:(h + 1) * 11, :], w=[f"fwo{h}"])
            at = [self.sb(f"fat{i}", [128, NFC, 256], BF16, st) for i in range(2)]
            xt = [self.sb(f"fx{i}", [128, 8, 256], F32, st) for i in range(2)]
            atv = AT.rearrange("(fc p) t -> p fc t", p=128)
            cnt = 0
            for it, (seg, t0, n) in enumerate(tiles):
                tok0 = t0 if seg == 0 else LC + t0
                s = 1 if seg == 0 else 0
                b = it % 2
                xv, xname = self.xview(seg)
                P.dma("sp", at[b][:, :, :n], atv[:, :, tok0:tok0 + n], r=bkeys("AT", tok0, tok0 + n), w=[f"fat{b}"])
                P.dma("sp", xt[b][:, :, :n], xv[:, :, t0:t0 + n], r=bkeys(xname, t0, t0 + n), w=[f"fx{b}"])
                for oc in range(8):
                    pb = 4 + cnt % 2
                    cnt += 1
                    ps = self.psum[pb]

                    def mm(e, ps=ps, b=b, oc=oc, n=n):
                        ins = None
                        for fc in range(NFC):
                            ins = e.matmul(ps[:, :n], lhsT=wo[:, fc, oc * 128:(oc + 1) * 128], rhs=at[b][:, fc, :n],
                                           start=(fc == 0), stop=(fc == NFC - 1))
                        return ins

                    P.op("pe", mm, r=["fwo0", "fwo1", f"fat{b}"], w=[f"ps{pb}"])
                    P.op("dve", lambda e, ps=ps, b=b, oc=oc, n=n, s=s: e.scalar_tensor_tensor(
                        out=xt[b][:, oc, :n], in0=ps[:, :n], scalar=self.modv[:, 40 + oc, s:s + 1], in1=xt[b][:, oc, :n],
                        op0=ALU.mult, op1=ALU.add), r=[f"ps{pb}", "modv", f"fx{b}"], w=[f"fx{b}"])
                P.dma("sp", xv[:, :, t0:t0 + n], xt[b][:, :, :n], r=[f"fx{b}"], w=bkeys(xname, t0, t0 + n))
            P.barrier()

    def mixer(self, li):
        getattr(self, ("s5_mixer", "lru_mixer", "ret_mixer", "gdn_mixer")[li % 4])(li)

    def sincos(self, ang, F, out_s, out_c, T, rk, wk_s, wk_c):
        P = self.P
        tk = "sc_tmp"
        y, y2, w, m, ki = (T[k][:, :F] for k in ("y", "y2", "w", "m", "ki"))
        P.op("dve", lambda e: e.tensor_scalar(out=y, in0=ang, scalar1=1.0 / (2 * math.pi), scalar2=64.5,
                                              op0=ALU.mult, op1=ALU.add), r=rk + [tk], w=[tk])
        for off, out, wk in ((0.0, out_s, wk_s), (0.25, out_c, wk_c)):
            P.op("dve", lambda e, off=off: e.tensor_scalar(out=y2, in0=y, scalar1=off, scalar2=None, op0=ALU.add),
                 r=[tk], w=[tk])
            P.op("dve", lambda e: e.tensor_copy(out=ki, in_=y2), r=[tk], w=[tk])
            P.op("dve", lambda e: e.tensor_copy(out=w, in_=ki), r=[tk], w=[tk])
            P.op("dve", lambda e: e.tensor_tensor(out=w, in0=y2, in1=w, op=ALU.subtract), r=[tk], w=[tk])
            P.op("dve", lambda e: e.tensor_scalar(out=m, in0=w, scalar1=0.0, scalar2=None, op0=ALU.is_lt), r=[tk], w=[tk])
            P.op("dve", lambda e: e.tensor_tensor(out=w, in0=w, in1=m, op=ALU.add), r=[tk], w=[tk])
            P.op("act", lambda e, out=out: e.activation(out=out, in_=w, func=AF.Sin, scale=2 * math.pi, bias=self.negpi[:]),
                 r=[tk, "negpi"], w=wk + [tk])

    def declare_s5(self):
        self.din("s5_col", [2, 128, 3, 32])
        self.din("s5_row", [2, 3, 4096])
        self.din("s5_B", [2, 2, 128, 32, 128])
        self.din("s5_C", [2, 2, 128, 32, 128])
        self.din("s5_tau", [128, 128])
        self.din("i_s5d", [128, 8])
        self.din("i_s5bglu", [128, 16])
        self.din("s5_w_glu", [D, 2 * D])
        self.dscr("H32", [D, NT])
        self.dscr("YF", [D, NT])
        self.dscr("ZB", [D, NT], BF16)

    def s5_mixer(self, li):
        P = self.P
        H32 = self.dram["H32"]
        h32v = H32.rearrange("(kc p) t -> p kc t", p=128)
        yfv = self.dram["YF"].rearrange("(kc p) t -> p kc t", p=128)
        zbv = self.dram["ZB"].rearrange("(kc p) t -> p kc t", p=128)
        self.norm_phase(self.A1, 0, ctx=True, h32=H32, write_ht=False)
        with ExitStack() as st:
            EC = self.sb("s5EC", [128, 32, 128], F32, st)
            ES = self.sb("s5ES", [128, 32, 128], F32, st)
            RT = self.sb("s5RT", [128, 32, 128], F32, st)
            BTr = self.sb("s5BTr", [128, 32, 128], F32, st)
            BTi = self.sb("s5BTi", [128, 32, 128], F32, st)
            CTr = self.sb("s5CTr", [128, 32, 128], F32, st)
            CTi = self.sb("s5CTi", [128, 32, 128], F32, st)
            tau = self.sb("s5tau", [128, 128], F32, st)
            dsk = self.sb("s5dsk", [128, 8], F32, st)
            colp = self.sb("s5colp", [128, 3, 32], F32, st)
            angc = self.sb("s5angc", [128, 32], F32, st)
            magc = self.sb("s5magc", [128, 32], F32, st)
            hpr = self.sb("s5hpr", [128, 32], F32, st)
            hpi = self.sb("s5hpi", [128, 32], F32, st)
            self.negpi = self.sb("negpi", [128, 1], F32, st)
            P.op("dve", lambda e: e.memset(self.negpi[:], -math.pi), w=["negpi"])
            P.dma("sp", tau[:], self.dram["s5_tau"], w=["s5tau"])
            P.dma("sp", dsk[:], self.dram["i_s5d"], w=["s5dsk"])
            for d in range(2):
                with ExitStack() as st2:
                    T = {k: self.sb("s5T" + k, [128, 1024], F32, st2) for k in ("y", "y2", "w", "m")}
                    T["ki"] = self.sb("s5Tki", [128, 1024], I32, st2)
                    rows = self.sb("s5rows", [128, 3, 1024], F32, st2)
                    R = {k: self.sb("s5R" + k, [128, 1024], F32, st2) for k in
                         ("step", "lr", "mag", "ang", "sn", "cs", "a1", "ai", "den", "kr", "ki2", "t")}
                    braw = self.sb("s5braw", [128, 2, 8, 128], F32, st2)

                    def disc(lre, lim, lst, F, tag, need_k):
                        k = "s5R"
                        step, lr, mag, ang = (R[x][:, :F] for x in ("step", "lr", "mag", "ang"))
                        P.op("act", lambda e: e.activation(out=step, in_=lst, func=AF.Exp), r=[tag, k], w=[k])
                        P.op("dve", lambda e: e.tensor_scalar(out=lr, in0=lre, scalar1=-1e-4, scalar2=None, op0=ALU.min), r=[tag, k], w=[k])
                        P.op("dve", lambda e: e.tensor_tensor(out=mag, in0=lr, in1=step, op=ALU.mult), r=[k], w=[k])
                        P.op("act", lambda e: e.activation(out=mag, in_=mag, func=AF.Exp), r=[k], w=[k])
                        P.op("dve", lambda e: e.tensor_tensor(out=ang, in0=lim, in1=step, op=ALU.mult), r=[tag, k], w=[k])
                        if not need_k:
                            return
                        sn, cs, a1, ai, den, kr, ki2, t = (R[x][:, :F] for x in ("sn", "cs", "a1", "ai", "den", "kr", "ki2", "t"))
                        self.sincos(ang, F, sn, cs, T, [k], [k], [k])
                        P.op("dve", lambda e: e.tensor_tensor(out=a1, in0=mag, in1=cs, op=ALU.mult), r=[k], w=[k])
                        P.op("dve", lambda e: e.tensor_scalar(out=a1, in0=a1, scalar1=-1.0, scalar2=None, op0=ALU.add), r=[k], w=[k])
                        P.op("dve", lambda e: e.tensor_tensor(out=ai, in0=mag, in1=sn, op=ALU.mult), r=[k], w=[k])
                        P.op("dve", lambda e: e.tensor_tensor(out=den, in0=lr, in1=lr, op=ALU.mult), r=[k], w=[k])
                        P.op("dve", lambda e: e.tensor_tensor(out=t, in0=lim, in1=lim, op=ALU.mult), r=[tag, k], w=[k])
                        P.op("dve", lambda e: e.tensor_tensor(out=den, in0=den, in1=t, op=ALU.add), r=[k], w=[k])
                        P.op("dve", lambda e: e.reciprocal(out=den, in_=den), r=[k], w=[k])
                        P.op("dve", lambda e: e.tensor_tensor(out=kr, in0=a1, in1=lr, op=ALU.mult), r=[k], w=[k])
                        P.op("dve", lambda e: e.tensor_tensor(out=t, in0=ai, in1=lim, op=ALU.mult), r=[tag, k], w=[k])
                        P.op("dve", lambda e: e.tensor_tensor(out=kr, in0=kr, in1=t, op=ALU.add), r=[k], w=[k])
                        P.op("dve", lambda e: e.tensor_tensor(out=kr, in0=kr, in1=den, op=ALU.mult), r=[k], w=[k])
                        P.op("dve", lambda e: e.tensor_tensor(out=ki2, in0=ai, in1=lr, op=ALU.mult), r=[k], w=[k])
                        P.op("dve", lambda e: e.tensor_tensor(out=t, in0=a1, in1=lim, op=ALU.mult), r=[tag, k], w=[k])
                        P.op("dve", lambda e: e.tensor_tensor(out=ki2, in0=ki2, in1=t, op=ALU.subtract), r=[k], w=[k])
                        P.op("dve", lambda e: e.tensor_tensor(out=ki2, in0=ki2, in1=den, op=ALU.mult), r=[k], w=[k])

                    P.dma("sp", colp[:], self.dram["s5_col"][d], w=["s5colp"])
                    disc(colp[:, 0, :], colp[:, 1, :], colp[:, 2, :], 32, "s5colp", False)
                    P.op("dve", lambda e: e.tensor_copy(out=angc[:], in_=R["ang"][:, :32]), r=["s5R"], w=["s5angc"])
                    P.op("dve", lambda e: e.tensor_copy(out=magc[:], in_=R["mag"][:, :32]), r=["s5R"], w=["s5magc"])
                    for q in range(4):
                        ph = R["t"][:, :1024].rearrange("p (g t) -> p g t", t=128)
                        P.op("dve", lambda e, q=q, ph=ph: e.tensor_tensor(
                            out=ph, in0=angc[:, q * 8:(q + 1) * 8].unsqueeze(2).to_broadcast([128, 8, 128]),
                            in1=tau[:].unsqueeze(1).to_broadcast([128, 8, 128]), op=ALU.mult),
                            r=["s5angc", "s5tau", "s5R"], w=["s5R"])
                        self.sincos(R["t"][:, :1024], 1024,
                                    ES[:, q * 8:(q + 1) * 8, :].rearrange("p g t -> p (g t)"),
                                    EC[:, q * 8:(q + 1) * 8, :].rearrange("p g t -> p (g t)"), T, ["s5R"], ["s5ES"], ["s5EC"])
                    P.op("dve", lambda e: e.tensor_copy(out=RT[:], in_=magc[:].unsqueeze(2).to_broadcast([128, 32, 128])),
                         r=["s5magc"], w=["s5RT"])
                    P.op("dve", lambda e: e.memset(RT[:, :, 0:1], 0.0), w=["s5RT"])
                    for q in range(4):
                        for i3 in range(3):
                            P.dma("sp", rows[:, i3, :], self.dram["s5_row"][d, i3:i3 + 1, q * 1024:(q + 1) * 1024].to_broadcast([128, 1024]),
                                  w=["s5rows"])
                        disc(rows[:, 0, :], rows[:, 1, :], rows[:, 2, :], 1024, "s5rows", True)
                        for ri in range(2):
                            P.dma("sp", braw[:, ri], self.dram["s5_B"][d, ri, :, q * 8:(q + 1) * 8, :], w=["s5braw"])
                        kr3 = R["kr"][:, :1024].rearrange("p (g t) -> p g t", t=128)
                        ki3 = R["ki2"][:, :1024].rearrange("p (g t) -> p g t", t=128)
                        t3 = R["t"][:, :1024].rearrange("p (g t) -> p g t", t=128)
                        bq_r, bq_i = BTr[:, q * 8:(q + 1) * 8, :], BTi[:, q * 8:(q + 1) * 8, :]
                        P.op("dve", lambda e, bq_r=bq_r, kr3=kr3, br0=braw[:, 0]: e.tensor_tensor(out=bq_r, in0=br0, in1=kr3, op=ALU.mult), r=["s5braw", "s5R"], w=["s5BTr"])
                        P.op("dve", lambda e, ki3=ki3, t3=t3, br1=braw[:, 1]: e.tensor_tensor(out=t3, in0=br1, in1=ki3, op=ALU.mult), r=["s5braw", "s5R"], w=["s5R"])
                        P.op("dve", lambda e, bq_r=bq_r, t3=t3: e.tensor_tensor(out=bq_r, in0=bq_r, in1=t3, op=ALU.subtract), r=["s5R", "s5BTr"], w=["s5BTr"])
                        P.op("dve", lambda e, bq_i=bq_i, kr3=kr3, br1=braw[:, 1]: e.tensor_tensor(out=bq_i, in0=br1, in1=kr3, op=ALU.mult), r=["s5braw", "s5R"], w=["s5BTi"])
                        P.op("dve", lambda e, ki3=ki3, t3=t3, br0=braw[:, 0]: e.tensor_tensor(out=t3, in0=br0, in1=ki3, op=ALU.mult), r=["s5braw", "s5R"], w=["s5R"])
                        P.op("dve", lambda e, bq_i=bq_i, t3=t3: e.tensor_tensor(out=bq_i, in0=bq_i, in1=t3, op=ALU.add), r=["s5R", "s5BTi"], w=["s5BTi"])
                    P.dma("sp", CTr[:], self.dram["s5_C"][d, 0], w=["s5CTr"])
                    P.dma("sp", CTi[:], self.dram["s5_C"][d, 1], w=["s5CTi"])
                    P.op("pool", lambda e: e.tensor_scalar(out=CTi[:], in0=CTi[:], scalar1=-1.0, scalar2=None, op0=ALU.mult), r=["s5CTi"], w=["s5CTi"])
                    P.barrier()
                st3 = ExitStack()
                W = []
                for i in range(2):
                    W.append({k: self.sb(f"s5{k}{i}", [128, 512], F32, st3) for k in
                              ("m1", "m2", "m3", "m4", "gr", "gi", "sr", "si", "hr", "hi")})
                ub = [self.sb(f"s5ub{i}", [128, 8, 128], F32, st3) for i in range(2)]
                yfb = [self.sb(f"s5yf{i}", [128, 8, 128], F32, st3) for i in range(2)]
                yst = [self.sb(f"s5yst{i}", [128, 128], F32, st3) for i in range(2)]
                zst = [self.sb(f"s5zst{i}", [128, 8, 128], BF16, st3) for i in range(2)]
                zst_f = [self.sb(f"s5yfo{i}", [128, 8, 128], F32, st3) for i in range(2)]
                cin = self.sb("s5cin", [128, 2, 4], F32, st3)
                P.op("dve", lambda e: e.memset(hpr[:], 0.0), w=["s5hp"])
                P.op("dve", lambda e: e.memset(hpi[:], 0.0), w=["s5hp"])
                order = [0, 1] + list(range(2, 34)) if d == 0 else [1, 0] + list(range(33, 1, -1))
                cnt = 0
                for bi, blk in enumerate(order):
                    tok0 = blk * 128
                    bb = bi % 2
                    u = ub[bb]
                    P.dma("sp", u[:], h32v[:, :, tok0:tok0 + 128], r=bkeys("H32", tok0, tok0 + 128), w=[f"s5ub{bb}"])
                    if d == 1:
                        P.dma("sp", yfb[bb][:], yfv[:, :, tok0:tok0 + 128], r=bkeys("YF", tok0, tok0 + 128), w=[f"s5yf{bb}"])

                    def tv(ap, d=d):
                        return ap if d == 0 else ap[:, ::-1]

                    for j in range(8):
                        wb = cnt % 2
                        cnt += 1
                        Wb = W[wb]
                        wk = f"s5W{wb}"
                        psR, psI, psY = self.psum[2 * wb], self.psum[2 * wb + 1], self.psum[4 + wb]
                        kR, kI, kY = f"ps{2 * wb}", f"ps{2 * wb + 1}", f"ps{4 + wb}"

                        u_j = tv(u[:, j, :])

                        def mmB(e, psR=psR, psI=psI, u_j=u_j, j=j):
                            ins = None
                            for g4 in range(4):
                                gp = 4 * j + g4
                                e.matmul(psR[:, g4 * 128:(g4 + 1) * 128], lhsT=BTr[:, gp, :], rhs=u_j, start=True, stop=True)
                                ins = e.matmul(psI[:, g4 * 128:(g4 + 1) * 128], lhsT=BTi[:, gp, :], rhs=u_j, start=True, stop=True)
                            return ins

                        P.op("pe", mmB, r=["s5BTr", "s5BTi", f"s5ub{bb}"], w=[kR, kI])
                        ec = EC[:, 4 * j:4 * j + 4, :].rearrange("p g t -> p (g t)")
                        es = ES[:, 4 * j:4 * j + 4, :].rearrange("p g t -> p (g t)")
                        rt = RT[:, 4 * j:4 * j + 4, :].rearrange("p g t -> p (g t)")
                        m1, m2, m3, m4, gr, gi, sr, si, hr, hi = (Wb[x][:] for x in ("m1", "m2", "m3", "m4", "gr", "gi", "sr", "si", "hr", "hi"))
                        P.op("dve", lambda e, m1=m1, psR=psR, ec=ec: e.tensor_tensor(out=m1, in0=psR[:], in1=ec, op=ALU.mult), r=[kR, "s5EC"], w=[wk + "m1"])
                        P.op("dve", lambda e, m2=m2, psI=psI, es=es: e.tensor_tensor(out=m2, in0=psI[:], in1=es, op=ALU.mult), r=[kI, "s5ES"], w=[wk + "m2"])
                        P.op("dve", lambda e, m3=m3, psI=psI, ec=ec: e.tensor_tensor(out=m3, in0=psI[:], in1=ec, op=ALU.mult), r=[kI, "s5EC"], w=[wk + "m3"])
                        P.op("dve", lambda e, m4=m4, psR=psR, es=es: e.tensor_tensor(out=m4, in0=psR[:], in1=es, op=ALU.mult), r=[kR, "s5ES"], w=[wk + "m4"])
                        P.op("pool", lambda e, gr=gr, m1=m1, m2=m2: e.tensor_tensor(out=gr, in0=m1, in1=m2, op=ALU.add), r=[wk + "m1", wk + "m2"], w=[wk + "gr"])
                        P.op("pool", lambda e, gi=gi, m3=m3, m4=m4: e.tensor_tensor(out=gi, in0=m3, in1=m4, op=ALU.subtract), r=[wk + "m3", wk + "m4"], w=[wk + "gi"])
                        P.op("pool", lambda e, j=j, c0_=cin[:, 0, :]: e.tensor_tensor(out=c0_, in0=magc[:, 4 * j:4 * j + 4], in1=hpr[:, 4 * j:4 * j + 4], op=ALU.mult), r=["s5magc", "s5hp"], w=["s5cin"])
                        P.op("pool", lambda e, j=j, c1_=cin[:, 1, :]: e.tensor_tensor(out=c1_, in0=magc[:, 4 * j:4 * j + 4], in1=hpi[:, 4 * j:4 * j + 4], op=ALU.mult), r=["s5magc", "s5hp"], w=["s5cin"])
                        gr0 = Wb["gr"][:].rearrange("p (g t) -> p g t", t=128)[:, :, 0]
                        gi0 = Wb["gi"][:].rearrange("p (g t) -> p g t", t=128)[:, :, 0]
                        P.op("pool", lambda e, gr0=gr0, c0_=cin[:, 0, :]: e.tensor_tensor(out=gr0, in0=gr0, in1=c0_, op=ALU.add), r=["s5cin", wk + "gr"], w=[wk + "gr"])
                        P.op("pool", lambda e, gi0=gi0, c1_=cin[:, 1, :]: e.tensor_tensor(out=gi0, in0=gi0, in1=c1_, op=ALU.add), r=["s5cin", wk + "gi"], w=[wk + "gi"])
                        P.op("dve", lambda e, sr=sr, gr=gr, rt=rt: e.tensor_tensor_scan(out=sr, data0=rt, data1=gr, initial=0.0, op0=ALU.mult, op1=ALU.add), r=["s5RT", wk + "gr"], w=[wk + "sr"])
                        P.op("dve", lambda e, si=si, gi=gi, rt=rt: e.tensor_tensor_scan(out=si, data0=rt, data1=gi, initial=0.0, op0=ALU.mult, op1=ALU.add), r=["s5RT", wk + "gi"], w=[wk + "si"])
                        P.op("pool", lambda e, m1=m1, sr=sr, ec=ec: e.tensor_tensor(out=m1, in0=sr, in1=ec, op=ALU.mult), r=[wk + "sr", "s5EC"], w=[wk + "m1"])
                        P.op("pool", lambda e, m2=m2, si=si, es=es: e.tensor_tensor(out=m2, in0=si, in1=es, op=ALU.mult), r=[wk + "si", "s5ES"], w=[wk + "m2"])
                        P.op("pool", lambda e, hr=hr, m1=m1, m2=m2: e.tensor_tensor(out=hr, in0=m1, in1=m2, op=ALU.subtract), r=[wk + "m1", wk + "m2"], w=[wk + "hr"])
                        P.op("pool", lambda e, m3=m3, sr=sr, es=es: e.tensor_tensor(out=m3, in0=sr, in1=es, op=ALU.mult), r=[wk + "sr", "s5ES"], w=[wk + "m3"])
                        P.op("pool", lambda e, m4=m4, si=si, ec=ec: e.tensor_tensor(out=m4, in0=si, in1=ec, op=ALU.mult), r=[wk + "si", "s5EC"], w=[wk + "m4"])
                        P.op("pool", lambda e, hi=hi, m3=m3, m4=m4: e.tensor_tensor(out=hi, in0=m3, in1=m4, op=ALU.add), r=[wk + "m3", wk + "m4"], w=[wk + "hi"])
                        hr3 = Wb["hr"][:].rearrange("p (g t) -> p g t", t=128)
                        hi3 = Wb["hi"][:].rearrange("p (g t) -> p g t", t=128)
                        P.op("act", lambda e, hr3=hr3, j=j: e.activation(out=hpr[:, 4 * j:4 * j + 4], in_=hr3[:, :, 127], func=AF.Copy), r=[wk + "hr"], w=["s5hp"])
                        P.op("act", lambda e, hi3=hi3, j=j: e.activation(out=hpi[:, 4 * j:4 * j + 4], in_=hi3[:, :, 127], func=AF.Copy), r=[wk + "hi"], w=["s5hp"])

                        def mmC(e, psY=psY, hr3=hr3, hi3=hi3, j=j):
                            ins = None
                            for g4 in range(4):
                                gp = 4 * j + g4
                                e.matmul(psY[:, 0:128], lhsT=CTr[:, gp, :], rhs=hr3[:, g4, :], start=(g4 == 0), stop=False)
                                ins = e.matmul(psY[:, 0:128], lhsT=CTi[:, gp, :], rhs=hi3[:, g4, :], start=False, stop=(g4 == 3))
                            return ins

                        P.op("pe", mmC, r=["s5CTr", "s5CTi", wk + "hr", wk + "hi"], w=[kY])
                        if d == 0:
                            P.op("act", lambda e, psY=psY, zo=zst_f[bb][:, j, :]: e.activation(out=zo, in_=psY[:, 0:128], func=AF.Copy), r=[kY], w=[f"s5yfo{bb}"])
                        else:
                            ys = yst[wb][:]
                            yf_j = tv(yfb[bb][:, j, :])
                            z_j = tv(zst[bb][:, j, :])
                            P.op("dve", lambda e, ys=ys, psY=psY, yf_j=yf_j: e.tensor_tensor(out=ys, in0=psY[:, 0:128], in1=yf_j, op=ALU.add), r=[kY, f"s5yf{bb}"], w=[f"s5yst{wb}"])
                            P.op("dve", lambda e, ys=ys, u_j=u_j, j=j: e.scalar_tensor_tensor(out=ys, in0=u_j, scalar=dsk[:, j:j + 1], in1=ys, op0=ALU.mult, op1=ALU.add), r=[f"s5ub{bb}", "s5dsk", f"s5yst{wb}"], w=[f"s5yst{wb}"])
                            P.op("act", lambda e, ys=ys, z_j=z_j: e.activation(out=z_j, in_=ys, func=AF.Gelu), r=[f"s5yst{wb}"], w=[f"s5zst{bb}"])
                    if d == 0:
                        P.dma("sp", yfv[:, :, tok0:tok0 + 128], zst_f[bb][:], r=[f"s5yfo{bb}"], w=bkeys("YF", tok0, tok0 + 128))
                    else:
                        P.dma("sp", zbv[:, :, tok0:tok0 + 128], zst[bb][:], r=[f"s5zst{bb}"], w=bkeys("ZB", tok0, tok0 + 128))
                P.barrier()
                st3.close()
            P.barrier()
        self.s5_glu(li)

    def s5_glu(self, li):
        P = self.P
        zbv = self.dram["ZB"].rearrange("(kc p) t -> p kc t", p=128)
        wv = self.dram["s5_w_glu"].rearrange("(kc p) n -> p kc n", p=128)
        with ExitStack() as st:
            self.alloc_ht(st)
            HT = self.HT
            P.dma("sp", HT[:, :, HT_CTX0:HT_CTX0 + LC], zbv[:, :, 0:LC], r=bkeys("ZB", 0, LC), w=bkeys("HT", HT_CTX0, HT_CTX0 + LC))
            for t0 in range(0, L, 512):
                c0 = HT_LAT0 + t0
                P.dma("sp", HT[:, :, c0:c0 + 512], zbv[:, :, LC + t0:LC + t0 + 512], r=bkeys("ZB", LC + t0, LC + t0 + 512), w=bkeys("HT", c0, c0 + 512))
            bgl = self.sb("s5bgl", [128, 16], F32, st)
            P.dma("sp", bgl[:], self.dram["i_s5bglu"], w=["s5bgl"])
            wvb = [self.sb(f"gwv{i}", [128, 8, 512], BF16, st) for i in range(2)]
            wgb = [self.sb(f"gwg{i}", [128, 8, 512], BF16, st) for i in range(2)]
            sg = [self.sb(f"gsg{i}", [128, 512], F32, st) for i in range(2)]
            vv = [self.sb(f"gvv{i}", [128, 512], F32, st) for i in range(2)]
            xt = [self.sb(f"gx{i}", [128, 512], F32, st) for i in range(2)]
            cnt = 0
            for g in range(2):
                P.dma("pool", wvb[g][:], wv[:, :, g * 512:(g + 1) * 512], w=[f"gwv{g}"])
                P.dma("pool", wgb[g][:], wv[:, :, D + g * 512:D + (g + 1) * 512], w=[f"gwg{g}"])
                for ol in range(4):
                    oc = 4 * g + ol
                    for (seg, t0, n) in self.tiles(ctx=True):
                        tok0 = t0 if seg == 0 else LC + t0
                        c0 = ht_col(tok0)
                        s = 1 if seg == 0 else 0
                        b = cnt % 2
                        cnt += 1
                        psV, psG = self.psum[2 * b], self.psum[2 * b + 1]
                        kV, kG = f"ps{2 * b}", f"ps{2 * b + 1}"
                        xv, xname = self.xview(seg)
                        xsl = self.dram[xname][oc * 128:(oc + 1) * 128, t0:t0 + n]
                        P.dma("sp", xt[b][:, :n], xsl, r=bkeys(xname, t0, t0 + n), w=[f"gx{b}"])

                        def mm(e, ps, wb, ol=ol, c0=c0, n=n):
                            ins = None
                            for kc in range(8):
                                ins = e.matmul(ps[:, :n], lhsT=wb[:, kc, ol * 128:(ol + 1) * 128], rhs=HT[:, kc, c0:c0 + n], start=(kc == 0), stop=(kc == 7))
                            return ins

                        hk = bkeys("HT", c0, c0 + n)
                        P.op("pe", lambda e, psV=psV, g=g, ol=ol, c0=c0, n=n: mm(e, psV, wvb[g], ol, c0, n), r=[f"gwv{g}"] + hk, w=[kV])
                        P.op("pe", lambda e, psG=psG, g=g, ol=ol, c0=c0, n=n: mm(e, psG, wgb[g], ol, c0, n), r=[f"gwg{g}"] + hk, w=[kG])
                        P.op("act", lambda e, psG=psG, b=b, oc=oc, n=n: e.activation(out=sg[b][:, :n], in_=psG[:, :n], func=AF.Sigmoid, bias=bgl[:, 8 + oc:9 + oc]), r=[kG, "s5bgl"], w=[f"gsg{b}"])
                        P.op("dve", lambda e, psV=psV, b=b, oc=oc, n=n: e.scalar_tensor_tensor(out=vv[b][:, :n], in0=psV[:, :n], scalar=bgl[:, oc:oc + 1], in1=sg[b][:, :n], op0=ALU.add, op1=ALU.mult), r=[kV, "s5bgl", f"gsg{b}"], w=[f"gvv{b}"])
                        P.op("dve", lambda e, b=b, oc=oc, n=n, s=s: e.scalar_tensor_tensor(out=xt[b][:, :n], in0=vv[b][:, :n], scalar=self.modv[:, 16 + oc, s:s + 1], in1=xt[b][:, :n], op0=ALU.mult, op1=ALU.add), r=[f"gvv{b}", "modv", f"gx{b}"], w=[f"gx{b}"])
                        P.dma("sp", xsl, xt[b][:, :n], r=[f"gx{b}"], w=bkeys(xname, t0, t0 + n))
            P.barrier()

    def proj_residual(self, wsrc, K_chunks, gsl, ctx=True):
        P = self.P
        HT = self.HT
        wv = wsrc.rearrange("(kc p) n -> p kc n", p=128)
        with ExitStack() as st:
            wb = [self.sb(f"prw{i}", [128, K_chunks, 512], BF16, st) for i in range(2)]
            xt = [self.sb(f"prx{i}", [128, 512], F32, st) for i in range(2)]
            cnt = 0
            for g in range(2):
                P.dma("pool", wb[g][:], wv[:, :, g * 512:(g + 1) * 512], w=[f"prw{g}"])
                for ol in range(4):
                    oc = 4 * g + ol
                    for (seg, t0, n) in self.tiles(ctx=ctx):
                        tok0 = t0 if seg == 0 else LC + t0
                        c0 = ht_col(tok0)
                        s = 1 if seg == 0 else 0
                        b = cnt % 2
                        cnt += 1
                        ps = self.psum[b]
                        xname = "XL" if seg == 1 else "XC"
                        xsl = self.dram[xname][oc * 128:(oc + 1) * 128, t0:t0 + n]
                        P.dma("sp", xt[b][:, :n], xsl, r=bkeys(xname, t0, t0 + n), w=[f"prx{b}"])

                        def mm(e, ps=ps, g=g, ol=ol, c0=c0, n=n):
                            ins = None
                            for kc in range(K_chunks):
                                ins = e.matmul(ps[:, :n], lhsT=wb[g][:, kc, ol * 128:(ol + 1) * 128], rhs=HT[:, kc, c0:c0 + n],
                                               start=(kc == 0), stop=(kc == K_chunks - 1))
                            return ins

                        P.op("pe", mm, r=[f"prw{g}"] + bkeys("HT", c0, c0 + n), w=[f"ps{b}"])
                        P.op("dve", lambda e, ps=ps, b=b, oc=oc, n=n, s=s: e.scalar_tensor_tensor(
                            out=xt[b][:, :n], in0=ps[:, :n], scalar=self.modv[:, gsl + oc, s:s + 1], in1=xt[b][:, :n],
                            op0=ALU.mult, op1=ALU.add), r=[f"ps{b}", "modv", f"prx{b}"], w=[f"prx{b}"])
                        P.dma("sp", xsl, xt[b][:, :n], r=[f"prx{b}"], w=bkeys(xname, t0, t0 + n))
            P.barrier()

    def declare_lru(self):
        self.din("lru_w_in", [D, 2 * D])
        self.din("lru_w_a", [2, 4, 256, 256])
        self.din("lru_w_x", [2, 4, 256, 256])
        self.din("lru_w_out", [D, D])
        self.din("i_lru_cw", [128, 8, 4])
        self.din("i_lru_cb", [128, 8])
        self.din("i_lru_vec", [128, 3, 2, 8])
        self.dscr("YB", [D, NT])
        self.dscr("XC32", [D, NT])
        self.dscr("HS", [D, NT])

    def lru_mixer(self, li):
        P = self.P
        YB, XC32, HS = self.dram["YB"], self.dram["XC32"], self.dram["HS"]
        win = self.dram["lru_w_in"].rearrange("(kc p) n -> p kc n", p=128)
        with ExitStack() as st:
            XCb = self.sb("lruXCb", [128, 8, NT], BF16, st)
            vec = self.sb("lruvec", [128, 3, 2, 8], F32, st)
            cs = self.sb("lrucs", [128, 2, 8], F32, st)
            onec = self.sb("lruone", [128, 1], F32, st)
            P.op("dve", lambda e: e.memset(onec[:], 1.0), w=["lruone"])
            P.dma("sp", vec[:], self.dram["i_lru_vec"], w=["lruvec"])
            with ExitStack() as st2:
                tt = {k: self.sb("lrut" + k, [128, 16], F32, st2) for k in ("t", "ab", "u", "s", "s2", "p")}
                lam = vec[:, 2].rearrange("p d k -> p (d k)")
                k = "lrutt"
                t, ab, u, s_, s2, p_ = (tt[x][:] for x in ("t", "ab", "u", "s", "s2", "p"))
                P.op("dve", lambda e: e.tensor_scalar(out=t, in0=lam, scalar1=-1.0, scalar2=None, op0=ALU.mult), r=["lruvec"], w=[k])
                P.op("dve", lambda e: e.tensor_tensor(out=ab, in0=t, in1=lam, op=ALU.max), r=[k, "lruvec"], w=[k])
                P.op("act", lambda e: e.activation(out=u, in_=ab, func=AF.Exp, scale=-1.0), r=[k], w=[k])
                P.op("dve", lambda e: e.tensor_scalar(out=s_, in0=u, scalar1=2.0, scalar2=None, op0=ALU.add), r=[k], w=[k])
                P.op("dve", lambda e: e.reciprocal(out=s_, in_=s_), r=[k], w=[k])
                P.op("dve", lambda e: e.tensor_tensor(out=s_, in0=s_, in1=u, op=ALU.mult), r=[k], w=[k])
                P.op("dve", lambda e: e.tensor_tensor(out=s2, in0=s_, in1=s_, op=ALU.mult), r=[k], w=[k])
                P.op("dve", lambda e: e.tensor_scalar(out=p_, in0=s2, scalar1=1.0 / 9, scalar2=1.0 / 7, op0=ALU.mult, op1=ALU.add), r=[k], w=[k])
                for cst in (1.0 / 5, 1.0 / 3, 1.0):
                    P.op("dve", lambda e: e.tensor_tensor(out=p_, in0=p_, in1=s2, op=ALU.mult), r=[k], w=[k])
                    P.op("dve", lambda e, cst=cst: e.tensor_scalar(out=p_, in0=p_, scalar1=cst, scalar2=None, op0=ALU.add), r=[k], w=[k])
                P.op("dve", lambda e: e.tensor_tensor(out=p_, in0=p_, in1=s_, op=ALU.mult), r=[k], w=[k])
                P.op("dve", lambda e: e.tensor_scalar(out=t, in0=t, scalar1=0.0, scalar2=None, op0=ALU.max), r=[k], w=[k])
                P.op("dve", lambda e: e.scalar_tensor_tensor(out=t, in0=p_, scalar=2.0, in1=t, op0=ALU.mult, op1=ALU.add), r=[k], w=[k])
                P.op("dve", lambda e: e.tensor_scalar(out=cs[:].rearrange("p d k -> p (d k)"), in0=t, scalar1=-8.0, scalar2=None, op0=ALU.mult), r=[k], w=["lrucs"])
                P.barrier()
            with ExitStack() as hst:
                self.alloc_ht(hst)
                HT = self.HT
                with ExitStack() as st2:
                    self.norm_phase(self.A1, 0, ctx=True)
                    cw = self.sb("lrucw", [128, 8, 4], F32, st2)
                    cb = self.sb("lrucb", [128, 8], F32, st2)
                    P.dma("sp", cw[:], self.dram["i_lru_cw"], w=["lrucw"])
                    P.dma("sp", cb[:], self.dram["i_lru_cb"], w=["lrucb"])
                    wb = [self.sb(f"lruw{i}", [128, 8, 512], BF16, st2) for i in range(2)]
                    yst = [self.sb(f"lruy{i}", [128, 512], F32, st2) for i in range(2)]
                    xbl = self.sb("lruxbl", [128, L + 3], F32, st2)
                    xbc = self.sb("lruxbc", [128, LC + 3], F32, st2)
                    xcf = self.sb("lruxcf", [128, L], F32, st2)
                    for bufx, nn in ((xbl, L), (xbc, LC)):
                        P.op("dve", lambda e, bufx=bufx: e.memset(bufx[:, 0:2], 0.0), w=["lruxb"])
                        P.op("dve", lambda e, bufx=bufx, nn=nn: e.memset(bufx[:, nn + 2:nn + 3], 0.0), w=["lruxb"])
                    cnt = 0
                    for g in range(4):
                        gb = g % 2
                        P.dma("pool", wb[gb][:], win[:, :, g * 512:(g + 1) * 512], w=[f"lruw{gb}"])
                        for ol in range(4):
                            oc_all = 4 * g + ol
                            for (seg, t0, n) in self.tiles(ctx=True):
                                tok0 = t0 if seg == 0 else LC + t0
                                c0 = ht_col(tok0)
                                b = cnt % 2
                                cnt += 1
                                ps = self.psum[b]

                                def mm(e, ps=ps, gb=gb, ol=ol, c0=c0, n=n):
                                    ins = None
                                    for kc in range(8):
                                        ins = e.matmul(ps[:, :n], lhsT=wb[gb][:, kc, ol * 128:(ol + 1) * 128], rhs=HT[:, kc, c0:c0 + n],
                                                       start=(kc == 0), stop=(kc == 7))
                                    return ins

                                P.op("pe", mm, r=[f"lruw{gb}"] + bkeys("HT", c0, c0 + n), w=[f"ps{b}"])
                                if oc_all >= 8:
                                    oc = oc_all - 8
                                    P.op("act", lambda e, ps=ps, b=b, n=n: e.activation(out=yst[b][:, :n], in_=ps[:, :n], func=AF.Gelu), r=[f"ps{b}"], w=[f"lruy{b}"])
                                    P.dma("sp", YB[oc * 128:(oc + 1) * 128, tok0:tok0 + n], yst[b][:, :n], r=[f"lruy{b}"], w=bkeys("YB", tok0, tok0 + n))
                                else:
                                    dst = xbc[:, 2 + t0:2 + t0 + n] if seg == 0 else xbl[:, 2 + t0:2 + t0 + n]
                                    P.op("act", lambda e, ps=ps, dst=dst, n=n: e.activation(out=dst, in_=ps[:, :n], func=AF.Copy), r=[f"ps{b}"], w=["lruxb"])
                            if oc_all < 8:
                                oc = oc_all
                                for (bufx, nn, tokb) in ((xbc, LC, 0), (xbl, L, LC)):
                                    o = xcf[:, :nn]
                                    P.op("dve", lambda e, o=o, bufx=bufx, nn=nn, oc=oc: e.tensor_scalar(out=o, in0=bufx[:, 0:nn], scalar1=cw[:, oc, 0:1], scalar2=cb[:, oc:oc + 1], op0=ALU.mult, op1=ALU.add), r=["lruxb", "lrucw", "lrucb"], w=["lruxcf"])
                                    for kk in range(1, 4):
                                        P.op("dve", lambda e, o=o, bufx=bufx, nn=nn, oc=oc, kk=kk: e.scalar_tensor_tensor(out=o, in0=bufx[:, kk:kk + nn], scalar=cw[:, oc, kk:kk + 1], in1=o, op0=ALU.mult, op1=ALU.add), r=["lruxb", "lrucw", "lruxcf"], w=["lruxcf"])
                                    P.op("act", lambda e, o=o, nn=nn, oc=oc, tokb=tokb: e.activation(out=XCb[:, oc, tokb:tokb + nn], in_=o, func=AF.Copy), r=["lruxcf"], w=["lruXCb"])
                                    P.dma("sp", XC32[oc * 128:(oc + 1) * 128, tokb:tokb + nn], o, r=["lruxcf"], w=bkeys("XC32", tokb, tokb + nn))
                    P.barrier()
                with ExitStack() as st2:
                    wa = [self.sb(f"lruwa{i}", [128, 2, 256], BF16, st2) for i in range(2)]
                    wx = [self.sb(f"lruwx{i}", [128, 2, 256], BF16, st2) for i in range(2)]
                    Wk = []
                    for i in range(2):
                        Wk.append({k: self.sb(f"lru{k}{i}", [128, 512], F32, st2) for k in ("r", "i", "a", "q", "b", "h", "xc", "hf", "yb")})
                    carr = self.sb("lrucarr", [128, 8], F32, st2)
                    cnt = 0
                    for d in range(2):
                        tl = self.tiles(ctx=True)
                        order = tl if d == 0 else [tl[0]] + tl[:0:-1]
                        for blk in range(4):
                            wbi = blk % 2
                            P.dma("pool", wa[wbi][:], self.dram["lru_w_a"][d, blk].rearrange("(ic p) n -> p ic n", p=128), w=[f"lruwa{wbi}"])
                            P.dma("pool", wx[wbi][:], self.dram["lru_w_x"][d, blk].rearrange("(ic p) n -> p ic n", p=128), w=[f"lruwx{wbi}"])
                            for ol in range(2):
                                oc = 2 * blk + ol
                                P.op("dve", lambda e, oc=oc: e.memset(carr[:, oc:oc + 1], 0.0), w=[f"lrucarr{oc}"])
                                for (seg, t0, n) in order:
                                    tok0 = t0 if seg == 0 else LC + t0
                                    b = cnt % 2
                                    cnt += 1
                                    Wb = Wk[b]
                                    wk = f"lruW{b}"
                                    psA, psX = self.psum[2 * b], self.psum[2 * b + 1]
                                    kA, kX = f"ps{2 * b}", f"ps{2 * b + 1}"

                                    def mmg(e, ps, wt, wbi=wbi, blk=blk, ol=ol, tok0=tok0, n=n):
                                        ins = None
                                        for ic in range(2):
                                            ins = e.matmul(ps[:, :n], lhsT=wt[wbi][:, ic, ol * 128:(ol + 1) * 128], rhs=XCb[:, 2 * blk + ic, tok0:tok0 + n], start=(ic == 0), stop=(ic == 1))
                                        return ins

                                    P.op("pe", lambda e, psA=psA, wbi=wbi, blk=blk, ol=ol, tok0=tok0, n=n: mmg(e, psA, wa, wbi, blk, ol, tok0, n), r=[f"lruwa{wbi}", "lruXCb"], w=[kA])
                                    P.op("pe", lambda e, psX=psX, wbi=wbi, blk=blk, ol=ol, tok0=tok0, n=n: mmg(e, psX, wx, wbi, blk, ol, tok0, n), r=[f"lruwx{wbi}", "lruXCb"], w=[kX])
                                    P.dma("sp", Wb["xc"][:, :n], XC32[oc * 128:(oc + 1) * 128, tok0:tok0 + n], r=bkeys("XC32", tok0, tok0 + n), w=[wk + "xc"])
                                    r_, i_, a_, q_, b_, h_, xc_, hf_, yb_ = (Wb[x][:, :n] for x in ("r", "i", "a", "q", "b", "h", "xc", "hf", "yb"))
                                    P.op("act", lambda e, r_=r_, psA=psA, n=n, d=d, oc=oc: e.activation(out=r_, in_=psA[:, :n], func=AF.Sigmoid, bias=vec[:, 0, d, oc:oc + 1]), r=[kA, "lruvec"], w=[wk + "r"])
                                    P.op("act", lambda e, i_=i_, psX=psX, n=n, d=d, oc=oc: e.activation(out=i_, in_=psX[:, :n], func=AF.Sigmoid, bias=vec[:, 1, d, oc:oc + 1]), r=[kX, "lruvec"], w=[wk + "i"])
                                    P.op("act", lambda e, a_=a_, r_=r_, d=d, oc=oc: e.activation(out=a_, in_=r_, func=AF.Exp, scale=cs[:, d, oc:oc + 1]), r=[wk + "r", "lrucs"], w=[wk + "a"])
                                    P.op("pool", lambda e, q_=q_, a_=a_: e.tensor_tensor(out=q_, in0=a_, in1=a_, op=ALU.mult), r=[wk + "a"], w=[wk + "q"])
                                    P.op("act", lambda e, q_=q_: e.activation(out=q_, in_=q_, func=AF.Sqrt, scale=-1.0, bias=onec[:]), r=[wk + "q", "lruone"], w=[wk + "q"])
                                    P.op("pool", lambda e, b_=b_, q_=q_, i_=i_: e.tensor_tensor(out=b_, in0=q_, in1=i_, op=ALU.mult), r=[wk + "q", wk + "i"], w=[wk + "b"])
                                    P.op("pool", lambda e, b_=b_, xc_=xc_: e.tensor_tensor(out=b_, in0=b_, in1=xc_, op=ALU.mult), r=[wk + "b", wk + "xc"], w=[wk + "b"])
                                    tv = (lambda ap: ap) if d == 0 else (lambda ap: ap[:, ::-1])
                                    P.op("dve", lambda e, ho=tv(h_), aa=tv(a_), bb_=tv(b_), oc=oc: e.tensor_tensor_scan(out=ho, data0=aa, data1=bb_, initial=carr[:, oc:oc + 1], op0=ALU.mult, op1=ALU.add), r=[wk + "a", wk + "b", f"lrucarr{oc}"], w=[wk + "h"])
                                    last = h_[:, n - 1:n] if d == 0 else h_[:, 0:1]
                                    P.op("act", lambda e, last=last, oc=oc: e.activation(out=carr[:, oc:oc + 1], in_=last, func=AF.Copy), r=[wk + "h"], w=[f"lrucarr{oc}"])
                                    hsl = HS[oc * 128:(oc + 1) * 128, tok0:tok0 + n]
                                    if d == 0:
                                        P.dma("sp", hsl, h_, r=[wk + "h"], w=bkeys("HS", tok0, tok0 + n))
                                    else:
                                        P.dma("sp", hf_, hsl, r=bkeys("HS", tok0, tok0 + n), w=[wk + "hf"])
                                        P.dma("sp", yb_, YB[oc * 128:(oc + 1) * 128, tok0:tok0 + n], r=bkeys("YB", tok0, tok0 + n), w=[wk + "yb"])
                                        P.op("pool", lambda e, hf_=hf_, h_=h_: e.tensor_tensor(out=hf_, in0=hf_, in1=h_, op=ALU.add), r=[wk + "h", wk + "hf"], w=[wk + "hf"])
                                        c0 = ht_col(tok0)
                                        P.op("dve", lambda e, hf_=hf_, yb_=yb_, oc=oc, c0=c0, n=n: e.tensor_tensor(out=HT[:, oc, c0:c0 + n], in0=hf_, in1=yb_, op=ALU.mult), r=[wk + "hf", wk + "yb"], w=bkeys("HT", c0, c0 + n))
                    P.barrier()
                self.proj_residual(self.dram["lru_w_out"], 8, 16, ctx=True)
            P.barrier()

    def declare_ret(self):
        self.din("ret_w_in", [D, 6144])
        self.din("ret_wk_perm", [D, 1024])
        self.din("ret_wq_perm", [D, 1024])
        self.din("ret_w_out", [2048, D])
        self.din("ret_ng", [128, 2048])
        self.din("ret_rope", [2, 4, 128, 64])
        self.din("ret_DT", [2, 128, 4, 128])
        self.din("ret_XI", [2, 128, 4, 128])
        self.din("ret_ZETA", [2, 128, 4])
        self.din("ident", [128, 128])
        self.dscr("KT", [D, NT])
        self.dscr("QT", [D, NT])
        self.dscr("VTOK", [NT, 2048])
        self.dscr("GTOK", [NT, 2048])
        self.dscr("OF", [NT, 2048])

    def ret_mixer(self, li):
        P = self.P
        KT, QT, VTOK, GTOK, OF = (self.dram[k] for k in ("KT", "QT", "VTOK", "GTOK", "OF"))
        win = self.dram["ret_w_in"].rearrange("(kc p) n -> p kc n", p=128)
        ident = self.sb("ident", [128, 128], F32)
        if not hasattr(self, "_ident_loaded"):
            P.dma("sp", ident[:], self.dram["ident"], w=["ident"])
            self._ident_loaded = True
        self.ident = ident
        with ExitStack() as hst:
            self.alloc_ht(hst)
            HT = self.HT
            self.norm_phase(self.A1, 0, ctx=True)
            with ExitStack() as st:
                rope = self.sb("retrope", [128, 2, 4, 64], F32, st)
                for a in range(2):
                    for b4 in range(4):
                        P.dma("sp", rope[:, a, b4, :], self.dram["ret_rope"][a, b4], w=["retrope"])
                w1 = [self.sb(f"retw1{i}", [128, 8, 512], BF16, st) for i in range(2)]
                w2 = [self.sb(f"retw2{i}", [128, 8, 512], BF16, st) for i in range(2)]
                m1 = [self.sb(f"retm1{i}", [128, 512], F32, st) for i in range(2)]
                m2 = [self.sb(f"retm2{i}", [128, 512], F32, st) for i in range(2)]
                cnt = 0
                gcount = 0
                for a, (col0, permname, dst) in enumerate(((0, "ret_wk_perm", KT), (3072, "ret_wq_perm", QT))):
                    pv = self.dram[permname].rearrange("(kc p) n -> p kc n", p=128)
                    for g in range(2):
                        gb = gcount % 2
                        gcount += 1
                        P.dma("pool", w1[gb][:], win[:, :, col0 + g * 512:col0 + (g + 1) * 512], w=[f"retw1{gb}"])
                        P.dma("pool", w2[gb][:], pv[:, :, g * 512:(g + 1) * 512], w=[f"retw2{gb}"])
                        for ol in range(4):
                            oc = 4 * g + ol
                            typ = oc % 2
                            for (seg, t0, n) in self.tiles(ctx=True):
                                tok0 = t0 if seg == 0 else LC + t0
                                c0 = ht_col(tok0)
                                b = cnt % 2
                                cnt += 1
                                ps1, ps2 = self.psum[2 * b], self.psum[2 * b + 1]
                                k1, k2 = f"ps{2 * b}", f"ps{2 * b + 1}"

                                def mmA(e, ps, wt, ol=ol, c0=c0, n=n):
                                    ins = None
                                    for kc in range(8):
                                        ins = e.matmul(ps[:, :n], lhsT=wt[:, kc, ol * 128:(ol + 1) * 128], rhs=HT[:, kc, c0:c0 + n], start=(kc == 0), stop=(kc == 7))
                                    return ins

                                hk = bkeys("HT", c0, c0 + n)
                                P.op("pe", lambda e, ps1=ps1, gb=gb, ol=ol, c0=c0, n=n: mmA(e, ps1, w1[gb], ol, c0, n), r=[f"retw1{gb}"] + hk, w=[k1])
                                o_ = m1[b][:, :n]
                                if seg == 0:
                                    P.op("act", lambda e, o_=o_, ps1=ps1, n=n, a=a: e.activation(out=o_, in_=ps1[:, :n], func=AF.Copy, scale=(1.0 if a == 0 else 1.0 / 16)), r=[k1], w=[f"retm1{b}"])
                                else:
                                    P.op("pe", lambda e, ps2=ps2, gb=gb, ol=ol, c0=c0, n=n: mmA(e, ps2, w2[gb], ol, c0, n), r=[f"retw2{gb}"] + hk, w=[k2])
                                    r0 = t0 // 64
                                    nr = n // 64
                                    if typ == 0:
                                        ct = rope[:, a, 0, r0:r0 + nr].unsqueeze(2).to_broadcast([128, nr, 64])
                                        sn = rope[:, a, 1, r0:r0 + nr].unsqueeze(2).to_broadcast([128, nr, 64])
                                    else:
                                        ct = rope[:, a, 2, :].unsqueeze(1).to_broadcast([128, nr, 64])
                                        sn = rope[:, a, 3, :].unsqueeze(1).to_broadcast([128, nr, 64])
                                    o3 = m1[b][:, :n].rearrange("p (r c) -> p r c", c=64)
                                    t3 = m2[b][:, :n].rearrange("p (r c) -> p r c", c=64)
                                    P.op("dve", lambda e, o3=o3, ps1=ps1, ct=ct, n=n: e.tensor_tensor(out=o3, in0=ps1[:, :n].rearrange("p (r c) -> p r c", c=64), in1=ct, op=ALU.mult), r=[k1, "retrope"], w=[f"retm1{b}"])
                                    P.op("dve", lambda e, t3=t3, ps2=ps2, sn=sn, n=n: e.tensor_tensor(out=t3, in0=ps2[:, :n].rearrange("p (r c) -> p r c", c=64), in1=sn, op=ALU.mult), r=[k2, "retrope"], w=[f"retm2{b}"])
                                    P.op("pool", lambda e, o_=o_, b=b, n=n: e.tensor_tensor(out=o_, in0=o_, in1=m2[b][:, :n], op=ALU.add), r=[f"retm1{b}", f"retm2{b}"], w=[f"retm1{b}"])
                                P.dma("sp", dst[oc * 128:(oc + 1) * 128, tok0:tok0 + n], o_, r=[f"retm1{b}"], w=bkeys(dst.name if hasattr(dst, "name") else str(a), tok0, tok0 + n))
                P.barrier()
            with ExitStack() as st:
                wv = [self.sb(f"retwv{i}", [128, 8, 512], BF16, st) for i in range(2)]
                og = [self.sb(f"retog{i}", [128, 512], F32, st) for i in range(2)]
                cnt = 0
                for gi in range(8):
                    is_gate = gi >= 4
                    col0 = (1024 if not is_gate else 4096) + (gi % 4) * 512
                    dst = GTOK if is_gate else VTOK
                    dkey = "GTOK" if is_gate else "VTOK"
                    gb = gi % 2
                    P.dma("pool", wv[gb][:], win[:, :, col0:col0 + 512], w=[f"retwv{gb}"])
                    for blk in range(NT // 128):
                        tok0 = blk * 128
                        c0 = ht_col(tok0)
                        b = cnt % 2
                        cnt += 1
                        ps = self.psum[b]

                        def mm(e, ps=ps, gb=gb, c0=c0):
                            ins = None
                            for kc in range(8):
                                ins = e.matmul(ps[:, :512], lhsT=HT[:, kc, c0:c0 + 128], rhs=wv[gb][:, kc, :], start=(kc == 0), stop=(kc == 7))
                            return ins

                        P.op("pe", mm, r=[f"retwv{gb}"] + bkeys("HT", c0, c0 + 128), w=[f"ps{b}"])
                        P.op("act", lambda e, ps=ps, b=b, is_gate=is_gate: e.activation(out=og[b][:], in_=ps[:, :512], func=(AF.Silu if is_gate else AF.Copy)), r=[f"ps{b}"], w=[f"retog{b}"])
                        P.dma("sp", dst[tok0:tok0 + 128, (gi % 4) * 512:(gi % 4 + 1) * 512], og[b][:], r=[f"retog{b}"], w=bkeys(dkey, tok0, tok0 + 128))
                P.barrier()
        ktv = KT.rearrange("(c p) t -> p c t", p=128)
        qtv = QT.rearrange("(c p) t -> p c t", p=128)
        xlv = self.dram["XL"].rearrange("(kc p) t -> p kc t", p=128)
        xcv = self.dram["XC"].rearrange("(kc p) t -> p kc t", p=128)
        gam = [1.0 - 2.0 ** (-5 - h) for h in range(4)]
        with ExitStack() as st:
            wo = self.sb("retwo", [128, 16, D], BF16, st)
            wsrc = self.dram["ret_w_out"].rearrange("(ec p) n -> p ec n", p=128)
            for hh in range(2):
                P.dma("pool", wo[:, hh * 8:(hh + 1) * 8, :], wsrc[:, hh * 8:(hh + 1) * 8, :], w=[f"retwo{hh}"])
            ng = self.sb("retng", [128, 2048], F32, st)
            P.dma("sp", ng[:], self.dram["ret_ng"], w=["retng"])
            DT = self.sb("retDT", [128, 4, 128], F32, st)
            XI = self.sb("retXI", [128, 4, 128], F32, st)
            ZE = self.sb("retZE", [128, 4], F32, st)
            rst = [self.sb(f"retr{h}", [128, 2, 512], F32, st) for h in range(4)]
            kt = [self.sb(f"retkt{i}", [128, 8, 128], F32, st) for i in range(2)]
            qt = [self.sb(f"retqt{i}", [128, 8, 128], F32, st) for i in range(2)]
            qx = self.sb("retqx", [128, 8, 128], F32, st)
            vt = [self.sb(f"retvt{i}", [128, 2048], F32, st) for i in range(2)]
            kk = self.sb("retkk", [128, 1024], F32, st)
            sm = [self.sb(f"retsm{i}", [128, 128], F32, st) for i in range(2)]
            ost = [self.sb(f"retost{i}", [128, 2048], F32, st) for i in range(2)]
            oft = self.sb("retoft", [128, 2048], F32, st)
            gt = self.sb("retgt", [128, 2048], F32, st)
            ss = self.sb("retss", [128, 4], F32, st)
            junk = self.sb("retjunk", [128, 512], F32, st)
            mT = self.sb("retmT", [128, 16, 128], BF16, st)
            xt = self.sb("retxt", [128, 8, 128], F32, st)
            for d in range(2):
                P.dma("sp", DT[:], self.dram["ret_DT"][d], w=["retDT"])
                P.dma("sp", XI[:], self.dram["ret_XI"][d], w=["retXI"])
                P.dma("sp", ZE[:], self.dram["ret_ZETA"][d], w=["retZE"])
                for h in range(4):
                    P.op("dve", lambda e, h=h: e.memset(rst[h][:], 0.0), w=[f"retr{h}"])
                order = list(range(34)) if d == 0 else [1, 0] + list(range(33, 1, -1))
                hc = 0
                for ci, blk in enumerate(order):
                    tok0 = blk * 128
                    b = ci % 2
                    P.dma("sp", kt[b][:], ktv[:, :, tok0:tok0 + 128], r=bkeys("0", tok0, tok0 + 128), w=[f"retkt{b}"])
                    P.dma("sp", qt[b][:], qtv[:, :, tok0:tok0 + 128], r=bkeys("1", tok0, tok0 + 128), w=[f"retqt{b}"])
                    P.dma("sp", vt[b][:], VTOK[tok0:tok0 + 128, :], r=bkeys("VTOK", tok0, tok0 + 128), w=[f"retvt{b}"])
                    if d == 1:
                        P.dma("sp", oft[:], OF[tok0:tok0 + 128, :], r=bkeys("OF", tok0, tok0 + 128), w=["retoft"])
                        P.dma("sp", gt[:], GTOK[tok0:tok0 + 128, :], r=bkeys("GTOK", tok0, tok0 + 128), w=["retgt"])
                    psk = (self.psum[0], self.psum[1])

                    def trk(e, b=b):
                        ins = None
                        for c in range(8):
                            ins = e.transpose(out=psk[c // 4][:, (c % 4) * 128:(c % 4 + 1) * 128], in_=kt[b][:, c, :], identity=ident[:])
                        return ins

                    P.op("pe", trk, r=[f"retkt{b}", "ident"], w=["ps0", "ps1"])
                    for h in range(4):
                        P.op("act", lambda e, h=h: e.activation(out=kk[:, h * 256:(h + 1) * 256], in_=psk[h // 2][:, (h % 2) * 256:(h % 2 + 1) * 256], func=AF.Copy, scale=ZE[:, h:h + 1]), r=["ps0", "ps1", "retZE"], w=[f"retkk{h}"])
                    P.op("dve", lambda e, b=b: e.tensor_tensor(out=qx[:].rearrange("p (h c) t -> p h c t", c=2), in0=qt[b][:].rearrange("p (h c) t -> p h c t", c=2), in1=XI[:].unsqueeze(2).to_broadcast([128, 4, 2, 128]), op=ALU.mult), r=[f"retqt{b}", "retXI"], w=["retqx"])
                    for h in range(4):
                        sb_ = hc % 2
                        hc += 1
                        psS = self.psum[2 + sb_]
                        psO = self.psum[4 + sb_]
                        kS, kO = f"ps{2 + sb_}", f"ps{4 + sb_}"

                        def mmS(e, psS=psS, b=b, h=h):
                            ins = None
                            for dc in range(2):
                                ins = e.matmul(psS[:, 0:128], lhsT=kt[b][:, 2 * h + dc, :], rhs=qt[b][:, 2 * h + dc, :], start=(dc == 0), stop=(dc == 1))
                            return ins

                        P.op("pe", mmS, r=[f"retkt{b}", f"retqt{b}"], w=[kS])
                        P.op("dve", lambda e, psS=psS, sb_=sb_, h=h: e.tensor_tensor(out=sm[sb_][:], in0=psS[:, 0:128], in1=DT[:, h, :], op=ALU.mult), r=[kS, "retDT"], w=[f"retsm{sb_}"])

                        def mmO(e, psO=psO, sb_=sb_, b=b, h=h):
                            e.matmul(psO[:, :512], lhsT=sm[sb_][:], rhs=vt[b][:, h * 512:(h + 1) * 512], start=True, stop=False)
                            ins = None
                            for dc in range(2):
                                ins = e.matmul(psO[:, :512], lhsT=qx[:, 2 * h + dc, :], rhs=rst[h][:, dc, :], start=False, stop=(dc == 1))
                            return ins

                        P.op("pe", mmO, r=[f"retsm{sb_}", f"retvt{b}", "retqx", f"retr{h}"], w=[kO])
                        for dc in range(2):
                            psKV = self.psum[6 + dc]

                            def mmKV(e, psKV=psKV, b=b, h=h, dc=dc):
                                return e.matmul(psKV[:, :512], lhsT=kk[:, (2 * h + dc) * 128:(2 * h + dc + 1) * 128], rhs=vt[b][:, h * 512:(h + 1) * 512], start=True, stop=True)

                            P.op("pe", mmKV, r=[f"retkk{h}", f"retvt{b}"], w=[f"ps{6 + dc}"])
                            P.op("dve", lambda e, psKV=psKV, h=h, dc=dc: e.scalar_tensor_tensor(out=rst[h][:, dc, :], in0=rst[h][:, dc, :], scalar=float(gam[h] ** 128), in1=psKV[:, :512], op0=ALU.mult, op1=ALU.add), r=[f"ps{6 + dc}", f"retr{h}"], w=[f"retr{h}"])
                        if d == 0:
                            P.op("act", lambda e, psO=psO, b=b, h=h: e.activation(out=ost[b][:, h * 512:(h + 1) * 512], in_=psO[:, :512], func=AF.Copy), r=[kO], w=[f"retost{b}h{h}"])
                        else:
                            P.op("dve", lambda e, psO=psO, h=h: e.tensor_tensor(out=oft[:, h * 512:(h + 1) * 512], in0=oft[:, h * 512:(h + 1) * 512], in1=psO[:, :512], op=ALU.add), r=[kO, "retoft"], w=[f"retofs{h}"])
                            P.op("act", lambda e, h=h: e.activation(out=junk[:], in_=oft[:, h * 512:(h + 1) * 512], func=AF.Square, accum_out=ss[:, h:h + 1]), r=[f"retofs{h}"], w=["retjunk", f"retss{h}"])
                    if d == 0:
                        P.dma("sp", OF[tok0:tok0 + 128, :], ost[b][:], r=[f"retost{b}h{h}" for h in range(4)], w=bkeys("OF", tok0, tok0 + 128))
                        continue
                    ssk = [f"retss{h}" for h in range(4)]
                    P.op("act", lambda e: e.activation(out=ss[:], in_=ss[:], func=AF.Sqrt, scale=1.0 / 512, bias=self.epsc[:]), r=ssk + ["epsc"], w=["retss"])
                    P.op("dve", lambda e: e.reciprocal(out=ss[:], in_=ss[:]), r=["retss"], w=["retss"])
                    for h in range(4):
                        P.op("dve", lambda e, h=h: e.tensor_scalar(out=oft[:, h * 512:(h + 1) * 512], in0=oft[:, h * 512:(h + 1) * 512], scalar1=ss[:, h:h + 1], scalar2=None, op0=ALU.mult), r=[f"retofs{h}", "retss"], w=[f"retofs{h}"])
                    ofk = [f"retofs{h}" for h in range(4)]
                    P.op("pool", lambda e: e.tensor_tensor(out=gt[:], in0=gt[:], in1=ng[:], op=ALU.mult), r=["retgt", "retng"], w=["retgt"])
                    P.op("pool", lambda e: e.tensor_tensor(out=oft[:], in0=oft[:], in1=gt[:], op=ALU.mult), r=ofk + ["retgt"], w=["retoft"] + ofk)
                    for eg in range(4):
                        pst = self.psum[eg % 2]

                        def tro(e, pst=pst, eg=eg):
                            ins = None
                            for e4 in range(4):
                                ec = 4 * eg + e4
                                ins = e.transpose(out=pst[:, e4 * 128:(e4 + 1) * 128], in_=oft[:, ec * 128:(ec + 1) * 128], identity=ident[:])
                            return ins

                        P.op("pe", tro, r=["retoft", "ident"], w=[f"ps{eg % 2}"])
                        P.op("act", lambda e, pst=pst, eg=eg: e.activation(out=mT[:, 4 * eg:4 * eg + 4, :].rearrange("p a t -> p (a t)"), in_=pst[:, :512], func=AF.Copy), r=[f"ps{eg % 2}"], w=[f"retmT{eg}"])
                    seg = 0 if blk < 2 else 1
                    t0 = tok0 if seg == 0 else tok0 - LC
                    xv = xcv if seg == 0 else xlv
                    xname = "XC" if seg == 0 else "XL"
                    s_ = 1 if seg == 0 else 0
                    P.dma("sp", xt[:], xv[:, :, t0:t0 + 128], r=bkeys(xname, t0, t0 + 128), w=["retxt"])
                    for oc in range(8):
                        psP = self.psum[2 + oc % 2]

                        def mmP(e, psP=psP, oc=oc):
                            ins = None
                            for ec in range(16):
                                ins = e.matmul(psP[:, 0:128], lhsT=wo[:, ec, oc * 128:(oc + 1) * 128], rhs=mT[:, ec, :], start=(ec == 0), stop=(ec == 15))
                            return ins

                        P.op("pe", mmP, r=["retwo0", "retwo1"] + [f"retmT{eg}" for eg in range(4)], w=[f"ps{2 + oc % 2}"])
                        P.op("dve", lambda e, psP=psP, oc=oc, s_=s_: e.scalar_tensor_tensor(out=xt[:, oc, :], in0=psP[:, 0:128], scalar=self.modv[:, 16 + oc, s_:s_ + 1], in1=xt[:, oc, :], op0=ALU.mult, op1=ALU.add), r=[f"ps{2 + oc % 2}", "modv", "retxt"], w=["retxt"])
                    P.dma("sp", xv[:, :, t0:t0 + 128], xt[:], r=["retxt"], w=bkeys(xname, t0, t0 + 128))
            P.barrier()

    def declare_gdn(self):
        self.din("gdn_w_in", [D, 6208])
        self.din("gdn_w_out", [2048, D])
        self.din("i_gdn_cw", [128, 32, 4])
        self.din("gdn_ng", [128, 128])
        self.din("gdn_ab", [128, 2, 2, 16])
        self.din("gdn_c64", [64, 4, 64])
        self.din("gdn_ones", [64, 128])
        if "ident" not in self.dram:
            self.din("ident", [128, 128])
        self.dscr("GK", [NT, 1024])
        self.dscr("GV", [NT, 2048])
        self.dscr("GQ", [NT, 1024])
        self.dscr("GZ", [NT, 2048])
        self.dscr("GGB", [NT, 64])
        self.dscr("GOF", [NT, 2048])
        self.dscr("GNM", [2176, 64, 64])
        self.dscr("GTM", [2176, 64, 64])

    def gdn_mixer(self, li):
        P = self.P
        GK, GV, GQ, GZ, GGB, GOF, GNM, GTM = (self.dram[k] for k in ("GK", "GV", "GQ", "GZ", "GGB", "GOF", "GNM", "GTM"))
        win = self.dram["gdn_w_in"].rearrange("(kc p) n -> p kc n", p=128)
        ident = self.sb("identg", [128, 128], F32)
        P.dma("sp", ident[:], self.dram["ident"], w=["identg"])
        onec = self.sb("gdnone", [128, 1], F32)
        P.op("dve", lambda e: e.memset(onec[:], 1.0), w=["gdnone"])
        eps6 = self.sb("gdneps6", [128, 1], F32)
        P.op("dve", lambda e: e.memset(eps6[:], 1e-6), w=["gdneps6"])
        with ExitStack() as hst:
            self.alloc_ht(hst)
            HT = self.HT
            self.norm_phase(self.A1, 0, ctx=True)
            with ExitStack() as st:
                cw = self.sb("gdncw", [128, 32, 4], F32, st)
                P.dma("sp", cw[:], self.dram["i_gdn_cw"], w=["gdncw"])
                wb = [self.sb(f"gdnw{i}", [128, 8, 512], BF16, st) for i in range(2)]
                xbl = self.sb("gdnxbl", [128, L + 3], F32, st)
                xbc = self.sb("gdnxbc", [128, LC + 3], F32, st)
                xsl = self.sb("gdnxsl", [128, L], F32, st)
                xsc = self.sb("gdnxsc", [128, LC], F32, st)
                tst = [self.sb(f"gdntst{i}", [128, 4, 128], F32, st) for i in range(2)]
                ssq = self.sb("gdnssq", [128, 4], F32, st)
                P.op("dve", lambda e: e.memset(ssq[:], 1.0), w=["gdnssq"])
                junk = self.sb("gdnjunk", [128, 128], F32, st)
                for bufx, nn in ((xbl, L), (xbc, LC)):
                    P.op("dve", lambda e, bufx=bufx: e.memset(bufx[:, 0:2], 0.0), w=["gdnxb"])
                    P.op("dve", lambda e, bufx=bufx, nn=nn: e.memset(bufx[:, nn + 2:nn + 3], 0.0), w=["gdnxb"])
                cnt = 0
                tcnt = 0
                for g in range(8):
                    gb = g % 2
                    col0 = g * 512 if g < 6 else 3136 + (g - 6) * 512
                    P.dma("pool", wb[gb][:], win[:, :, col0:col0 + 512], w=[f"gdnw{gb}"])
                    for ol in range(4):
                        fc = 4 * g + ol
                        kind = "k" if fc < 8 else ("v" if fc < 24 else "q")
                        use_ctx = kind != "q"
                        dst, dcol = (GK, fc * 128) if kind == "k" else ((GV, (fc - 8) * 128) if kind == "v" else (GQ, (fc - 24) * 128))
                        dkey = {"k": "GK", "v": "GV", "q": "GQ"}[kind]
                        for (seg, t0, n) in self.tiles(ctx=use_ctx):
                            tok0 = t0 if seg == 0 else LC + t0
                            c0 = ht_col(tok0)
                            b = cnt % 2
                            cnt += 1
                            ps = self.psum[b]

                            def mm(e, ps=ps, gb=gb, ol=ol, c0=c0, n=n):
                                ins = None
                                for kc in range(8):
                                    ins = e.matmul(ps[:, :n], lhsT=wb[gb][:, kc, ol * 128:(ol + 1) * 128], rhs=HT[:, kc, c0:c0 + n], start=(kc == 0), stop=(kc == 7))
                                return ins

                            P.op("pe", mm, r=[f"gdnw{gb}"] + bkeys("HT", c0, c0 + n), w=[f"ps{b}"])
                            dstb = xbc[:, 2 + t0:2 + t0 + n] if seg == 0 else xbl[:, 2 + t0:2 + t0 + n]
                            P.op("act", lambda e, ps=ps, dstb=dstb, n=n: e.activation(out=dstb, in_=ps[:, :n], func=AF.Copy), r=[f"ps{b}"], w=["gdnxb"])
                        segs = ((xbc, xsc, LC, 0), (xbl, xsl, L, LC)) if use_ctx else ((xbl, xsl, L, LC),)
                        for (bufx, xs, nn, tokb) in segs:
                            o = xs[:, :nn]
                            P.op("dve", lambda e, o=o, bufx=bufx, nn=nn, fc=fc: e.tensor_scalar(out=o, in0=bufx[:, 0:nn], scalar1=cw[:, fc, 0:1], scalar2=None, op0=ALU.mult), r=["gdnxb", "gdncw"], w=["gdnxs"])
                            for kk in range(1, 4):
                                P.op("dve", lambda e, o=o, bufx=bufx, nn=nn, fc=fc, kk=kk: e.scalar_tensor_tensor(out=o, in0=bufx[:, kk:kk + nn], scalar=cw[:, fc, kk:kk + 1], in1=o, op0=ALU.mult, op1=ALU.add), r=["gdnxb", "gdncw", "gdnxs"], w=["gdnxs"])
                            P.op("act", lambda e, o=o: e.activation(out=o, in_=o, func=AF.Silu), r=["gdnxs"], w=["gdnxs"])
                            for q4 in range(nn // 512 if nn >= 512 else 1):
                                nb = 4 if nn >= 512 else nn // 128
                                tb = tcnt % 2
                                tcnt += 1
                                pst = self.psum[2 + tb]

                                def tr(e, pst=pst, xs=xs, q4=q4, nb=nb):
                                    ins = None
                                    for i4 in range(nb):
                                        ins = e.transpose(out=pst[:, i4 * 128:(i4 + 1) * 128], in_=xs[:, q4 * 512 + i4 * 128:q4 * 512 + (i4 + 1) * 128], identity=ident[:])
                                    return ins

                                P.op("pe", tr, r=["gdnxs", "identg"], w=[f"ps{2 + tb}"])
                                tk = f"gdntst{tb}"
                                if kind == "v":
                                    P.op("act", lambda e, pst=pst, tb=tb, nb=nb: e.activation(out=tst[tb][:, :nb, :].rearrange("p a d -> p (a d)"), in_=pst[:, :nb * 128], func=AF.Copy), r=[f"ps{2 + tb}"], w=[tk])
                                else:
                                    for i4 in range(nb):
                                        P.op("act", lambda e, pst=pst, i4=i4: e.activation(out=junk[:], in_=pst[:, i4 * 128:(i4 + 1) * 128], func=AF.Square, accum_out=ssq[:, i4:i4 + 1]), r=[f"ps{2 + tb}"], w=["gdnjunk", "gdnssq"])
                                    P.op("act", lambda e: e.activation(out=ssq[:], in_=ssq[:], func=AF.Sqrt, bias=eps6[:]), r=["gdnssq", "gdneps6"], w=["gdnssq"])
                                    P.op("dve", lambda e: e.reciprocal(out=ssq[:], in_=ssq[:]), r=["gdnssq"], w=["gdnssq"])
                                    sc2 = 1.0 if kind == "k" else 128.0 ** -0.5
                                    for i4 in range(nb):
                                        P.op("dve", lambda e, pst=pst, tb=tb, i4=i4, sc2=sc2: e.tensor_scalar(out=tst[tb][:, i4, :], in0=pst[:, i4 * 128:(i4 + 1) * 128], scalar1=ssq[:, i4:i4 + 1], scalar2=sc2, op0=ALU.mult, op1=ALU.mult), r=[f"ps{2 + tb}", "gdnssq"], w=[tk])
                                r0 = tokb + q4 * 512
                                P.dma("sp", dst[r0:r0 + nb * 128, dcol:dcol + 128].rearrange("(a p) c -> p a c", p=128), tst[tb][:, :nb, :], r=[tk], w=bkeys(dkey, r0, r0 + nb * 128))
                P.barrier()
            with ExitStack() as st:
                wz = [self.sb(f"gdnwz{i}", [128, 8, 512], BF16, st) for i in range(2)]
                wba = self.sb("gdnwba", [128, 8, 64], BF16, st)
                og = [self.sb(f"gdnog{i}", [128, 512], F32, st) for i in range(2)]
                ab = self.sb("gdnab", [128, 2, 2, 16], F32, st)
                negA = self.sb("gdnnegA", [128, 2, 16], F32, st)
                P.dma("sp", ab[:], self.dram["gdn_ab"], w=["gdnab"])
                P.op("act", lambda e: e.activation(out=negA[:], in_=ab[:, 0], func=AF.Exp), r=["gdnab"], w=["gdnnegA"])
                P.op("dve", lambda e: e.tensor_scalar(out=negA[:], in0=negA[:], scalar1=-1.0, scalar2=None, op0=ALU.mult), r=["gdnnegA"], w=["gdnnegA"])
                P.dma("pool", wba[:], win[:, :, 3072:3136], w=["gdnwba"])
                gt_ = {k: self.sb("gdng" + k, [128, 2, 16], F32, st) for k in ("x", "nx", "e", "o")}
                gbo = [self.sb(f"gdngbo{i}", [128, 64], F32, st) for i in range(2)]
                cnt = 0
                for blk in range(NT // 128):
                    tok0 = blk * 128
                    c0 = ht_col(tok0)
                    b = cnt % 2
                    cnt += 1
                    ps = self.psum[b]

                    def mmb(e, ps=ps, c0=c0):
                        ins = None
                        for kc in range(8):
                            ins = e.matmul(ps[:, :64], lhsT=HT[:, kc, c0:c0 + 128], rhs=wba[:, kc, :], start=(kc == 0), stop=(kc == 7))
                        return ins

                    P.op("pe", mmb, r=["gdnwba"] + bkeys("HT", c0, c0 + 128), w=[f"ps{b}"])
                    ps4 = ps[:, :64].rearrange("p (d s h) -> p d s h", d=2, s=2)
                    gb4 = gbo[b][:].rearrange("p (d s h) -> p d s h", d=2, s=2)
                    kq = "gdngt"
                    x_, nx_, e_, o_ = (gt_[k][:] for k in ("x", "nx", "e", "o"))
                    P.op("act", lambda e, ps4=ps4, gb4=gb4: e.activation(out=gb4[:, :, 0, :], in_=ps4[:, :, 0, :], func=AF.Sigmoid), r=[f"ps{b}"], w=[f"gdngbo{b}"])
                    P.op("dve", lambda e, ps4=ps4, x_=x_: e.tensor_tensor(out=x_, in0=ps4[:, :, 1, :], in1=ab[:, 1], op=ALU.add), r=[f"ps{b}", "gdnab", kq], w=[kq])
                    P.op("dve", lambda e, x_=x_, nx_=nx_: e.tensor_scalar(out=nx_, in0=x_, scalar1=-1.0, scalar2=None, op0=ALU.mult), r=[kq], w=[kq])
                    P.op("dve", lambda e, x_=x_, nx_=nx_: e.tensor_tensor(out=nx_, in0=nx_, in1=x_, op=ALU.max), r=[kq], w=[kq])
                    P.op("act", lambda e, nx_=nx_, e_=e_: e.activation(out=e_, in_=nx_, func=AF.Exp, scale=-1.0), r=[kq], w=[kq])
                    P.op("act", lambda e, e_=e_: e.activation(out=e_, in_=e_, func=AF.Ln, bias=onec[:]), r=[kq, "gdnone"], w=[kq])
                    P.op("dve", lambda e, x_=x_: e.tensor_scalar(out=x_, in0=x_, scalar1=0.0, scalar2=None, op0=ALU.max), r=[kq], w=[kq])
                    P.op("dve", lambda e, x_=x_, e_=e_: e.tensor_tensor(out=x_, in0=x_, in1=e_, op=ALU.add), r=[kq], w=[kq])
                    P.op("dve", lambda e, x_=x_, gb4=gb4: e.tensor_tensor(out=gb4[:, :, 1, :], in0=x_, in1=negA[:], op=ALU.mult), r=[kq, "gdnnegA", f"gdngbo{b}"], w=[f"gdngbo{b}"])
                    P.dma("sp", GGB[tok0:tok0 + 128, :], gbo[b][:], r=[f"gdngbo{b}"], w=bkeys("GGB", tok0, tok0 + 128))
                cnt = 0
                for gi in range(4):
                    gb = gi % 2
                    P.dma("pool", wz[gb][:], win[:, :, 4160 + gi * 512:4160 + (gi + 1) * 512], w=[f"gdnwz{gb}"])
                    for blk in range(2, NT // 128):
                        tok0 = blk * 128
                        c0 = ht_col(tok0)
                        b = cnt % 2
                        cnt += 1
                        ps = self.psum[2 + b]

                        def mmz(e, ps=ps, gb=gb, c0=c0):
                            ins = None
                            for kc in range(8):
                                ins = e.matmul(ps[:, :512], lhsT=HT[:, kc, c0:c0 + 128], rhs=wz[gb][:, kc, :], start=(kc == 0), stop=(kc == 7))
                            return ins

                        P.op("pe", mmz, r=[f"gdnwz{gb}"] + bkeys("HT", c0, c0 + 128), w=[f"ps{2 + b}"])
                        P.op("act", lambda e, ps=ps, b=b: e.activation(out=og[b][:], in_=ps[:, :512], func=AF.Silu), r=[f"ps{2 + b}"], w=[f"gdnog{b}"])
                        P.dma("sp", GZ[tok0:tok0 + 128, gi * 512:(gi + 1) * 512], og[b][:], r=[f"gdnog{b}"], w=bkeys("GZ", tok0, tok0 + 128))
                P.barrier()
        def chunk_order(d):
            return list(range(68)) if d == 0 else [3, 2, 1, 0] + list(range(67, 3, -1))

        rltmp = self.sb("gdnrltmp", [64, 2048], F32)

        def rows_load(dst, src, tok0, ncols, d, rkeys, wkey):
            if d == 0:
                P.dma("sp", dst, src[tok0:tok0 + 64], r=rkeys, w=[wkey])
                return
            P.dma("sp", rltmp[:, :ncols], src[tok0:tok0 + 64], r=rkeys, w=["gdnrltmp"])
            for c0 in range(0, ncols, 512):
                n = min(512, ncols - c0)
                P.op("pe", lambda e, c0=c0, n=n: e.matmul(self.psum[0][0:64, :n], lhsT=c64[:, 3, :], rhs=rltmp[:, c0:c0 + n], start=True, stop=True), r=["gdnrltmp", "gdnc64"], w=["ps0"])
                P.op("act", lambda e, c0=c0, n=n, dst=dst: e.activation(out=dst[:, c0:c0 + n], in_=self.psum[0][0:64, :n], func=AF.Copy), r=["ps0"], w=[wkey])

        c64 = self.sb("gdnc64", [64, 4, 64], F32)
        P.dma("sp", c64[:], self.dram["gdn_c64"], w=["gdnc64"])
        ones64 = self.sb("gdnones", [64, 128], F32)
        P.dma("sp", ones64[:], self.dram["gdn_ones"], w=["gdnones"])
        LTm, mLs, mU = c64[:, 0, :], c64[:, 1, :], c64[:, 2, :]

        def gates(st_bufs, gbt, d, bkey, tag):
            G3, gcum = st_bufs
            graw = gbt[:, d * 32 + 16:d * 32 + 32]
            P.op("dve", lambda e: e.tensor_tensor(out=G3[:], in0=LTm.unsqueeze(1).to_broadcast([64, 16, 64]), in1=graw.unsqueeze(2).to_broadcast([64, 16, 64]), op=ALU.mult), r=[bkey, "gdnc64"], w=[tag + "G3"])
            psg = self.psum[1]
            P.op("pe", lambda e: e.matmul(psg[0:64, 0:16], lhsT=LTm, rhs=graw, start=True, stop=True), r=[bkey, "gdnc64"], w=["ps1"])
            P.op("act", lambda e: e.activation(out=gcum[:], in_=psg[0:64, 0:16], func=AF.Copy), r=["ps1"], w=[tag + "gcum"])
            G3f = G3[:].rearrange("p h i -> p (h i)")

            def mmD(e):
                e.matmul(self.psum[2][0:64, :512], lhsT=ones64[:, 0:64], rhs=G3f[:, 0:512], start=True, stop=True)
                return e.matmul(self.psum[3][0:64, :512], lhsT=ones64[:, 0:64], rhs=G3f[:, 512:1024], start=True, stop=True)

            P.op("pe", mmD, r=[tag + "G3", "gdnones"], w=["ps2", "ps3"])

        def decay_from(out3, gcum, transposed, mask, tag, wkey):
            for half in range(2):
                psd = self.psum[2 + half][0:64, :512].rearrange("p (h i) -> p h i", i=64)
                gb_ = gcum[:, half * 8:(half + 1) * 8].unsqueeze(2).to_broadcast([64, 8, 64])
                o = out3[:, half * 8:(half + 1) * 8, :]
                if transposed:
                    P.op("dve", lambda e, o=o, psd=psd, gb_=gb_: e.tensor_tensor(out=o, in0=psd, in1=gb_, op=ALU.subtract), r=[f"ps{2 + half}", tag + "gcum"], w=[wkey])
                else:
                    P.op("dve", lambda e, o=o, psd=psd, gb_=gb_: e.tensor_tensor(out=o, in0=gb_, in1=psd, op=ALU.subtract), r=[f"ps{2 + half}", tag + "gcum"], w=[wkey])
            P.op("dve", lambda e: e.tensor_scalar(out=out3, in0=out3, scalar1=0.0, scalar2=None, op0=ALU.min), r=[wkey], w=[wkey])
            P.op("act", lambda e: e.activation(out=out3, in_=out3, func=AF.Exp), r=[wkey], w=[wkey])
            P.op("pool", lambda e: e.tensor_tensor(out=out3, in0=out3, in1=mask.unsqueeze(1).to_broadcast([64, 16, 64]), op=ALU.mult), r=[wkey, "gdnc64"], w=[wkey])

        with ExitStack() as st:
            kA_ = [self.sb(f"gdnAk{i}", [64, 1024], F32, st) for i in range(2)]
            gbA_ = [self.sb(f"gdnAgb{i}", [64, 64], F32, st) for i in range(2)]
            G3A = self.sb("gdnAG3A", [64, 16, 64], F32, st)
            gcumA = self.sb("gdnAgcumA", [64, 16], F32, st)
            kTcA = self.sb("gdnAkT", [128, 8, 64], F32, st)
            KKs = self.sb("gdnAKK", [64, 8, 64], F32, st)
            Nm = [self.sb(f"gdnAN{i}", [64, 16, 64], F32, st) for i in range(2)]
            for d in range(2):
                for ci, ch in enumerate(chunk_order(d)):
                    tok0 = ch * 64
                    b = ci % 2
                    rows_load(kA_[b][:], GK, tok0, 1024, d, bkeys("GK", tok0, tok0 + 64), f"gdnAk{b}")
                    rows_load(gbA_[b][:], GGB, tok0, 64, d, bkeys("GGB", tok0, tok0 + 64), f"gdnAgb{b}")
                    gates((G3A, gcumA), gbA_[b], d, f"gdnAgb{b}", "A")
                    decay_from(Nm[b][:], gcumA, False, mLs, "A", f"gdnAN{b}")

                    def trk(e, b=b):
                        ins = None
                        for kh in range(8):
                            ins = e.transpose(out=self.psum[0][:, kh * 64:(kh + 1) * 64], in_=kA_[b][:, kh * 128:(kh + 1) * 128], identity=ident[0:64, 0:64])
                        return ins

                    P.op("pe", trk, r=[f"gdnAk{b}", "identg"], w=["ps0"])
                    P.op("act", lambda e: e.activation(out=kTcA[:].rearrange("p a t -> p (a t)"), in_=self.psum[0][:, :512], func=AF.Copy), r=["ps0"], w=["gdnAkT"])

                    def mmKK(e):
                        ins = None
                        for kh in range(8):
                            ins = e.matmul(self.psum[4][0:64, kh * 64:(kh + 1) * 64], lhsT=kTcA[:, kh, :], rhs=kTcA[:, kh, :], start=True, stop=True)
                        return ins

                    P.op("pe", mmKK, r=["gdnAkT"], w=["ps4"])
                    P.op("act", lambda e: e.activation(out=KKs[:].rearrange("p a t -> p (a t)"), in_=self.psum[4][0:64, :512], func=AF.Copy), r=["ps4"], w=["gdnAKK"])
                    N4 = Nm[b][:].rearrange("p (a r) j -> p a r j", r=2)
                    P.op("dve", lambda e, N4=N4: e.tensor_tensor(out=N4, in0=N4, in1=KKs[:].unsqueeze(2).to_broadcast([64, 8, 2, 64]), op=ALU.mult), r=[f"gdnAN{b}", "gdnAKK"], w=[f"gdnAN{b}"])
                    beta = gbA_[b][:, d * 32:d * 32 + 16]
                    P.op("dve", lambda e, b=b, beta=beta: e.tensor_tensor(out=Nm[b][:], in0=Nm[b][:], in1=beta.unsqueeze(2).to_broadcast([64, 16, 64]), op=ALU.mult), r=[f"gdnAN{b}", f"gdnAgb{b}"], w=[f"gdnAN{b}"])
                    pid0 = (d * 68 + ci) * 16
                    P.dma("sp", GNM[pid0:pid0 + 16].rearrange("h i j -> i h j"), Nm[b][:], r=[f"gdnAN{b}"], w=bkeys("GNM", pid0, pid0 + 16))
            P.barrier()
        with ExitStack() as st:
            Nb = self.sb("gdnBN", [128, 64, 64], F32, st)
            X = self.sb("gdnBX", [128, 64, 64], F32, st)
            pr = self.sb("gdnBpr", [128, 64, 64], F32, st)
            XT = self.sb("gdnBXT", [128, 64, 64], F32, st)
            for bt in range(17):
                P.dma("sp", Nb[:], GNM[bt * 128:(bt + 1) * 128], r=bkeys("GNM", bt * 128, (bt + 1) * 128), w=["gdnBN"])
                P.op("pool", lambda e: e.memset(X[:], 0.0), w=["gdnBX"])
                P.op("pool", lambda e: e.memset(X[:].rearrange("p a b -> p (a b)")[:, 0:4096:65], 1.0), w=["gdnBX"])
                for i in range(1, 64):
                    P.op("dve", lambda e, i=i: e.tensor_tensor(out=pr[:, 0:i, 0:i], in0=X[:, 0:i, 0:i].rearrange("p m j -> p j m"), in1=Nb[:, i, 0:i].unsqueeze(1).to_broadcast([128, i, i]), op=ALU.mult), r=["gdnBN", "gdnBX"], w=["gdnBpr"])
                    P.op("dve", lambda e, i=i: e.tensor_reduce(out=X[:, i, 0:i], in_=pr[:, 0:i, 0:i], axis=AX.X, op=ALU.add, negate=True), r=["gdnBpr"], w=["gdnBX"])
                P.op("pool", lambda e: e.tensor_copy(out=XT[:], in_=X[:].rearrange("p i j -> p j i")), r=["gdnBX"], w=["gdnBXT"])
                P.dma("sp", GTM[bt * 128:(bt + 1) * 128], XT[:], r=["gdnBXT"], w=bkeys("GTM", bt * 128, (bt + 1) * 128))
            P.barrier()
        xlv = self.dram["XL"].rearrange("(kc p) t -> p kc t", p=128)
        with ExitStack() as st:
            wo = self.sb("gdnwo", [128, 16, D], BF16, st)
            wsrc = self.dram["gdn_w_out"].rearrange("(ec p) n -> p ec n", p=128)
            for hh in range(2):
                P.dma("pool", wo[:, hh * 8:(hh + 1) * 8, :], wsrc[:, hh * 8:(hh + 1) * 8, :], w=[f"gdnwo{hh}"])
            ngr = self.sb("gdnng", [64, 128], F32, st)
            P.dma("sp", ngr[:], self.dram["gdn_ng"][0:64, :], w=["gdnng"])
            S = self.sb("gdnS", [128, 16, 128], F32, st)
            kt_ = [self.sb(f"gdnCk{i}", [64, 1024], F32, st) for i in range(2)]
            vt_ = [self.sb(f"gdnCv{i}", [64, 2048], F32, st) for i in range(2)]
            qt_ = [self.sb(f"gdnCq{i}", [64, 1024], F32, st) for i in range(2)]
            gbt_ = [self.sb(f"gdnCgb{i}", [64, 64], F32, st) for i in range(2)]
            TT = [self.sb(f"gdnCT{i}", [64, 16, 64], F32, st) for i in range(2)]
            zt_ = self.sb("gdnCz", [64, 2048], F32, st)
            oft = self.sb("gdnCof", [64, 2048], F32, st)
            G3 = self.sb("gdnCG3", [64, 16, 64], F32, st)
            gcum = self.sb("gdnCgcum", [64, 16], F32, st)
            dT = self.sb("gdnCdT", [64, 16, 64], F32, st)
            eg = self.sb("gdnCeg", [64, 16], F32, st)
            cf = self.sb("gdnCcf", [64, 16], F32, st)
            kd = self.sb("gdnCkd", [64, 16], F32, st)
            egl = self.sb("gdnCegl", [128, 16], F32, st)
            kbg = self.sb("gdnCkbg", [64, 16, 128], F32, st)
            vb = self.sb("gdnCvb", [64, 16, 128], F32, st)
            kdc = self.sb("gdnCkdc", [64, 16, 128], F32, st)
            qeg = self.sb("gdnCqeg", [64, 16, 128], F32, st)
            kTc = self.sb("gdnCkT", [128, 8, 64], F32, st)
            qTc = self.sb("gdnCqT", [128, 8, 64], F32, st)
            qegT = self.sb("gdnCqegT", [128, 16, 64], F32, st)
            usb = [self.sb(f"gdnCu{i}", [64, 128], F32, st) for i in range(2)]
            wTs = [self.sb(f"gdnCwT{i}", [128, 64], F32, st) for i in range(2)]
            vn = [self.sb(f"gdnCvn{i}", [64, 128], F32, st) for i in range(2)]
            am = [self.sb(f"gdnCam{i}", [64, 64], F32, st) for i in range(2)]
            ost = [self.sb(f"gdnCo{i}", [64, 16, 128], F32, st) for i in range(2)]
            ssn = self.sb("gdnCss", [64, 16], F32, st)
            sq = self.sb("gdnCsq", [64, 16, 128], F32, st)
            mT = self.sb("gdnCmT", [128, 16, 64], BF16, st)
            xt = self.sb("gdnCxt", [128, 8, 64], F32, st)
            for d in range(2):
                P.op("dve", lambda e: e.memset(S[:], 0.0), w=["gdnS"])
                hc = 0
                for ci, ch in enumerate(chunk_order(d)):
                    tok0 = ch * 64
                    is_lat = ch >= 4
                    b = ci % 2
                    rows_load(kt_[b][:], GK, tok0, 1024, d, bkeys("GK", tok0, tok0 + 64), f"gdnCk{b}")
                    rows_load(vt_[b][:], GV, tok0, 2048, d, bkeys("GV", tok0, tok0 + 64), f"gdnCv{b}")
                    rows_load(gbt_[b][:], GGB, tok0, 64, d, bkeys("GGB", tok0, tok0 + 64), f"gdnCgb{b}")
                    pid0 = (d * 68 + ci) * 16
                    P.dma("sp", TT[b][:], GTM[pid0:pid0 + 16].rearrange("h j i -> j h i"), r=bkeys("GTM", pid0, pid0 + 16), w=[f"gdnCT{b}"])
                    if is_lat:
                        rows_load(qt_[b][:], GQ, tok0, 1024, d, bkeys("GQ", tok0, tok0 + 64), f"gdnCq{b}")
                        if d == 1:
                            rows_load(zt_[:], GZ, tok0, 2048, d, bkeys("GZ", tok0, tok0 + 64), "gdnCz")
                            rows_load(oft[:], GOF, tok0, 2048, d, bkeys("GOF", tok0, tok0 + 64), "gdnCof")
                    gbk = f"gdnCgb{b}"
                    gates((G3, gcum), gbt_[b], d, gbk, "C")
                    if is_lat:
                        decay_from(dT[:], gcum, True, mU, "C", "gdnCdT")
                    graw = gbt_[b][:, d * 32 + 16:d * 32 + 32]
                    beta = gbt_[b][:, d * 32:d * 32 + 16]
                    P.op("pe", lambda e, graw=graw: e.matmul(self.psum[1][:, 16:32], lhsT=ones64[:, :], rhs=graw, start=True, stop=True), r=[gbk, "gdnones"], w=["ps1"])
                    P.op("act", lambda e: e.activation(out=egl[:], in_=self.psum[1][:, 16:32], func=AF.Exp), r=["ps1"], w=["gdnCegl"])
                    P.op("dve", lambda e: e.tensor_tensor(out=kd[:], in0=self.psum[1][0:64, 16:32], in1=gcum[:], op=ALU.subtract), r=["ps1", "Cgcum"], w=["gdnCkd"])
                    P.op("act", lambda e: e.activation(out=kd[:], in_=kd[:], func=AF.Exp), r=["gdnCkd"], w=["gdnCkd"])
                    P.op("act", lambda e: e.activation(out=eg[:], in_=gcum[:], func=AF.Exp), r=["Cgcum"], w=["gdnCeg"])
                    P.op("dve", lambda e, beta=beta: e.tensor_tensor(out=cf[:], in0=eg[:], in1=beta, op=ALU.mult), r=["gdnCeg", gbk], w=["gdnCcf"])
                    k4 = kt_[b][:].rearrange("p (a c) -> p a c", c=128).unsqueeze(2).to_broadcast([64, 8, 2, 128])

                    def sc4(t16):
                        return t16.rearrange("p (a r) -> p a r", r=2).unsqueeze(3).to_broadcast([64, 8, 2, 128])

                    P.op("pool", lambda e, k4=k4: e.tensor_tensor(out=kbg[:].rearrange("p (a r) c -> p a r c", r=2), in0=k4, in1=sc4(cf[:]), op=ALU.mult), r=[f"gdnCk{b}", "gdnCcf"], w=["gdnCkbg"])
                    P.op("pool", lambda e, k4=k4: e.tensor_tensor(out=kdc[:].rearrange("p (a r) c -> p a r c", r=2), in0=k4, in1=sc4(kd[:]), op=ALU.mult), r=[f"gdnCk{b}", "gdnCkd"], w=["gdnCkdc"])
                    P.op("dve", lambda e, b=b, beta=beta: e.tensor_tensor(out=vb[:], in0=vt_[b][:].rearrange("p (h c) -> p h c", c=128), in1=beta.unsqueeze(2).to_broadcast([64, 16, 128]), op=ALU.mult), r=[f"gdnCv{b}", gbk], w=["gdnCvb"])
                    if is_lat:
                        def trkq(e, b=b):
                            ins = None
                            for kh in range(8):
                                ins = e.transpose(out=self.psum[0][:, kh * 64:(kh + 1) * 64], in_=kt_[b][:, kh * 128:(kh + 1) * 128], identity=ident[0:64, 0:64])
                            return ins

                        P.op("pe", trkq, r=[f"gdnCk{b}", "identg"], w=["ps0"])
                        P.op("act", lambda e: e.activation(out=kTc[:].rearrange("p a t -> p (a t)"), in_=self.psum[0][:, :512], func=AF.Copy), r=["ps0"], w=["gdnCkT"])

                        def trq(e, b=b):
                            ins = None
                            for kh in range(8):
                                ins = e.transpose(out=self.psum[0][:, kh * 64:(kh + 1) * 64], in_=qt_[b][:, kh * 128:(kh + 1) * 128], identity=ident[0:64, 0:64])
                            return ins

                        P.op("pe", trq, r=[f"gdnCq{b}", "identg"], w=["ps0"])
                        P.op("act", lambda e: e.activation(out=qTc[:].rearrange("p a t -> p (a t)"), in_=self.psum[0][:, :512], func=AF.Copy), r=["ps0"], w=["gdnCqT"])
                        q4 = qt_[b][:].rearrange("p (a c) -> p a c", c=128).unsqueeze(2).to_broadcast([64, 8, 2, 128])
                        P.op("pool", lambda e, q4=q4: e.tensor_tensor(out=qeg[:].rearrange("p (a r) c -> p a r c", r=2), in0=q4, in1=sc4(eg[:]), op=ALU.mult), r=[f"gdnCq{b}", "gdnCeg"], w=["gdnCqeg"])
                        for half in range(2):
                            def trqe(e, half=half):
                                ins = None
                                for h8 in range(8):
                                    hv = half * 8 + h8
                                    ins = e.transpose(out=self.psum[2 + half][:, h8 * 64:(h8 + 1) * 64], in_=qeg[:, hv, :], identity=ident[0:64, 0:64])
                                return ins

                            P.op("pe", trqe, r=["gdnCqeg", "identg", "gdnCdT"], w=[f"ps{2 + half}"])
                            P.op("act", lambda e, half=half: e.activation(out=qegT[:, half * 8:(half + 1) * 8, :].rearrange("p a t -> p (a t)"), in_=self.psum[2 + half][:, :512], func=AF.Copy), r=[f"ps{2 + half}"], w=[f"gdnCqegT{half}"])
                    for hv in range(16):
                        kh = hv // 2
                        pb = hc % 2
                        hc += 1
                        psA, psB = self.psum[4 + pb], self.psum[6 + pb]
                        kU, kW, kO, kAt, kWT, kSn = (f"ps{4 + pb}u", f"ps{4 + pb}w", f"ps{4 + pb}o", f"ps{4 + pb}a", f"ps{6 + pb}wT", f"ps{6 + pb}S")
                        P.op("pe", lambda e, psA=psA, b=b, hv=hv: e.matmul(psA[0:64, 0:128], lhsT=TT[b][:, hv, :], rhs=vb[:, hv, :], start=True, stop=True), r=[f"gdnCT{b}", "gdnCvb"], w=[kU])
                        P.op("pe", lambda e, psB=psB, b=b, hv=hv: e.matmul(psB[:, 0:64], lhsT=kbg[:, hv, :], rhs=TT[b][:, hv, :], start=True, stop=True), r=[f"gdnCT{b}", "gdnCkbg"], w=[kWT])
                        P.op("act", lambda e, psA=psA, pb=pb: e.activation(out=usb[pb][:], in_=psA[0:64, 0:128], func=AF.Copy), r=[kU], w=[f"gdnCu{pb}"])
                        P.op("act", lambda e, psB=psB, pb=pb: e.activation(out=wTs[pb][:], in_=psB[:, 0:64], func=AF.Copy), r=[kWT], w=[f"gdnCwT{pb}"])
                        P.op("pe", lambda e, psA=psA, pb=pb, hv=hv: e.matmul(psA[0:64, 128:256], lhsT=wTs[pb][:], rhs=S[:, hv, :], start=True, stop=True), r=[f"gdnCwT{pb}", "gdnS"], w=[kW])
                        P.op("dve", lambda e, psA=psA, pb=pb: e.tensor_tensor(out=vn[pb][:], in0=usb[pb][:], in1=psA[0:64, 128:256], op=ALU.subtract), r=[kW, f"gdnCu{pb}"], w=[f"gdnCvn{pb}"])
                        if is_lat:
                            P.op("pe", lambda e, psA=psA, kh=kh: e.matmul(psA[0:64, 384:448], lhsT=kTc[:, kh, :], rhs=qTc[:, kh, :], start=True, stop=True), r=["gdnCkT", "gdnCqT"], w=[kAt])
                            P.op("dve", lambda e, psA=psA, pb=pb, hv=hv: e.tensor_tensor(out=am[pb][:], in0=psA[0:64, 384:448], in1=dT[:, hv, :], op=ALU.mult), r=[kAt, "gdnCdT"], w=[f"gdnCam{pb}"])

                            def mmo(e, psA=psA, pb=pb, hv=hv):
                                e.matmul(psA[0:64, 256:384], lhsT=am[pb][:], rhs=vn[pb][:], start=True, stop=False)
                                return e.matmul(psA[0:64, 256:384], lhsT=qegT[:, hv, :], rhs=S[:, hv, :], start=False, stop=True)

                            P.op("pe", mmo, r=[f"gdnCam{pb}", f"gdnCvn{pb}", f"gdnCqegT{hv // 8}", "gdnS"], w=[kO])
                            if d == 0:
                                P.op("act", lambda e, psA=psA, b=b, hv=hv: e.activation(out=ost[b][:, hv, :], in_=psA[0:64, 256:384], func=AF.Copy), r=[kO], w=[f"gdnCo{b}h{hv}"])
                            else:
                                P.op("dve", lambda e, psA=psA, hv=hv: e.tensor_tensor(out=oft[:, hv * 128:(hv + 1) * 128], in0=oft[:, hv * 128:(hv + 1) * 128], in1=psA[0:64, 256:384], op=ALU.add), r=[kO, "gdnCof"], w=[f"gdnCofh{hv}"])
                        P.op("pe", lambda e, psB=psB, pb=pb, hv=hv: e.matmul(psB[:, 64:192], lhsT=kdc[:, hv, :], rhs=vn[pb][:], start=True, stop=True), r=["gdnCkdc", f"gdnCvn{pb}"], w=[kSn])
                        P.op("dve", lambda e, psB=psB, hv=hv: e.scalar_tensor_tensor(out=S[:, hv, :], in0=S[:, hv, :], scalar=egl[:, hv:hv + 1], in1=psB[:, 64:192], op0=ALU.mult, op1=ALU.add), r=[kSn, "gdnCegl", "gdnS"], w=["gdnS"])
                    if not is_lat:
                        continue
                    if d == 0:
                        P.dma("sp", GOF[tok0:tok0 + 64, :], ost[b][:].rearrange("p h c -> p (h c)"), r=[f"gdnCo{b}h{hv}" for hv in range(16)], w=bkeys("GOF", tok0, tok0 + 64))
                        continue
                    ofk = [f"gdnCofh{hv}" for hv in range(16)]
                    o3 = oft[:].rearrange("p (h c) -> p h c", c=128)
                    P.op("pool", lambda e, o3=o3: e.tensor_tensor(out=sq[:], in0=o3, in1=o3, op=ALU.mult), r=ofk, w=["gdnCsq"])
                    P.op("dve", lambda e: e.tensor_reduce(out=ssn[:], in_=sq[:], axis=AX.X, op=ALU.add), r=["gdnCsq"], w=["gdnCss"])
                    P.op("act", lambda e: e.activation(out=ssn[:], in_=ssn[:], func=AF.Sqrt, scale=1.0 / 128, bias=self.epsc[0:64, :]), r=["gdnCss", "epsc"], w=["gdnCss"])
                    P.op("dve", lambda e: e.reciprocal(out=ssn[:], in_=ssn[:]), r=["gdnCss"], w=["gdnCss"])
                    P.op("dve", lambda e, o3=o3: e.tensor_tensor(out=o3, in0=o3, in1=ssn[:].unsqueeze(2).to_broadcast([64, 16, 128]), op=ALU.mult), r=ofk + ["gdnCss"], w=["gdnCof"] + ofk)
                    P.op("pool", lambda e, o3=o3: e.tensor_tensor(out=o3, in0=o3, in1=ngr[:].unsqueeze(1).to_broadcast([64, 16, 128]), op=ALU.mult), r=["gdnCof", "gdnng"], w=["gdnCof"] + ofk)
                    P.op("pool", lambda e: e.tensor_tensor(out=oft[:], in0=oft[:], in1=zt_[:], op=ALU.mult), r=["gdnCof", "gdnCz"], w=["gdnCof"] + ofk)
                    for half in range(2):
                        def tro(e, half=half):
                            ins = None
                            for h8 in range(8):
                                hv = half * 8 + h8
                                ins = e.transpose(out=self.psum[half][:, h8 * 64:(h8 + 1) * 64], in_=oft[:, hv * 128:(hv + 1) * 128], identity=ident[0:64, 0:64])
                            return ins

                        P.op("pe", tro, r=["gdnCof", "identg"], w=[f"ps{half}"])
                        P.op("act", lambda e, half=half: e.activation(out=mT[:, half * 8:(half + 1) * 8, :].rearrange("p a t -> p (a t)"), in_=self.psum[half][:, :512], func=AF.Copy), r=[f"ps{half}"], w=[f"gdnCmT{half}"])
                    t0 = tok0 - LC
                    P.dma("sp", xt[:], xlv[:, :, t0:t0 + 64], r=bkeys("XL", t0, t0 + 64), w=["gdnCxt"])
                    for oc in range(8):
                        psP = self.psum[2 + oc % 2]

                        def mmP(e, psP=psP, oc=oc):
                            ins = None
                            for hv in range(16):
                                ins = e.matmul(psP[:, 0:64], lhsT=wo[:, hv, oc * 128:(oc + 1) * 128], rhs=mT[:, hv, :], start=(hv == 0), stop=(hv == 15))
                            return ins

                        P.op("pe", mmP, r=["gdnwo0", "gdnwo1", "gdnCmT0", "gdnCmT1"], w=[f"ps{2 + oc % 2}"])
                        xo = xt[:, oc, :] if d == 0 else xt[:, oc, ::-1]
                        P.op("dve", lambda e, psP=psP, oc=oc, xo=xo: e.scalar_tensor_tensor(out=xo, in0=psP[:, 0:64], scalar=self.modv[:, 16 + oc, 0:1], in1=xo, op0=ALU.mult, op1=ALU.add), r=[f"ps{2 + oc % 2}", "modv", "gdnCxt"], w=["gdnCxt"])
                    P.dma("sp", xlv[:, :, t0:t0 + 64], xt[:], r=["gdnCxt"], w=bkeys("XL", t0, t0 + 64))
            P.barrier()

    def load_inputs(self):
        P = self.P
        xin, cin = self.dram["xT"], self.dram["cT"]
        XL, XC = self.dram["XL"], self.dram["XC"]
        for t0 in range(0, L, 512):
            P.dma("sp", XL[:, t0:t0 + 512], xin[:, t0:t0 + 512], w=bkeys("XL", t0, t0 + 512))
        P.dma("sp", XC[:, :], cin[:, :], w=bkeys("XC", 0, LC))

    def final_phase(self, do_norm=True):
        P = self.P
        outv = self.dram["outT"].rearrange("(kc p) t -> p kc t", p=128)
        XL, XC = self.dram["XL"], self.dram["XC"]
        P.dma("sp", self.dram["outC"][:, :], XC[:, :], r=bkeys("XC", 0, LC), w=["outC"])
        if not do_norm:
            for t0 in range(0, L, 512):
                P.dma("sp", self.dram["outT"][:, t0:t0 + 512], XL[:, t0:t0 + 512], r=bkeys("XL", t0, t0 + 512),
                      w=bkeys("outT", t0, t0 + 512))
            return
        with ExitStack() as st:
            xts = [self.sb(f"nx{i}", [128, 8, 512], F32, st) for i in range(2)]
            sq = self.sb("nsq", [128, 8, 512], F32, st)
            rs = self.sb("nrs", [128, 512], F32, st)
            rinv = self.sb("nrinv", [128, 512], F32, st)
            tmp = [self.sb(f"ntmp{i}", [128, 8, 512], F32, st) for i in range(2)]
            for it, (seg, t0, n) in enumerate(self.tiles(ctx=False)):
                xt = xts[it % 2]
                xk = f"nx{it % 2}"
                tb = tmp[it % 2]
                tk = f"ntmp{it % 2}"
                xv, xname = self.xview(seg)
                P.dma("sp", xt[:, :, :n], xv[:, :, t0:t0 + n], r=bkeys(xname, t0, t0 + n), w=[xk])
                self.rstd_tile((sq, rs, rinv), xt, n, xk)
                P.op("dve", lambda e, xt=xt, n=n, tb=tb: e.tensor_tensor(
                    out=tb[:, :, :n], in0=xt[:, :, :n], in1=rinv[:, :n].unsqueeze(1).to_broadcast([128, 8, n]), op=ALU.mult),
                    r=[xk, "nrinv"], w=[tk])
                P.op("dve", lambda e, n=n, tb=tb: e.tensor_tensor(
                    out=tb[:, :, :n], in0=tb[:, :, :n], in1=self.fng[:].unsqueeze(2).to_broadcast([128, 8, n]), op=ALU.mult),
                    r=[tk, "fng"], w=[tk])
                P.dma("sp", outv[:, :, t0:t0 + n], tb[:, :, :n], r=[tk], w=bkeys("outT", t0, t0 + n))
            P.barrier()

    def declare_io(self):
        self.din("xT", [D, L])
        self.din("cT", [D, LC])
        self.din("mod_w", [DEPTH, D, 6 * D])
        self.din("ffn_w_in", [DEPTH, D, 2 * DFF])
        self.din("ffn_w_out", [DEPTH, DFF, D])
        self.din("i_ffn_cw", [DEPTH, 128, NFC, 3])
        self.din("i_ffn_cb", [DEPTH, 128, NFC])
        self.dout("outT", [D, L])
        self.dout("outC", [D, LC])
        self.dscr("XL", [D, L])
        self.dscr("XC", [D, LC])
        self.dscr("AT", [DFF, NT], BF16)
        kinds = {li % 4 for (li, m, f) in self.cfg["layers"] if m}
        if 0 in kinds:
            self.declare_s5()
        if 1 in kinds:
            self.declare_lru()
        if 2 in kinds:
            self.declare_ret()
        if 3 in kinds:
            self.declare_gdn()


def build(cfg):
    nc = bass.Bass("TRN2", target_bir_lowering=False)
    with ExitStack() as stack:
        k = K(nc, stack, cfg)
        k.declare_io()
        k.setup_common()
        k.load_inputs()
        for (li, do_mixer, do_ffn) in cfg["layers"]:
            k.mod_phase(li)
            if do_mixer:
                k.mixer(li)
            if do_ffn:
                k.ffn_phase(li, ctx=(li < DEPTH - 1))
        for nm in cfg.get("dump", []):
            src = k.dram[nm]
            dst = k.dout("dbg_" + nm, list(src.shape), src.dtype)
            k.P.barrier()
            nrow = src.shape[0]
            step = max(1, nrow // 8)
            for r0 in range(0, nrow, step):
                r1 = min(nrow, r0 + step)
                k.P.dma("sp", dst[r0:r1], src[r0:r1], r=[], w=[("dbg", nm, r0)])
        k.final_phase(do_norm=cfg.get("final", True))
        k.P.finish()
        k.P.emit()
        k.n_ops = k.P.n_ops
    return nc


def col_layout(v, nchunk):
    return np.ascontiguousarray(v.reshape(nchunk, 128).T)


def prep_shared(inp, cfg):
    sh = {}
    kinds = {li % 4 for (li, m, f) in cfg["layers"] if m}
    want = {k: (k in kinds) for k in range(4)}
    sh["mod_w"] = np.ascontiguousarray(inp["mod_w"])
    sh["ffn_w_in"] = np.ascontiguousarray(inp["ffn_w_in"])
    sh["ffn_w_out"] = np.ascontiguousarray(inp["ffn_w_out"])
    sh["i_n1g"] = np.ascontiguousarray(np.stack([col_layout(inp["norm1_g"][i], 8) for i in range(DEPTH)], axis=1))
    sh["i_n2g"] = np.ascontiguousarray(np.stack([col_layout(inp["norm2_g"][i], 8) for i in range(DEPTH)], axis=1))
    sh["i_fng"] = col_layout(inp["final_norm_g"], 8)
    sh["i_modb"] = np.ascontiguousarray(np.stack([col_layout(inp["mod_b"][i], 48) for i in range(DEPTH)], axis=1))
    cw = inp["ffn_conv_w"]
    sh["i_ffn_cw"] = np.ascontiguousarray(cw.reshape(DEPTH, 3, NFC, 128).transpose(0, 3, 2, 1))
    sh["i_ffn_cb"] = np.ascontiguousarray(inp["ffn_conv_b"].reshape(DEPTH, NFC, 128).transpose(0, 2, 1))
    if "s5_lam_re" in inp and want.get(0, True):
        lre, lim, lst = inp["s5_lam_re"][0], inp["s5_lam_im"][0], inp["s5_log_step"][0]
        col = np.zeros((2, 128, 3, 32), np.float32)
        row = np.zeros((2, 3, 4096), np.float32)
        for d in range(2):
            for i, X in enumerate((lre[d], lim[d], np.repeat(lst[d][:, None], 64, axis=1))):
                col[d, :, i, :] = X.reshape(32, 2, 64).transpose(1, 2, 0).reshape(128, 32)
                row[d, i, :] = X.reshape(4096)
        sh["s5_col"], sh["s5_row"] = col, row
        Bp = np.zeros((2, 2, 128, 32, 128), np.float32)
        Cp = np.zeros((2, 2, 128, 32, 128), np.float32)
        Bs = (inp["s5_b_re"][0], inp["s5_b_im"][0])
        Cs = (inp["s5_c_re"][0], inp["s5_c_im"][0])
        for d in range(2):
            for ri in range(2):
                for gp in range(32):
                    for g2 in range(2):
                        g = 2 * gp + g2
                        k0 = (2 * (gp % 4) + g2) * 16
                        Bp[d, ri, k0:k0 + 16, gp, g2 * 64:(g2 + 1) * 64] = Bs[ri][d, g].T
                        Cp[d, ri, g2 * 64:(g2 + 1) * 64, gp, k0:k0 + 16] = Cs[ri][d, g].T
        sh["s5_B"], sh["s5_C"] = Bp, Cp
        sh["s5_tau"] = np.ascontiguousarray(np.broadcast_to(np.arange(1, 129, dtype=np.float32)[None, :], (128, 128)))
        sh["i_s5d"] = col_layout(inp["s5_d"][0], 8)
        sh["i_s5bglu"] = col_layout(inp["s5_b_glu"][0], 16)
        sh["s5_w_glu"] = np.ascontiguousarray(inp["s5_w_glu"][0])
    if want.get(1, False):
        sh["lru_w_in"] = np.ascontiguousarray(inp["lru_w_in"][0])
        sh["lru_w_a"] = np.ascontiguousarray(inp["lru_w_a"][0])
        sh["lru_w_x"] = np.ascontiguousarray(inp["lru_w_x"][0])
        sh["lru_w_out"] = np.ascontiguousarray(inp["lru_w_out"][0])
        sh["i_lru_cw"] = np.ascontiguousarray(inp["lru_conv_w"][0].reshape(4, 8, 128).transpose(2, 1, 0))
        sh["i_lru_cb"] = col_layout(inp["lru_conv_b"][0], 8)
        v = np.stack([inp["lru_b_a"][0], inp["lru_b_x"][0], inp["lru_lam"][0]], axis=0)
        sh["i_lru_vec"] = np.ascontiguousarray(v.reshape(3, 2, 8, 128).transpose(3, 0, 1, 2))
    if want.get(2, False):
        w = inp["ret_w_in"][0]
        sh["ret_w_in"] = np.ascontiguousarray(w)
        perm = np.arange(1024).reshape(8, 128)
        perm = ((perm % 128 + 64) % 128 + (perm // 128) * 128).reshape(-1)
        sh["ret_wk_perm"] = np.ascontiguousarray(w[:, 0:1024][:, perm])
        sh["ret_wq_perm"] = np.ascontiguousarray(w[:, 3072:4096][:, perm])
        sh["ret_w_out"] = np.ascontiguousarray(inp["ret_w_out"][0])
        sh["ret_ng"] = np.ascontiguousarray(np.broadcast_to(inp["ret_norm_g"][0][None, :], (128, 2048)))
        sh.update(ret_consts())
    if want.get(3, False):
        sh["gdn_w_in"] = np.ascontiguousarray(inp["gdn_w_in"][0])
        sh["gdn_w_out"] = np.ascontiguousarray(inp["gdn_w_out"][0])
        sh["i_gdn_cw"] = np.ascontiguousarray(inp["gdn_conv_w"][0].reshape(4, 32, 128).transpose(2, 1, 0))
        sh["gdn_ng"] = np.ascontiguousarray(np.broadcast_to(inp["gdn_norm_g"][0][None, :], (128, 128)))
        ab = np.stack([inp["gdn_a_log"][0], inp["gdn_dt_bias"][0]], axis=0)
        sh["gdn_ab"] = np.ascontiguousarray(np.broadcast_to(ab[None], (128, 2, 2, 16))).astype(np.float32)
        i = np.arange(64)
        c64 = np.zeros((64, 4, 64), np.float32)
        c64[:, 0, :] = (i[:, None] <= i[None, :])
        c64[:, 1, :] = (i[:, None] > i[None, :])
        c64[:, 2, :] = (i[:, None] <= i[None, :])
        c64[:, 3, :] = np.eye(64, dtype=np.float32)[::-1]
        sh["gdn_c64"] = c64
        sh["gdn_ones"] = np.ones((64, 128), np.float32)
        sh["ident"] = np.eye(128, dtype=np.float32)
    return sh


def ret_consts():
    c = {}
    p = np.arange(128)
    inv = 10000.0 ** (-(p % 64).astype(np.float64) / 64.0)
    pos = np.arange(64, dtype=np.float64) - 31.5
    ang = inv[:, None] * pos[None, :]
    sign = np.where(p < 64, -1.0, 1.0)[:, None]
    base = np.stack([np.cos(ang), sign * np.sin(ang), np.cos(ang), sign * np.sin(ang)], axis=0)
    c["ret_rope"] = np.stack([base, base / 16.0], axis=0).astype(np.float32)
    gam = np.array([1.0 - 2.0 ** (-5 - h) for h in range(4)], np.float64)
    lg = np.log(gam)
    idx = np.arange(128, dtype=np.float64)
    DT = np.zeros((2, 128, 4, 128)); XI = np.zeros((2, 128, 4, 128)); ZE = np.zeros((2, 128, 4))
    for h in range(4):
        dif = idx[None, :] - idx[:, None]
        DT[0, :, h, :] = np.where(dif >= 0, np.exp(np.maximum(dif, 0) * lg[h]), 0.0)
        DT[1, :, h, :] = np.where(dif < 0, np.exp(np.maximum(-dif, 0) * lg[h]), 0.0)
        XI[0, :, h, :] = np.exp((idx + 1.0) * lg[h])[None, :]
        XI[1, :, h, :] = np.exp((128.0 - idx) * lg[h])[None, :]
        ZE[0, :, h] = np.exp((127.0 - idx) * lg[h])
        ZE[1, :, h] = np.exp(idx * lg[h])
    c["ret_DT"], c["ret_XI"], c["ret_ZETA"] = DT.astype(np.float32), XI.astype(np.float32), ZE.astype(np.float32)
    c["ident"] = np.eye(128, dtype=np.float32)
    return c


def prep_core(inp, b, x=None, ctx=None):
    x = inp["x"][b] if x is None else x
    ctx = inp["ctx"][b] if ctx is None else ctx
    m = {}
    m["xT"] = np.ascontiguousarray(x.T)
    m["cT"] = np.ascontiguousarray(ctx.T)
    cc = np.stack([col_layout(inp["c"][b], 8), col_layout(inp["c_ctx"], 8)], axis=2)
    m["i_ccol"] = np.ascontiguousarray(cc.astype(np.float32))
    return m


_CACHE = {}


def run(inp, cfg, xs=None, ctxs=None, ncores=8):
    key = repr(cfg)
    if key not in _CACHE:
        _CACHE[key] = build(cfg)
    nc = _CACHE[key]
    sh = prep_shared(inp, cfg)
    in_maps = []
    for b in range(ncores):
        m = dict(sh)
        m.update(prep_core(inp, b, None if xs is None else xs[b], None if ctxs is None else ctxs[b]))
        in_maps.append(m)
    res = run_bass_kernel_spmd(nc, in_maps, core_ids=list(range(ncores)))
    global LAST_RESULTS
    LAST_RESULTS = res.results
    outs = [np.ascontiguousarray(r["outT"].T) for r in res.results]
    outc = [np.ascontiguousarray(r["outC"].T) for r in res.results]
    return np.stack(outs), np.stack(outc)


FULL_CFG = {"layers": [(0, True, True), (1, True, True), (2, True, True), (3, True, True)], "final": True}


def kernel(**inputs):
    inp = {k: np.asarray(v) for k, v in inputs.items()}
    out, _ = run(inp, FULL_CFG)
    return out.astype(np.float32)
```

```python
import math
from contextlib import ExitStack

import numpy as np
import concourse.bass as bass
import concourse.mybir as mybir
from concourse.bass_utils import run_bass_kernel_spmd

F32 = mybir.dt.float32
BF16 = mybir.dt.bfloat16
I32 = mybir.dt.int32
AF = mybir.ActivationFunctionType
ALU = mybir.AluOpType
AX = mybir.AxisListType

D = 1024
L = 4096
LC = 256
NT = L + LC
DEPTH = 4
DFF = 2816
NFC = DFF // 128
EPS = 1e-6
HT_CTX0 = 1
HT_LAT0 = 259
HT_COLS = 4360
NS_DMA = 24


def ht_col(tok):
    return HT_CTX0 + tok if tok < LC else HT_LAT0 + (tok - LC)


class Prog:
    def __init__(self, nc, stack):
        self.nc = nc
        self.streams = {e: [] for e in ("pe", "act", "dve", "pool", "sp")}
        self.csem = {e: stack.enter_context(nc.semaphore("c_" + e)) for e in ("pe", "act", "dve", "pool")}
        self.ccount = {e: 0 for e in self.csem}
        self.dsem = {q: [stack.enter_context(nc.semaphore(f"d_{q}{i}")) for i in range(NS_DMA)] for q in ("sp", "pool")}
        self.dcount = {"sp": 0, "pool": 0}
        self.semobj = {}
        for e, s in self.csem.items():
            self.semobj[("c", e)] = s
        for q, lst in self.dsem.items():
            for i, s in enumerate(lst):
                self.semobj[("d", q, i)] = s
        self.waited = {e: {} for e in self.streams}
        self.lastw = {}
        self.readers = {}
        self.n_ops = 0

    def _deps(self, r, w):
        deps = {}

        def add(tok):
            sk, v = tok
            if deps.get(sk, 0) < v:
                deps[sk] = v

        for k in list(r) + list(w):
            t = self.lastw.get(k)
            if t is not None:
                add(t)
        for k in w:
            for sk, v in self.readers.get(k, {}).items():
                add((sk, v))
        return deps

    def _record(self, r, w, tok):
        sk, v = tok
        for k in r:
            d = self.readers.setdefault(k, {})
            if d.get(sk, 0) < v:
                d[sk] = v
        for k in w:
            self.lastw[k] = tok
            self.readers[k] = {}

    def _waits(self, eng, deps):
        out = []
        wd = self.waited[eng]
        for sk, v in deps.items():
            if eng == "pe" and sk == ("c", "pe"):
                continue
            if wd.get(sk, 0) >= v:
                continue
            wd[sk] = v
            out.append((self.semobj[sk], v))
        return out

    def op(self, eng, fn, r=(), w=()):
        deps = self._deps(r, w)
        self.ccount[eng] += 1
        tok = (("c", eng), self.ccount[eng])
        self.streams[eng].append((self._waits(eng, deps), fn, (self.csem[eng], 1)))
        self._record(r, w, tok)
        self.n_ops += 1

    def dma(self, q, out, in_, r=(), w=()):
        k = self.dcount[q]
        self.dcount[q] += 1
        slot = k % NS_DMA
        val = 16 * (k // NS_DMA + 1)
        sk = ("d", q, slot)
        deps = self._deps(r, w)
        if k >= NS_DMA and deps.get(sk, 0) < val - 16:
            deps[sk] = val - 16
        self.streams[q].append((self._waits(q, deps), lambda e: e.dma_start(out=out, in_=in_), (self.semobj[sk], 16)))
        self._record(r, w, (sk, val))
        self.n_ops += 1

    def barrier(self):
        toks = {}
        for q in ("sp", "pool"):
            k = self.dcount[q]
            for slot in range(NS_DMA):
                n = (k - slot + NS_DMA - 1) // NS_DMA if k > slot else 0
                if n > 0:
                    toks[("d", q, slot)] = 16 * n
        for e, c in self.ccount.items():
            if c > 0:
                toks[("c", e)] = c
        for eng in self.streams:
            w = self._waits(eng, dict(toks)) if eng != "pe" else self._waits_pe_barrier(toks)
            if w:
                self.streams[eng].append((w, None, None))

    def _waits_pe_barrier(self, toks):
        t = dict(toks)
        t.pop(("c", "pe"), None)
        return self._waits("pe", t)

    def finish(self):
        waits = []
        for q in ("sp", "pool"):
            k = self.dcount[q]
            for slot in range(NS_DMA):
                n = (k - slot + NS_DMA - 1) // NS_DMA if k > slot else 0
                if n > 0:
                    waits.append((self.semobj[("d", q, slot)], 16 * n))
        for e, c in self.ccount.items():
            if c > 0:
                waits.append((self.csem[e], c))
        self.streams["sp"].append((waits, None, None))

    def emit(self):
        nc = self.nc
        with nc.Block() as block:
            for name, deco in (("pe", block.tensor), ("act", block.scalar), ("dve", block.vector),
                               ("pool", block.gpsimd), ("sp", block.sync)):
                lst = self.streams[name]
                if not lst:
                    continue

                def body(e, lst=lst):
                    for waits, fn, sig in lst:
                        for sem, val in waits:
                            e.wait_ge(sem, val)
                        if fn is not None:
                            ins = fn(e)
                            ins.then_inc(sig[0], sig[1])

                deco(body)


class Stager:
    def __init__(self, P, enabled=True):
        self.P = P
        self.items = []
        self.enabled = enabled

    def op(self, stage, eng, fn, r=(), w=()):
        if not self.enabled:
            self.P.op(eng, fn, r=r, w=w)
            return
        self.items.append((stage, len(self.items), eng, fn, list(r), list(w)))

    def flush(self):
        for stage, _, eng, fn, r, w in sorted(self.items, key=lambda t: (t[0], t[1])):
            self.P.op(eng, fn, r=r, w=w)
        self.items = []


def bkeys(name, lo, hi, g=128):
    return [(name, b) for b in range(lo // g, (hi - 1) // g + 1)]


class K:
    def __init__(self, nc, stack, cfg):
        self.nc = nc
        self.stack = stack
        self.cfg = cfg
        self.P = Prog(nc, stack)
        self.dram = {}
        self.uid = 0

    def din(self, name, shape, dt=F32):
        t = self.nc.dram_tensor(name, list(shape), dt, kind="ExternalInput").ap()
        self.dram[name] = t
        return t

    def dout(self, name, shape, dt=F32):
        t = self.nc.dram_tensor(name, list(shape), dt, kind="ExternalOutput").ap()
        self.dram[name] = t
        return t

    def dscr(self, name, shape, dt=F32):
        t = self.nc.dram_tensor(name, list(shape), dt).ap()
        self.dram[name] = t
        return t

    def sb(self, name, shape, dt=F32, stack=None):
        self.uid += 1
        return (stack or self.stack).enter_context(self.nc.sbuf_tensor(f"{name}_{self.uid}", list(shape), dt))

    def ps(self, name, shape=(128, 512), dt=F32, stack=None):
        return (stack or self.stack).enter_context(self.nc.psum_tensor(name, list(shape), dt))

    def setup_common(self):
        P = self.P
        self.ones = self.sb("ones", [128, 128])
        P.op("dve", lambda e: e.memset(self.ones[:], 1.0), w=["ones"])
        self.psum = [self.ps(f"ps{i}") for i in range(8)]
        self.n1g = self.sb("n1g", [128, DEPTH, 8])
        self.n2g = self.sb("n2g", [128, DEPTH, 8])
        self.fng = self.sb("fng", [128, 8])
        self.modb = self.sb("modb", [128, DEPTH, 48])
        self.ccol = self.sb("ccol", [128, 8, 2])
        self.scol = self.sb("scol", [128, 8, 2])
        for nm, t in (("n1g", self.n1g), ("n2g", self.n2g), ("fng", self.fng), ("modb", self.modb), ("ccol", self.ccol)):
            src = self.din("i_" + nm, list(t[:].shape))
            P.dma("sp", t[:], src, w=[nm])
        P.op("act", lambda e: e.activation(out=self.scol[:], in_=self.ccol[:], func=AF.Silu), r=["ccol"], w=["scol"])
        self.modv = self.sb("modv", [128, 48, 2])
        self.A1 = self.sb("A1", [128, 8, 2])
        self.A2 = self.sb("A2", [128, 8, 2])
        self.epsc = self.sb("epsc", [128, 1])
        P.op("dve", lambda e: e.memset(self.epsc[:], EPS), w=["epsc"])

    def alloc_ht(self, st):
        P = self.P
        self.HT = self.sb("HT", [128, 8, HT_COLS], BF16, st)
        HT = self.HT
        for c in (0, 257, 258, 4355):
            P.op("dve", lambda e, c=c, HT=HT: e.memset(HT[:, :, c:c + 1], 0.0), w=bkeys("HT", c, c + 1))

    def mod_phase(self, li):
        P = self.P
        mw = self.dram["mod_w"][li].rearrange("(kc p) n -> p kc n", p=128)
        ps = self.psum[7]
        with ExitStack() as st:
            wb = [self.sb(f"modw{i}", [128, 8, 512], F32, st) for i in range(2)]
            for pc in range(12):
                b = wb[pc % 2]
                bk = f"modw{pc % 2}"
                P.dma("sp", b[:], mw[:, :, pc * 512:(pc + 1) * 512], w=[bk])
                for jj in range(4):
                    j = pc * 4 + jj

                    def mm(e, b=b, jj=jj, j=j):
                        ins = None
                        for kc in range(8):
                            ins = e.matmul(ps[:, 2 * j:2 * j + 2], lhsT=b[:, kc, jj * 128:(jj + 1) * 128],
                                           rhs=self.scol[:, kc, :], start=(kc == 0), stop=(kc == 7))
                        return ins

                    P.op("pe", mm, r=[bk, "scol"], w=["ps7"])
            P.op("dve", lambda e: e.tensor_tensor(
                out=self.modv[:], in0=ps[:, 0:96].rearrange("p (j s) -> p j s", s=2),
                in1=self.modb[:, li, :].unsqueeze(2).to_broadcast([128, 48, 2]), op=ALU.add),
                r=["ps7", "modb"], w=["modv"])
            P.op("dve", lambda e: e.scalar_tensor_tensor(
                out=self.A1[:], in0=self.modv[:, 8:16, :], scalar=1.0,
                in1=self.n1g[:, li, :].unsqueeze(2).to_broadcast([128, 8, 2]), op0=ALU.add, op1=ALU.mult),
                r=["modv", "n1g"], w=["A1"])
            P.op("dve", lambda e: e.scalar_tensor_tensor(
                out=self.A2[:], in0=self.modv[:, 32:40, :], scalar=1.0,
                in1=self.n2g[:, li, :].unsqueeze(2).to_broadcast([128, 8, 2]), op0=ALU.add, op1=ALU.mult),
                r=["modv", "n2g"], w=["A2"])
            P.barrier()

    def xview(self, seg):
        name = "XL" if seg == 1 else "XC"
        return self.dram[name].rearrange("(kc p) t -> p kc t", p=128), name

    def tiles(self, n_lat=512, n_ctx=256, ctx=True):
        out = []
        if ctx:
            for t0 in range(0, LC, n_ctx):
                out.append((0, t0, n_ctx))
        for t0 in range(0, L, n_lat):
            out.append((1, t0, n_lat))
        return out

    def rstd_tile(self, st_bufs, xt, n, tagk):
        P = self.P
        sq, rs, rinv = st_bufs
        ps = self.psum[6]
        P.op("act", lambda e: e.activation(out=sq[:, :, :n], in_=xt[:, :, :n], func=AF.Square), r=[tagk], w=["nsq"])

        def mm(e):
            ins = None
            for kc in range(8):
                ins = e.matmul(ps[:, :n], lhsT=self.ones[:], rhs=sq[:, kc, :n], start=(kc == 0), stop=(kc == 7))
            return ins

        P.op("pe", mm, r=["nsq", "ones"], w=["ps6"])
        P.op("act", lambda e: e.activation(out=rs[:, :n], in_=ps[:, :n], func=AF.Sqrt, scale=1.0 / D, bias=self.epsc[:]),
             r=["ps6", "epsc"], w=["nrs"])
        P.op("dve", lambda e: e.reciprocal(out=rinv[:, :n], in_=rs[:, :n]), r=["nrs"], w=["nrinv"])

    def norm_phase(self, A, Bsl, ctx=True, h32=None, write_ht=True):
        P = self.P
        HT = self.HT if write_ht else None
        with ExitStack() as st:
            xts = [self.sb(f"nx{i}", [128, 8, 512], F32, st) for i in range(2)]
            sq = self.sb("nsq", [128, 8, 512], F32, st)
            rs = self.sb("nrs", [128, 512], F32, st)
            rinv = self.sb("nrinv", [128, 512], F32, st)
            tmp = self.sb("ntmp", [128, 8, 512], F32, st)
            if h32 is not None:
                self.sb32 = self.sb("h32t", [128, 8, 512], F32, st)
            for it, (seg, t0, n) in enumerate(self.tiles(ctx=ctx)):
                xt = xts[it % 2]
                xk = f"nx{it % 2}"
                xv, xname = self.xview(seg)
                P.dma("sp", xt[:, :, :n], xv[:, :, t0:t0 + n], r=bkeys(xname, t0, t0 + n), w=[xk])
                self.rstd_tile((sq, rs, rinv), xt, n, xk)
                P.op("dve", lambda e, xt=xt, n=n: e.tensor_tensor(
                    out=tmp[:, :, :n], in0=xt[:, :, :n], in1=rinv[:, :n].unsqueeze(1).to_broadcast([128, 8, n]), op=ALU.mult),
                    r=[xk, "nrinv"], w=["ntmp"])
                tok0 = t0 if seg == 0 else LC + t0
                c0 = ht_col(tok0)
                s = 1 if seg == 0 else 0
                for kc in range(8 if write_ht else 0):
                    P.op("act" if kc % 2 else "dve",
                         (lambda e, kc=kc, n=n, c0=c0, s=s: e.activation(
                             out=HT[:, kc, c0:c0 + n], in_=tmp[:, kc, :n], func=AF.Identity,
                             scale=A[:, kc, s:s + 1], bias=self.modv[:, Bsl + kc, s:s + 1])) if kc % 2 else
                         (lambda e, kc=kc, n=n, c0=c0, s=s: e.tensor_scalar(
                             out=HT[:, kc, c0:c0 + n], in0=tmp[:, kc, :n],
                             scalar1=A[:, kc, s:s + 1], scalar2=self.modv[:, Bsl + kc, s:s + 1], op0=ALU.mult, op1=ALU.add)),
                         r=["ntmp", "A1", "A2", "modv"], w=bkeys("HT", c0, c0 + n))
                if h32 is not None:
                    h32t = self.sb32
                    for kc in range(8):
                        P.op("dve", lambda e, kc=kc, n=n, s=s: e.tensor_scalar(
                            out=h32t[:, kc, :n], in0=tmp[:, kc, :n], scalar1=A[:, kc, s:s + 1],
                            scalar2=self.modv[:, Bsl + kc, s:s + 1], op0=ALU.mult, op1=ALU.add),
                            r=["ntmp", "A1", "A2", "modv"], w=["h32t"])
                    P.dma("sp", h32.rearrange("(kc p) t -> p kc t", p=128)[:, :, tok0:tok0 + n], h32t[:, :, :n],
                          r=["h32t"], w=bkeys("H32", tok0, tok0 + n))
            P.barrier()


    def ffn_phase(self, li, ctx=True):
        P = self.P
        with ExitStack() as hst:
            self.alloc_ht(hst)
            self.norm_phase(self.A2, 24, ctx=ctx)
            self._ffn1(li, ctx)
        self._ffn2(li, ctx)

    def _ffn1(self, li, ctx):
        P = self.P
        HT = self.HT
        AT = self.dram["AT"]
        win = self.dram["ffn_w_in"][li].rearrange("(kc p) n -> p kc n", p=128)
        tiles = self.tiles(n_lat=256, n_ctx=256, ctx=ctx)
        with ExitStack() as st:
            wg = [self.sb(f"fwg{i}", [128, 8, 512], BF16, st) for i in range(2)]
            wu = [self.sb(f"fwu{i}", [128, 8, 512], BF16, st) for i in range(2)]
            cw = self.sb("fcw", [128, NFC, 3], F32, st)
            cb = self.sb("fcb", [128, NFC], F32, st)
            P.dma("sp", cw[:], self.dram["i_ffn_cw"][li], w=["fcw"])
            P.dma("sp", cb[:], self.dram["i_ffn_cb"][li], w=["fcb"])
            t1 = [self.sb(f"ft1{i}", [128, 256], F32, st) for i in range(2)]
            t2 = [self.sb(f"ft2{i}", [128, 256], F32, st) for i in range(2)]
            ge = [self.sb(f"fge{i}", [128, 256], F32, st) for i in range(2)]
            ast = [self.sb(f"fa{i}", [128, 256], BF16, st) for i in range(2)]
            cnt = 0
            for g in range((NFC + 3) // 4):
                nfc = min(4, NFC - 4 * g)
                gb = g % 2
                P.dma("pool", wg[gb][:, :, :nfc * 128], win[:, :, g * 512:g * 512 + nfc * 128], w=[f"fwg{gb}"])
                P.dma("pool", wu[gb][:, :, :nfc * 128], win[:, :, DFF + g * 512:DFF + g * 512 + nfc * 128], w=[f"fwu{gb}"])
                for fl in range(nfc):
                    fc = 4 * g + fl
                    for (seg, t0, n) in tiles:
                        tok0 = t0 if seg == 0 else LC + t0
                        c0 = ht_col(tok0)
                        b = cnt % 2
                        cnt += 1
                        psG, psU = self.psum[2 * b], self.psum[2 * b + 1]
                        kG, kU = f"ps{2 * b}", f"ps{2 * b + 1}"

                        def mmG(e, psG=psG, gb=gb, fl=fl, c0=c0, n=n):
                            ins = None
                            for kc in range(8):
                                ins = e.matmul(psG[:, :n + 2], lhsT=wg[gb][:, kc, fl * 128:(fl + 1) * 128],
                                               rhs=HT[:, kc, c0 - 1:c0 + n + 1], start=(kc == 0), stop=(kc == 7))
                            return ins

                        def mmU(e, psU=psU, gb=gb, fl=fl, c0=c0, n=n):
                            ins = None
                            for kc in range(8):
                                ins = e.matmul(psU[:, :n], lhsT=wu[gb][:, kc, fl * 128:(fl + 1) * 128],
                                               rhs=HT[:, kc, c0:c0 + n], start=(kc == 0), stop=(kc == 7))
                            return ins

                        hk = bkeys("HT", c0 - 1, c0 + n + 1)
                        P.op("pe", mmG, r=[f"fwg{gb}"] + hk, w=[kG])
                        P.op("pe", mmU, r=[f"fwu{gb}"] + hk, w=[kU])
                        P.op("dve", lambda e, psG=psG, b=b, fc=fc, n=n: e.tensor_scalar(
                            out=t1[b][:, :n], in0=psG[:, 0:n], scalar1=cw[:, fc, 0:1], scalar2=None, op0=ALU.mult),
                            r=[kG, "fcw"], w=[f"ft1{b}"])
                        P.op("dve", lambda e, psG=psG, b=b, fc=fc, n=n: e.scalar_tensor_tensor(
                            out=t2[b][:, :n], in0=psG[:, 1:n + 1], scalar=cw[:, fc, 1:2], in1=t1[b][:, :n],
                            op0=ALU.mult, op1=ALU.add), r=[kG, "fcw", f"ft1{b}"], w=[f"ft2{b}"])
                        P.op("dve", lambda e, psG=psG, b=b, fc=fc, n=n: e.scalar_tensor_tensor(
                            out=t1[b][:, :n], in0=psG[:, 2:n + 2], scalar=cw[:, fc, 2:3], in1=t2[b][:, :n],
                            op0=ALU.mult, op1=ALU.add), r=[kG, "fcw", f"ft2{b}"], w=[f"ft1{b}"])
                        P.op("act", lambda e, b=b, fc=fc, n=n: e.activation(
                            out=ge[b][:, :n], in_=t1[b][:, :n], func=AF.Gelu, bias=cb[:, fc:fc + 1]),
                            r=[f"ft1{b}", "fcb"], w=[f"fge{b}"])
                        P.op("dve", lambda e, psU=psU, b=b, n=n: e.tensor_tensor(
                            out=ast[b][:, :n], in0=ge[b][:, :n], in1=psU[:, :n], op=ALU.mult),
                            r=[f"fge{b}", kU], w=[f"fa{b}"])
                        P.dma("sp", AT[fc * 128:(fc + 1) * 128, tok0:tok0 + n], ast[b][:, :n],
                              r=[f"fa{b}"], w=bkeys("AT", tok0, tok0 + n))
            P.barrier()

    def _ffn2(self, li, ctx):
        P = self.P
        AT = self.dram["AT"]
        tiles = self.tiles(n_lat=256, n_ctx=256, ctx=ctx)
        with ExitStack() as st:
            wo = self.sb("fwo", [128, NFC, D], BF16, st)
            wsrc = self.dram["ffn_w_out"][li].rearrange("(fc p) n -> p fc n", p=128)
            for h in range(2):
                P.dma("pool", wo[:, h * 11:(h + 1) * 11, :], wsrc[:, h * 11:(h + 1) * 11, :], w=[f"fwo{h}"])
            at = [self.sb(f"fat{i}", [128, NFC, 256], BF16, st) for i in range(2)]
            xt = [self.sb(f"fx{i}", [128, 8, 256], F32, st) for i in range(2)]
            atv = AT.rearrange("(fc p) t -> p fc t", p=128)
            cnt = 0
            for it, (seg, t0, n) in enumerate(tiles):
                tok0 = t0 if seg == 0 else LC + t0
                s = 1 if seg == 0 else 0
                b = it % 2
                xv, xname = self.xview(seg)
                P.dma("sp", at[b][:, :, :n], atv[:, :, tok0:tok0 + n], r=bkeys("AT", tok0, tok0 + n), w=[f"fat{b}"])
                P.dma("sp", xt[b][:, :, :n], xv[:, :, t0:t0 + n], r=bkeys(xname, t0, t0 + n), w=[f"fx{b}"])
                for oc in range(8):
                    pb = 4 + cnt % 2
                    cnt += 1
                    ps = self.psum[pb]

                    def mm(e, ps=ps, b=b, oc=oc, n=n):
                        ins = None
                        for fc in range(NFC):
                            ins = e.matmul(ps[:, :n], lhsT=wo[:, fc, oc * 128:(oc + 1) * 128], rhs=at[b][:, fc, :n],
                                           start=(fc == 0), stop=(fc == NFC - 1))
                        return ins

                    P.op("pe", mm, r=["fwo0", "fwo1", f"fat{b}"], w=[f"ps{pb}"])
                    P.op("dve", lambda e, ps=ps, b=b, oc=oc, n=n, s=s: e.scalar_tensor_tensor(
                        out=xt[b][:, oc, :n], in0=ps[:, :n], scalar=self.modv[:, 40 + oc, s:s + 1], in1=xt[b][:, oc, :n],
                        op0=ALU.mult, op1=ALU.add), r=[f"ps{pb}", "modv", f"fx{b}"], w=[f"fx{b}"])
                P.dma("sp", xv[:, :, t0:t0 + n], xt[b][:, :, :n], r=[f"fx{b}"], w=bkeys(xname, t0, t0 + n))
            P.barrier()

    def mixer(self, li):
        getattr(self, ("s5_mixer", "lru_mixer", "ret_mixer", "gdn_mixer")[li % 4])(li)

    def sincos(self, ang, F, out_s, out_c, T, rk, wk_s, wk_c):
        P = self.P
        tk = "sc_tmp"
        y, y2, w, m, ki = (T[k][:, :F] for k in ("y", "y2", "w", "m", "ki"))
        P.op("dve", lambda e: e.tensor_scalar(out=y, in0=ang, scalar1=1.0 / (2 * math.pi), scalar2=64.5,
                                              op0=ALU.mult, op1=ALU.add), r=rk + [tk], w=[tk])
        for off, out, wk in ((0.0, out_s, wk_s), (0.25, out_c, wk_c)):
            P.op("dve", lambda e, off=off: e.tensor_scalar(out=y2, in0=y, scalar1=off, scalar2=None, op0=ALU.add),
                 r=[tk], w=[tk])
            P.op("dve", lambda e: e.tensor_copy(out=ki, in_=y2), r=[tk], w=[tk])
            P.op("dve", lambda e: e.tensor_copy(out=w, in_=ki), r=[tk], w=[tk])
            P.op("dve", lambda e: e.tensor_tensor(out=w, in0=y2, in1=w, op=ALU.subtract), r=[tk], w=[tk])
            P.op("dve", lambda e: e.tensor_scalar(out=m, in0=w, scalar1=0.0, scalar2=None, op0=ALU.is_lt), r=[tk], w=[tk])
            P.op("dve", lambda e: e.tensor_tensor(out=w, in0=w, in1=m, op=ALU.add), r=[tk], w=[tk])
            P.op("act", lambda e, out=out: e.activation(out=out, in_=w, func=AF.Sin, scale=2 * math.pi, bias=self.negpi[:]),
                 r=[tk, "negpi"], w=wk + [tk])

    def declare_s5(self):
        self.din("s5_col", [2, 128, 3, 32])
        self.din("s5_row", [2, 3, 4096])
        self.din("s5_B", [2, 2, 128, 32, 128])
        self.din("s5_C", [2, 2, 128, 32, 128])
        self.din("s5_tau", [128, 128])
        self.din("i_s5d", [128, 8])
        self.din("i_s5bglu", [128, 16])
        self.din("s5_w_glu", [D, 2 * D])
        self.dscr("H32", [D, NT])
        self.dscr("YF", [D, NT])
        self.dscr("ZB", [D, NT], BF16)

    def s5_mixer(self, li):
        P = self.P
        H32 = self.dram["H32"]
        h32v = H32.rearrange("(kc p) t -> p kc t", p=128)
        yfv = self.dram["YF"].rearrange("(kc p) t -> p kc t", p=128)
        zbv = self.dram["ZB"].rearrange("(kc p) t -> p kc t", p=128)
        self.norm_phase(self.A1, 0, ctx=True, h32=H32, write_ht=False)
        with ExitStack() as st:
            EC = self.sb("s5EC", [128, 32, 128], F32, st)
            ES = self.sb("s5ES", [128, 32, 128], F32, st)
            BTr = self.sb("s5BTr", [128, 32, 128], F32, st)
            BTi = self.sb("s5BTi", [128, 32, 128], F32, st)
            CTr = self.sb("s5CTr", [128, 32, 128], F32, st)
            CTi = self.sb("s5CTi", [128, 32, 128], F32, st)
            tau = self.sb("s5tau", [128, 128], F32, st)
            dsk = self.sb("s5dsk", [128, 8], F32, st)
            colp = self.sb("s5colp", [128, 3, 32], F32, st)
            angc = self.sb("s5angc", [128, 32], F32, st)
            magc = self.sb("s5magc", [128, 32], F32, st)
            hpr = self.sb("s5hpr", [128, 32], F32, st)
            hpi = self.sb("s5hpi", [128, 32], F32, st)
            self.negpi = self.sb("negpi", [128, 1], F32, st)
            P.op("dve", lambda e: e.memset(self.negpi[:], -math.pi), w=["negpi"])
            P.dma("sp", tau[:], self.dram["s5_tau"], w=["s5tau"])
            P.dma("sp", dsk[:], self.dram["i_s5d"], w=["s5dsk"])
            for d in range(2):
                with ExitStack() as st2:
                    T = {k: self.sb("s5T" + k, [128, 1024], F32, st2) for k in ("y", "y2", "w", "m")}
                    T["ki"] = self.sb("s5Tki", [128, 1024], I32, st2)
                    rows = self.sb("s5rows", [128, 3, 1024], F32, st2)
                    R = {k: self.sb("s5R" + k, [128, 1024], F32, st2) for k in
                         ("step", "lr", "mag", "ang", "sn", "cs", "a1", "ai", "den", "kr", "ki2", "t")}
                    braw = self.sb("s5braw", [128, 2, 8, 128], F32, st2)

                    def disc(lre, lim, lst, F, tag, need_k):
                        k = "s5R"
                        step, lr, mag, ang = (R[x][:, :F] for x in ("step", "lr", "mag", "ang"))
                        P.op("act", lambda e: e.activation(out=step, in_=lst, func=AF.Exp), r=[tag, k], w=[k])
                        P.op("dve", lambda e: e.tensor_scalar(out=lr, in0=lre, scalar1=-1e-4, scalar2=None, op0=ALU.min), r=[tag, k], w=[k])
                        P.op("dve", lambda e: e.tensor_tensor(out=mag, in0=lr, in1=step, op=ALU.mult), r=[k], w=[k])
                        P.op("act", lambda e: e.activation(out=mag, in_=mag, func=AF.Exp), r=[k], w=[k])
                        P.op("dve", lambda e: e.tensor_tensor(out=ang, in0=lim, in1=step, op=ALU.mult), r=[tag, k], w=[k])
                        if not need_k:
                            return
                        sn, cs, a1, ai, den, kr, ki2, t = (R[x][:, :F] for x in ("sn", "cs", "a1", "ai", "den", "kr", "ki2", "t"))
                        self.sincos(ang, F, sn, cs, T, [k], [k], [k])
                        P.op("dve", lambda e: e.tensor_tensor(out=a1, in0=mag, in1=cs, op=ALU.mult), r=[k], w=[k])
                        P.op("dve", lambda e: e.tensor_scalar(out=a1, in0=a1, scalar1=-1.0, scalar2=None, op0=ALU.add), r=[k], w=[k])
                        P.op("dve", lambda e: e.tensor_tensor(out=ai, in0=mag, in1=sn, op=ALU.mult), r=[k], w=[k])
                        P.op("dve", lambda e: e.tensor_tensor(out=den, in0=lr, in1=lr, op=ALU.mult), r=[k], w=[k])
                        P.op("dve", lambda e: e.tensor_tensor(out=t, in0=lim, in1=lim, op=ALU.mult), r=[tag, k], w=[k])
                        P.op("dve", lambda e: e.tensor_tensor(out=den, in0=den, in1=t, op=ALU.add), r=[k], w=[k])
                        P.op("dve", lambda e: e.reciprocal(out=den, in_=den), r=[k], w=[k])
                        P.op("dve", lambda e: e.tensor_tensor(out=kr, in0=a1, in1=lr, op=ALU.mult), r=[k], w=[k])
                        P.op("dve", lambda e: e.tensor_tensor(out=t, in0=ai, in1=lim, op=ALU.mult), r=[tag, k], w=[k])
                        P.op("dve", lambda e: e.tensor_tensor(out=kr, in0=kr, in1=t, op=ALU.add), r=[k], w=[k])
                        P.op("dve", lambda e: e.tensor_tensor(out=kr, in0=kr, in1=den, op=ALU.mult), r=[k], w=[k])
                        P.op("dve", lambda e: e.tensor_tensor(out=ki2, in0=ai, in1=lr, op=ALU.mult), r=[k], w=[k])
                        P.op("dve", lambda e: e.tensor_tensor(out=t, in0=a1, in1=lim, op=ALU.mult), r=[tag, k], w=[k])
                        P.op("dve", lambda e: e.tensor_tensor(out=ki2, in0=ki2, in1=t, op=ALU.subtract), r=[k], w=[k])
                        P.op("dve", lambda e: e.tensor_tensor(out=ki2, in0=ki2, in1=den, op=ALU.mult), r=[k], w=[k])

                    P.dma("sp", colp[:], self.dram["s5_col"][d], w=["s5colp"])
                    disc(colp[:, 0, :], colp[:, 1, :], colp[:, 2, :], 32, "s5colp", False)
                    P.op("dve", lambda e: e.tensor_copy(out=angc[:], in_=R["ang"][:, :32]), r=["s5R"], w=["s5angc"])
                    P.op("dve", lambda e: e.tensor_copy(out=magc[:], in_=R["mag"][:, :32]), r=["s5R"], w=["s5magc"])
                    for q in range(4):
                        ph = R["t"][:, :1024].rearrange("p (g t) -> p g t", t=128)
                        P.op("dve", lambda e, q=q, ph=ph: e.tensor_tensor(
                            out=ph, in0=angc[:, q * 8:(q + 1) * 8].unsqueeze(2).to_broadcast([128, 8, 128]),
                            in1=tau[:].unsqueeze(1).to_broadcast([128, 8, 128]), op=ALU.mult),
                            r=["s5angc", "s5tau", "s5R"], w=["s5R"])
                        self.sincos(R["t"][:, :1024], 1024,
                                    ES[:, q * 8:(q + 1) * 8, :].rearrange("p g t -> p (g t)"),
                                    EC[:, q * 8:(q + 1) * 8, :].rearrange("p g t -> p (g t)"), T, ["s5R"], ["s5ES"], ["s5EC"])
                    for q in range(4):
                        for i3 in range(3):
                            P.dma("sp", rows[:, i3, :], self.dram["s5_row"][d, i3:i3 + 1, q * 1024:(q + 1) * 1024].to_broadcast([128, 1024]),
                                  w=["s5rows"])
                        disc(rows[:, 0, :], rows[:, 1, :], rows[:, 2, :], 1024, "s5rows", True)
                        for ri in range(2):
                            P.dma("sp", braw[:, ri], self.dram["s5_B"][d, ri, :, q * 8:(q + 1) * 8, :], w=["s5braw"])
                        kr3 = R["kr"][:, :1024].rearrange("p (g t) -> p g t", t=128)
                        ki3 = R["ki2"][:, :1024].rearrange("p (g t) -> p g t", t=128)
                        t3 = R["t"][:, :1024].rearrange("p (g t) -> p g t", t=128)
                        bq_r, bq_i = BTr[:, q * 8:(q + 1) * 8, :], BTi[:, q * 8:(q + 1) * 8, :]
                        P.op("dve", lambda e, bq_r=bq_r, kr3=kr3, br0=braw[:, 0]: e.tensor_tensor(out=bq_r, in0=br0, in1=kr3, op=ALU.mult), r=["s5braw", "s5R"], w=["s5BTr"])
                        P.op("dve", lambda e, ki3=ki3, t3=t3, br1=braw[:, 1]: e.tensor_tensor(out=t3, in0=br1, in1=ki3, op=ALU.mult), r=["s5braw", "s5R"], w=["s5R"])
                        P.op("dve", lambda e, bq_r=bq_r, t3=t3: e.tensor_tensor(out=bq_r, in0=bq_r, in1=t3, op=ALU.subtract), r=["s5R", "s5BTr"], w=["s5BTr"])
                        P.op("dve", lambda e, bq_i=bq_i, kr3=kr3, br1=braw[:, 1]: e.tensor_tensor(out=bq_i, in0=br1, in1=kr3, op=ALU.mult), r=["s5braw", "s5R"], w=["s5BTi"])
                        P.op("dve", lambda e, ki3=ki3, t3=t3, br0=braw[:, 0]: e.tensor_tensor(out=t3, in0=br0, in1=ki3, op=ALU.mult), r=["s5braw", "s5R"], w=["s5R"])
                        P.op("dve", lambda e, bq_i=bq_i, t3=t3: e.tensor_tensor(out=bq_i, in0=bq_i, in1=t3, op=ALU.add), r=["s5R", "s5BTi"], w=["s5BTi"])
                    P.dma("sp", CTr[:], self.dram["s5_C"][d, 0], w=["s5CTr"])
                    P.dma("sp", CTi[:], self.dram["s5_C"][d, 1], w=["s5CTi"])
                    P.op("pool", lambda e: e.tensor_scalar(out=CTi[:], in0=CTi[:], scalar1=-1.0, scalar2=None, op0=ALU.mult), r=["s5CTi"], w=["s5CTi"])
                    P.barrier()
                st3 = ExitStack()
                W = []
                for i in range(4):
                    Wd = {k: self.sb(f"s5{k}{i}", [128, 512], F32, st3) for k in ("m1", "m2", "m3", "m4", "sr", "si", "br", "bi")}
                    Wd["gr"], Wd["gi"], Wd["hr"], Wd["hi"] = Wd["br"], Wd["bi"], Wd["m1"], Wd["m3"]
                    W.append(Wd)
                ub = [self.sb(f"s5ub{i}", [128, 8, 128], F32, st3) for i in range(2)]
                yfb = [self.sb(f"s5yf{i}", [128, 8, 128], F32, st3) for i in range(2)]
                yst = [self.sb(f"s5yst{i}", [128, 128], F32, st3) for i in range(2)]
                zst = [self.sb(f"s5zst{i}", [128, 8, 128], BF16, st3) for i in range(2)]
                zst_f = [self.sb(f"s5yfo{i}", [128, 8, 128], F32, st3) for i in range(2)]
                P.op("dve", lambda e: e.memset(hpr[:], 0.0), w=[f"s5hpr{j}" for j in range(8)])
                P.op("dve", lambda e: e.memset(hpi[:], 0.0), w=[f"s5hpi{j}" for j in range(8)])
                order = [0, 1] + list(range(2, 34)) if d == 0 else [1, 0] + list(range(33, 1, -1))
                cnt = 0
                G = Stager(P, enabled=self.cfg.get("stage_s5", False))
                for bi, blk in enumerate(order):
                    tok0 = blk * 128
                    bb = bi % 2
                    u = ub[bb]
                    P.dma("sp", u[:], h32v[:, :, tok0:tok0 + 128], r=bkeys("H32", tok0, tok0 + 128), w=[f"s5ub{bb}"])
                    if d == 1:
                        P.dma("sp", yfb[bb][:], yfv[:, :, tok0:tok0 + 128], r=bkeys("YF", tok0, tok0 + 128), w=[f"s5yf{bb}"])

                    def tv(ap, d=d):
                        return ap if d == 0 else ap[:, ::-1]

                    for j in range(8):
                        if j == 4:
                            G.flush()
                        wb = cnt % 2
                        ws = cnt % 4
                        cnt += 1
                        Wb = W[ws]
                        wk = f"s5W{ws}"
                        psR, psI, psY = self.psum[2 * wb], self.psum[2 * wb + 1], self.psum[4 + wb]
                        kR, kI, kY = f"ps{2 * wb}", f"ps{2 * wb + 1}", f"ps{4 + wb}"

                        u_j = tv(u[:, j, :])

                        def mmB(e, psR=psR, psI=psI, u_j=u_j, j=j):
                            ins = None
                            for g4 in range(4):
                                gp = 4 * j + g4
                                e.matmul(psR[:, g4 * 128:(g4 + 1) * 128], lhsT=BTr[:, gp, :], rhs=u_j, start=True, stop=True)
                                ins = e.matmul(psI[:, g4 * 128:(g4 + 1) * 128], lhsT=BTi[:, gp, :], rhs=u_j, start=True, stop=True)
                            return ins

                        G.op(0, "pe", mmB, r=["s5BTr", "s5BTi", f"s5ub{bb}"], w=[kR, kI])
                        ec = EC[:, 4 * j:4 * j + 4, :].rearrange("p g t -> p (g t)")
                        es = ES[:, 4 * j:4 * j + 4, :].rearrange("p g t -> p (g t)")
                        m1, m2, m3, m4, gr, gi, sr, si, hr, hi, br, bi_ = (Wb[x][:] for x in ("m1", "m2", "m3", "m4", "gr", "gi", "sr", "si", "hr", "hi", "br", "bi"))
                        G.op(0, "act", lambda e, br=br, psR=psR: e.activation(out=br, in_=psR[:], func=AF.Copy), r=[kR], w=[wk + "br"])
                        G.op(0, "act", lambda e, bi_=bi_, psI=psI: e.activation(out=bi_, in_=psI[:], func=AF.Copy), r=[kI], w=[wk + "bi"])
                        G.op(1, "dve", lambda e, m1=m1, br=br, ec=ec: e.tensor_tensor(out=m1, in0=br, in1=ec, op=ALU.mult), r=[wk + "br", "s5EC"], w=[wk + "m1"])
                        G.op(1, "dve", lambda e, m2=m2, bi_=bi_, es=es: e.tensor_tensor(out=m2, in0=bi_, in1=es, op=ALU.mult), r=[wk + "bi", "s5ES"], w=[wk + "m2"])
                        G.op(1, "dve", lambda e, m3=m3, bi_=bi_, ec=ec: e.tensor_tensor(out=m3, in0=bi_, in1=ec, op=ALU.mult), r=[wk + "bi", "s5EC"], w=[wk + "m3"])
                        G.op(1, "pool", lambda e, m4=m4, br=br, es=es: e.tensor_tensor(out=m4, in0=br, in1=es, op=ALU.mult), r=[wk + "br", "s5ES"], w=[wk + "m4"])
                        G.op(2, "pool", lambda e, gr=gr, m1=m1, m2=m2: e.tensor_tensor(out=gr, in0=m1, in1=m2, op=ALU.add), r=[wk + "m1", wk + "m2"], w=[wk + "br"])
                        G.op(2, "dve", lambda e, gi=gi, m3=m3, m4=m4: e.tensor_tensor(out=gi, in0=m3, in1=m4, op=ALU.subtract), r=[wk + "m3", wk + "m4"], w=[wk + "bi"])
                        for g4 in range(4):
                            gp = 4 * j + g4
                            mg = magc[:, gp:gp + 1].to_broadcast([128, 128])
                            G.op(3, "dve", lambda e, o=Wb["sr"][:, g4 * 128:(g4 + 1) * 128], i_=Wb["gr"][:, g4 * 128:(g4 + 1) * 128], mg=mg, gp=gp: e.tensor_tensor_scan(out=o, data0=mg, data1=i_, initial=hpr[:, gp:gp + 1], op0=ALU.mult, op1=ALU.add), r=["s5magc", wk + "br", f"s5hpr{j}"], w=[wk + "sr"])
                            G.op(3, "dve", lambda e, o=Wb["si"][:, g4 * 128:(g4 + 1) * 128], i_=Wb["gi"][:, g4 * 128:(g4 + 1) * 128], mg=mg, gp=gp: e.tensor_tensor_scan(out=o, data0=mg, data1=i_, initial=hpi[:, gp:gp + 1], op0=ALU.mult, op1=ALU.add), r=["s5magc", wk + "bi", f"s5hpi{j}"], w=[wk + "si"])
                        G.op(4, "dve", lambda e, m1=m1, sr=sr, ec=ec: e.tensor_tensor(out=m1, in0=sr, in1=ec, op=ALU.mult), r=[wk + "sr", "s5EC"], w=[wk + "m1"])
                        G.op(4, "pool", lambda e, m2=m2, si=si, es=es: e.tensor_tensor(out=m2, in0=si, in1=es, op=ALU.mult), r=[wk + "si", "s5ES"], w=[wk + "m2"])
                        G.op(5, "dve", lambda e, hr=hr, m1=m1, m2=m2: e.tensor_tensor(out=hr, in0=m1, in1=m2, op=ALU.subtract), r=[wk + "m1", wk + "m2"], w=[wk + "m1"])
                        G.op(4, "dve", lambda e, m3=m3, sr=sr, es=es: e.tensor_tensor(out=m3, in0=sr, in1=es, op=ALU.mult), r=[wk + "sr", "s5ES"], w=[wk + "m3"])
                        G.op(4, "pool", lambda e, m4=m4, si=si, ec=ec: e.tensor_tensor(out=m4, in0=si, in1=ec, op=ALU.mult), r=[wk + "si", "s5EC"], w=[wk + "m4"])
                        G.op(5, "pool", lambda e, hi=hi, m3=m3, m4=m4: e.tensor_tensor(out=hi, in0=m3, in1=m4, op=ALU.add), r=[wk + "m3", wk + "m4"], w=[wk + "m3"])
                        hr3 = Wb["hr"][:].rearrange("p (g t) -> p g t", t=128)
                        hi3 = Wb["hi"][:].rearrange("p (g t) -> p g t", t=128)
                        G.op(6, "act", lambda e, hr3=hr3, j=j: e.activation(out=hpr[:, 4 * j:4 * j + 4], in_=hr3[:, :, 127], func=AF.Copy), r=[wk + "m1"], w=[f"s5hpr{j}"])
                        G.op(6, "act", lambda e, hi3=hi3, j=j: e.activation(out=hpi[:, 4 * j:4 * j + 4], in_=hi3[:, :, 127], func=AF.Copy), r=[wk + "m3"], w=[f"s5hpi{j}"])

                        def mmC(e, psY=psY, hr3=hr3, hi3=hi3, j=j):
                            ins = None
                            for g4 in range(4):
                                gp = 4 * j + g4
                                e.matmul(psY[:, 0:128], lhsT=CTr[:, gp, :], rhs=hr3[:, g4, :], start=(g4 == 0), stop=False)
                                ins = e.matmul(psY[:, 0:128], lhsT=CTi[:, gp, :], rhs=hi3[:, g4, :], start=False, stop=(g4 == 3))
                            return ins

                        G.op(6, "pe", mmC, r=["s5CTr", "s5CTi", wk + "m1", wk + "m3"], w=[kY])
                        if d == 0:
                            G.op(6, "act", lambda e, psY=psY, zo=zst_f[bb][:, j, :]: e.activation(out=zo, in_=psY[:, 0:128], func=AF.Copy), r=[kY], w=[f"s5yfo{bb}"])
                        else:
                            ys = yst[wb][:]
                            yf_j = tv(yfb[bb][:, j, :])
                            z_j = tv(zst[bb][:, j, :])
                            G.op(6, "dve", lambda e, ys=ys, psY=psY, yf_j=yf_j: e.tensor_tensor(out=ys, in0=psY[:, 0:128], in1=yf_j, op=ALU.add), r=[kY, f"s5yf{bb}"], w=[f"s5yst{wb}"])
                            G.op(6, "dve", lambda e, ys=ys, u_j=u_j, j=j: e.scalar_tensor_tensor(out=ys, in0=u_j, scalar=dsk[:, j:j + 1], in1=ys, op0=ALU.mult, op1=ALU.add), r=[f"s5ub{bb}", "s5dsk", f"s5yst{wb}"], w=[f"s5yst{wb}"])
                            G.op(6, "act", lambda e, ys=ys, z_j=z_j: e.activation(out=z_j, in_=ys, func=AF.Gelu), r=[f"s5yst{wb}"], w=[f"s5zst{bb}"])
                    G.flush()
                    if d == 0:
                        P.dma("sp", yfv[:, :, tok0:tok0 + 128], zst_f[bb][:], r=[f"s5yfo{bb}"], w=bkeys("YF", tok0, tok0 + 128))
                    else:
                        P.dma("sp", zbv[:, :, tok0:tok0 + 128], zst[bb][:], r=[f"s5zst{bb}"], w=bkeys("ZB", tok0, tok0 + 128))
                P.barrier()
                st3.close()
            P.barrier()
        self.s5_glu(li)

    def s5_glu(self, li):
        P = self.P
        zbv = self.dram["ZB"].rearrange("(kc p) t -> p kc t", p=128)
        wv = self.dram["s5_w_glu"].rearrange("(kc p) n -> p kc n", p=128)
        with ExitStack() as st:
            self.alloc_ht(st)
            HT = self.HT
            P.dma("sp", HT[:, :, HT_CTX0:HT_CTX0 + LC], zbv[:, :, 0:LC], r=bkeys("ZB", 0, LC), w=bkeys("HT", HT_CTX0, HT_CTX0 + LC))
            for t0 in range(0, L, 512):
                c0 = HT_LAT0 + t0
                P.dma("sp", HT[:, :, c0:c0 + 512], zbv[:, :, LC + t0:LC + t0 + 512], r=bkeys("ZB", LC + t0, LC + t0 + 512), w=bkeys("HT", c0, c0 + 512))
            bgl = self.sb("s5bgl", [128, 16], F32, st)
            P.dma("sp", bgl[:], self.dram["i_s5bglu"], w=["s5bgl"])
            wvb = [self.sb(f"gwv{i}", [128, 8, 512], BF16, st) for i in range(2)]
            wgb = [self.sb(f"gwg{i}", [128, 8, 512], BF16, st) for i in range(2)]
            sg = [self.sb(f"gsg{i}", [128, 512], F32, st) for i in range(2)]
            vv = [self.sb(f"gvv{i}", [128, 512], F32, st) for i in range(2)]
            xt = [self.sb(f"gx{i}", [128, 512], F32, st) for i in range(2)]
            cnt = 0
            for g in range(2):
                P.dma("pool", wvb[g][:], wv[:, :, g * 512:(g + 1) * 512], w=[f"gwv{g}"])
                P.dma("pool", wgb[g][:], wv[:, :, D + g * 512:D + (g + 1) * 512], w=[f"gwg{g}"])
                for ol in range(4):
                    oc = 4 * g + ol
                    for (seg, t0, n) in self.tiles(ctx=True):
                        tok0 = t0 if seg == 0 else LC + t0
                        c0 = ht_col(tok0)
                        s = 1 if seg == 0 else 0
                        b = cnt % 2
                        cnt += 1
                        psV, psG = self.psum[2 * b], self.psum[2 * b + 1]
                        kV, kG = f"ps{2 * b}", f"ps{2 * b + 1}"
                        xv, xname = self.xview(seg)
                        xsl = self.dram[xname][oc * 128:(oc + 1) * 128, t0:t0 + n]
                        P.dma("sp", xt[b][:, :n], xsl, r=bkeys(xname, t0, t0 + n), w=[f"gx{b}"])

                        def mm(e, ps, wb, ol=ol, c0=c0, n=n):
                            ins = None
                            for kc in range(8):
                                ins = e.matmul(ps[:, :n], lhsT=wb[:, kc, ol * 128:(ol + 1) * 128], rhs=HT[:, kc, c0:c0 + n], start=(kc == 0), stop=(kc == 7))
                            return ins

                        hk = bkeys("HT", c0, c0 + n)
                        P.op("pe", lambda e, psV=psV, g=g, ol=ol, c0=c0, n=n: mm(e, psV, wvb[g], ol, c0, n), r=[f"gwv{g}"] + hk, w=[kV])
                        P.op("pe", lambda e, psG=psG, g=g, ol=ol, c0=c0, n=n: mm(e, psG, wgb[g], ol, c0, n), r=[f"gwg{g}"] + hk, w=[kG])
                        P.op("act", lambda e, psG=psG, b=b, oc=oc, n=n: e.activation(out=sg[b][:, :n], in_=psG[:, :n], func=AF.Sigmoid, bias=bgl[:, 8 + oc:9 + oc]), r=[kG, "s5bgl"], w=[f"gsg{b}"])
                        P.op("dve", lambda e, psV=psV, b=b, oc=oc, n=n: e.scalar_tensor_tensor(out=vv[b][:, :n], in0=psV[:, :n], scalar=bgl[:, oc:oc + 1], in1=sg[b][:, :n], op0=ALU.add, op1=ALU.mult), r=[kV, "s5bgl", f"gsg{b}"], w=[f"gvv{b}"])
                        P.op("dve", lambda e, b=b, oc=oc, n=n, s=s: e.scalar_tensor_tensor(out=xt[b][:, :n], in0=vv[b][:, :n], scalar=self.modv[:, 16 + oc, s:s + 1], in1=xt[b][:, :n], op0=ALU.mult, op1=ALU.add), r=[f"gvv{b}", "modv", f"gx{b}"], w=[f"gx{b}"])
                        P.dma("sp", xsl, xt[b][:, :n], r=[f"gx{b}"], w=bkeys(xname, t0, t0 + n))
            P.barrier()

    def proj_residual(self, wsrc, K_chunks, gsl, ctx=True):
        P = self.P
        HT = self.HT
        wv = wsrc.rearrange("(kc p) n -> p kc n", p=128)
        with ExitStack() as st:
            wb = [self.sb(f"prw{i}", [128, K_chunks, 512], BF16, st) for i in range(2)]
            xt = [self.sb(f"prx{i}", [128, 512], F32, st) for i in range(2)]
            cnt = 0
            for g in range(2):
                P.dma("pool", wb[g][:], wv[:, :, g * 512:(g + 1) * 512], w=[f"prw{g}"])
                for ol in range(4):
                    oc = 4 * g + ol
                    for (seg, t0, n) in self.tiles(ctx=ctx):
                        tok0 = t0 if seg == 0 else LC + t0
                        c0 = ht_col(tok0)
                        s = 1 if seg == 0 else 0
                        b = cnt % 2
                        cnt += 1
                        ps = self.psum[b]
                        xname = "XL" if seg == 1 else "XC"
                        xsl = self.dram[xname][oc * 128:(oc + 1) * 128, t0:t0 + n]
                        P.dma("sp", xt[b][:, :n], xsl, r=bkeys(xname, t0, t0 + n), w=[f"prx{b}"])

                        def mm(e, ps=ps, g=g, ol=ol, c0=c0, n=n):
                            ins = None
                            for kc in range(K_chunks):
                                ins = e.matmul(ps[:, :n], lhsT=wb[g][:, kc, ol * 128:(ol + 1) * 128], rhs=HT[:, kc, c0:c0 + n],
                                               start=(kc == 0), stop=(kc == K_chunks - 1))
                            return ins

                        P.op("pe", mm, r=[f"prw{g}"] + bkeys("HT", c0, c0 + n), w=[f"ps{b}"])
                        P.op("dve", lambda e, ps=ps, b=b, oc=oc, n=n, s=s: e.scalar_tensor_tensor(
                            out=xt[b][:, :n], in0=ps[:, :n], scalar=self.modv[:, gsl + oc, s:s + 1], in1=xt[b][:, :n],
                            op0=ALU.mult, op1=ALU.add), r=[f"ps{b}", "modv", f"prx{b}"], w=[f"prx{b}"])
                        P.dma("sp", xsl, xt[b][:, :n], r=[f"prx{b}"], w=bkeys(xname, t0, t0 + n))
            P.barrier()

    def declare_lru(self):
        self.din("lru_w_in", [D, 2 * D])
        self.din("lru_w_a", [2, 4, 256, 256])
        self.din("lru_w_x", [2, 4, 256, 256])
        self.din("lru_w_out", [D, D])
        self.din("i_lru_cw", [128, 8, 4])
        self.din("i_lru_cb", [128, 8])
        self.din("i_lru_vec", [128, 3, 2, 8])
        self.dscr("YB", [D, NT])
        self.dscr("XC32", [D, NT])
        self.dscr("HS", [D, NT])

    def lru_mixer(self, li):
        P = self.P
        YB, XC32, HS = self.dram["YB"], self.dram["XC32"], self.dram["HS"]
        win = self.dram["lru_w_in"].rearrange("(kc p) n -> p kc n", p=128)
        with ExitStack() as st:
            XCb = self.sb("lruXCb", [128, 8, NT], BF16, st)
            vec = self.sb("lruvec", [128, 3, 2, 8], F32, st)
            cs = self.sb("lrucs", [128, 2, 8], F32, st)
            onec = self.sb("lruone", [128, 1], F32, st)
            P.op("dve", lambda e: e.memset(onec[:], 1.0), w=["lruone"])
            P.dma("sp", vec[:], self.dram["i_lru_vec"], w=["lruvec"])
            with ExitStack() as st2:
                tt = {k: self.sb("lrut" + k, [128, 16], F32, st2) for k in ("t", "ab", "u", "s", "s2", "p")}
                lam = vec[:, 2].rearrange("p d k -> p (d k)")
                k = "lrutt"
                t, ab, u, s_, s2, p_ = (tt[x][:] for x in ("t", "ab", "u", "s", "s2", "p"))
                P.op("dve", lambda e: e.tensor_scalar(out=t, in0=lam, scalar1=-1.0, scalar2=None, op0=ALU.mult), r=["lruvec"], w=[k])
                P.op("dve", lambda e: e.tensor_tensor(out=ab, in0=t, in1=lam, op=ALU.max), r=[k, "lruvec"], w=[k])
                P.op("act", lambda e: e.activation(out=u, in_=ab, func=AF.Exp, scale=-1.0), r=[k], w=[k])
                P.op("dve", lambda e: e.tensor_scalar(out=s_, in0=u, scalar1=2.0, scalar2=None, op0=ALU.add), r=[k], w=[k])
                P.op("dve", lambda e: e.reciprocal(out=s_, in_=s_), r=[k], w=[k])
                P.op("dve", lambda e: e.tensor_tensor(out=s_, in0=s_, in1=u, op=ALU.mult), r=[k], w=[k])
                P.op("dve", lambda e: e.tensor_tensor(out=s2, in0=s_, in1=s_, op=ALU.mult), r=[k], w=[k])
                P.op("dve", lambda e: e.tensor_scalar(out=p_, in0=s2, scalar1=1.0 / 9, scalar2=1.0 / 7, op0=ALU.mult, op1=ALU.add), r=[k], w=[k])
                for cst in (1.0 / 5, 1.0 / 3, 1.0):
                    P.op("dve", lambda e: e.tensor_tensor(out=p_, in0=p_, in1=s2, op=ALU.mult), r=[k], w=[k])
                    P.op("dve", lambda e, cst=cst: e.tensor_scalar(out=p_, in0=p_, scalar1=cst, scalar2=None, op0=ALU.add), r=[k], w=[k])
                P.op("dve", lambda e: e.tensor_tensor(out=p_, in0=p_, in1=s_, op=ALU.mult), r=[k], w=[k])
                P.op("dve", lambda e: e.tensor_scalar(out=t, in0=t, scalar1=0.0, scalar2=None, op0=ALU.max), r=[k], w=[k])
                P.op("dve", lambda e: e.scalar_tensor_tensor(out=t, in0=p_, scalar=2.0, in1=t, op0=ALU.mult, op1=ALU.add), r=[k], w=[k])
                P.op("dve", lambda e: e.tensor_scalar(out=cs[:].rearrange("p d k -> p (d k)"), in0=t, scalar1=-8.0, scalar2=None, op0=ALU.mult), r=[k], w=["lrucs"])
                P.barrier()
            with ExitStack() as hst:
                self.alloc_ht(hst)
                HT = self.HT
                with ExitStack() as st2:
                    self.norm_phase(self.A1, 0, ctx=True)
                    cw = self.sb("lrucw", [128, 8, 4], F32, st2)
                    cb = self.sb("lrucb", [128, 8], F32, st2)
                    P.dma("sp", cw[:], self.dram["i_lru_cw"], w=["lrucw"])
                    P.dma("sp", cb[:], self.dram["i_lru_cb"], w=["lrucb"])
                    wb = [self.sb(f"lruw{i}", [128, 8, 512], BF16, st2) for i in range(2)]
                    yst = [self.sb(f"lruy{i}", [128, 512], F32, st2) for i in range(2)]
                    xbl = self.sb("lruxbl", [128, L + 3], F32, st2)
                    xbc = self.sb("lruxbc", [128, LC + 3], F32, st2)
                    xcf = self.sb("lruxcf", [128, L], F32, st2)
                    for bufx, nn in ((xbl, L), (xbc, LC)):
                        P.op("dve", lambda e, bufx=bufx: e.memset(bufx[:, 0:2], 0.0), w=["lruxb"])
                        P.op("dve", lambda e, bufx=bufx, nn=nn: e.memset(bufx[:, nn + 2:nn + 3], 0.0), w=["lruxb"])
                    cnt = 0
                    for g in range(4):
                        gb = g % 2
                        P.dma("pool", wb[gb][:], win[:, :, g * 512:(g + 1) * 512], w=[f"lruw{gb}"])
                        for ol in range(4):
                            oc_all = 4 * g + ol
                            for (seg, t0, n) in self.tiles(ctx=True):
                                tok0 = t0 if seg == 0 else LC + t0
                                c0 = ht_col(tok0)
                                b = cnt % 2
                                cnt += 1
                                ps = self.psum[b]

                                def mm(e, ps=ps, gb=gb, ol=ol, c0=c0, n=n):
                                    ins = None
                                    for kc in range(8):
                                        ins = e.matmul(ps[:, :n], lhsT=wb[gb][:, kc, ol * 128:(ol + 1) * 128], rhs=HT[:, kc, c0:c0 + n],
                                                       start=(kc == 0), stop=(kc == 7))
                                    return ins

                                P.op("pe", mm, r=[f"lruw{gb}"] + bkeys("HT", c0, c0 + n), w=[f"ps{b}"])
                                if oc_all >= 8:
                                    oc = oc_all - 8
                                    P.op("act", lambda e, ps=ps, b=b, n=n: e.activation(out=yst[b][:, :n], in_=ps[:, :n], func=AF.Gelu), r=[f"ps{b}"], w=[f"lruy{b}"])
                                    P.dma("sp", YB[oc * 128:(oc + 1) * 128, tok0:tok0 + n], yst[b][:, :n], r=[f"lruy{b}"], w=bkeys("YB", tok0, tok0 + n))
                                else:
                                    dst = xbc[:, 2 + t0:2 + t0 + n] if seg == 0 else xbl[:, 2 + t0:2 + t0 + n]
                                    P.op("act", lambda e, ps=ps, dst=dst, n=n: e.activation(out=dst, in_=ps[:, :n], func=AF.Copy), r=[f"ps{b}"], w=["lruxb"])
                            if oc_all < 8:
                                oc = oc_all
                                for (bufx, nn, tokb) in ((xbc, LC, 0), (xbl, L, LC)):
                                    o = xcf[:, :nn]
                                    P.op("dve", lambda e, o=o, bufx=bufx, nn=nn, oc=oc: e.tensor_scalar(out=o, in0=bufx[:, 0:nn], scalar1=cw[:, oc, 0:1], scalar2=cb[:, oc:oc + 1], op0=ALU.mult, op1=ALU.add), r=["lruxb", "lrucw", "lrucb"], w=["lruxcf"])
                                    for kk in range(1, 4):
                                        P.op("dve", lambda e, o=o, bufx=bufx, nn=nn, oc=oc, kk=kk: e.scalar_tensor_tensor(out=o, in0=bufx[:, kk:kk + nn], scalar=cw[:, oc, kk:kk + 1], in1=o, op0=ALU.mult, op1=ALU.add), r=["lruxb", "lrucw", "lruxcf"], w=["lruxcf"])
                                    P.op("act", lambda e, o=o, nn=nn, oc=oc, tokb=tokb: e.activation(out=XCb[:, oc, tokb:tokb + nn], in_=o, func=AF.Copy), r=["lruxcf"], w=["lruXCb"])
                                    P.dma("sp", XC32[oc * 128:(oc + 1) * 128, tokb:tokb + nn], o, r=["lruxcf"], w=bkeys("XC32", tokb, tokb + nn))
                    P.barrier()
                with ExitStack() as st2:
                    wa = [self.sb(f"lruwa{i}", [128, 2, 256], BF16, st2) for i in range(2)]
                    wx = [self.sb(f"lruwx{i}", [128, 2, 256], BF16, st2) for i in range(2)]
                    Wk = []
                    for i in range(2):
                        Wk.append({k: self.sb(f"lru{k}{i}", [128, 512], F32, st2) for k in ("r", "i", "a", "q", "b", "h", "xc", "hf", "yb")})
                    carr = self.sb("lrucarr", [128, 8], F32, st2)
                    cnt = 0
                    for d in range(2):
                        tl = self.tiles(ctx=True)
                        order = tl if d == 0 else [tl[0]] + tl[:0:-1]
                        for blk in range(4):
                            wbi = blk % 2
                            P.dma("pool", wa[wbi][:], self.dram["lru_w_a"][d, blk].rearrange("(ic p) n -> p ic n", p=128), w=[f"lruwa{wbi}"])
                            P.dma("pool", wx[wbi][:], self.dram["lru_w_x"][d, blk].rearrange("(ic p) n -> p ic n", p=128), w=[f"lruwx{wbi}"])
                            for ol in range(2):
                                oc = 2 * blk + ol
                                P.op("dve", lambda e, oc=oc: e.memset(carr[:, oc:oc + 1], 0.0), w=[f"lrucarr{oc}"])
                                for (seg, t0, n) in order:
                                    tok0 = t0 if seg == 0 else LC + t0
                                    b = cnt % 2
                                    cnt += 1
                                    Wb = Wk[b]
                                    wk = f"lruW{b}"
                                    psA, psX = self.psum[2 * b], self.psum[2 * b + 1]
                                    kA, kX = f"ps{2 * b}", f"ps{2 * b + 1}"

                                    def mmg(e, ps, wt, wbi=wbi, blk=blk, ol=ol, tok0=tok0, n=n):
                                        ins = None
                                        for ic in range(2):
                                            ins = e.matmul(ps[:, :n], lhsT=wt[wbi][:, ic, ol * 128:(ol + 1) * 128], rhs=XCb[:, 2 * blk + ic, tok0:tok0 + n], start=(ic == 0), stop=(ic == 1))
                                        return ins

                                    P.op("pe", lambda e, psA=psA, wbi=wbi, blk=blk, ol=ol, tok0=tok0, n=n: mmg(e, psA, wa, wbi, blk, ol, tok0, n), r=[f"lruwa{wbi}", "lruXCb"], w=[kA])
                                    P.op("pe", lambda e, psX=psX, wbi=wbi, blk=blk, ol=ol, tok0=tok0, n=n: mmg(e, psX, wx, wbi, blk, ol, tok0, n), r=[f"lruwx{wbi}", "lruXCb"], w=[kX])
                                    P.dma("sp", Wb["xc"][:, :n], XC32[oc * 128:(oc + 1) * 128, tok0:tok0 + n], r=bkeys("XC32", tok0, tok0 + n), w=[wk + "xc"])
                                    r_, i_, a_, q_, b_, h_, xc_, hf_, yb_ = (Wb[x][:, :n] for x in ("r", "i", "a", "q", "b", "h", "xc", "hf", "yb"))
                                    P.op("act", lambda e, r_=r_, psA=psA, n=n, d=d, oc=oc: e.activation(out=r_, in_=psA[:, :n], func=AF.Sigmoid, bias=vec[:, 0, d, oc:oc + 1]), r=[kA, "lruvec"], w=[wk + "r"])
                                    P.op("act", lambda e, i_=i_, psX=psX, n=n, d=d, oc=oc: e.activation(out=i_, in_=psX[:, :n], func=AF.Sigmoid, bias=vec[:, 1, d, oc:oc + 1]), r=[kX, "lruvec"], w=[wk + "i"])
                                    P.op("act", lambda e, a_=a_, r_=r_, d=d, oc=oc: e.activation(out=a_, in_=r_, func=AF.Exp, scale=cs[:, d, oc:oc + 1]), r=[wk + "r", "lrucs"], w=[wk + "a"])
                                    P.op("pool", lambda e, q_=q_, a_=a_: e.tensor_tensor(out=q_, in0=a_, in1=a_, op=ALU.mult), r=[wk + "a"], w=[wk + "q"])
                                    P.op("act", lambda e, q_=q_: e.activation(out=q_, in_=q_, func=AF.Sqrt, scale=-1.0, bias=onec[:]), r=[wk + "q", "lruone"], w=[wk + "q"])
                                    P.op("pool", lambda e, b_=b_, q_=q_, i_=i_: e.tensor_tensor(out=b_, in0=q_, in1=i_, op=ALU.mult), r=[wk + "q", wk + "i"], w=[wk + "b"])
                                    P.op("pool", lambda e, b_=b_, xc_=xc_: e.tensor_tensor(out=b_, in0=b_, in1=xc_, op=ALU.mult), r=[wk + "b", wk + "xc"], w=[wk + "b"])
                                    tv = (lambda ap: ap) if d == 0 else (lambda ap: ap[:, ::-1])
                                    P.op("dve", lambda e, ho=tv(h_), aa=tv(a_), bb_=tv(b_), oc=oc: e.tensor_tensor_scan(out=ho, data0=aa, data1=bb_, initial=carr[:, oc:oc + 1], op0=ALU.mult, op1=ALU.add), r=[wk + "a", wk + "b", f"lrucarr{oc}"], w=[wk + "h"])
                                    last = h_[:, n - 1:n] if d == 0 else h_[:, 0:1]
                                    P.op("act", lambda e, last=last, oc=oc: e.activation(out=carr[:, oc:oc + 1], in_=last, func=AF.Copy), r=[wk + "h"], w=[f"lrucarr{oc}"])
                                    hsl = HS[oc * 128:(oc + 1) * 128, tok0:tok0 + n]
                                    if d == 0:
                                        P.dma("sp", hsl, h_, r=[wk + "h"], w=bkeys("HS", tok0, tok0 + n))
                                    else:
                                        P.dma("sp", hf_, hsl, r=bkeys("HS", tok0, tok0 + n), w=[wk + "hf"])
                                        P.dma("sp", yb_, YB[oc * 128:(oc + 1) * 128, tok0:tok0 + n], r=bkeys("YB", tok0, tok0 + n), w=[wk + "yb"])
                                        P.op("pool", lambda e, hf_=hf_, h_=h_: e.tensor_tensor(out=hf_, in0=hf_, in1=h_, op=ALU.add), r=[wk + "h", wk + "hf"], w=[wk + "hf"])
                                        c0 = ht_col(tok0)
                                        P.op("dve", lambda e, hf_=hf_, yb_=yb_, oc=oc, c0=c0, n=n: e.tensor_tensor(out=HT[:, oc, c0:c0 + n], in0=hf_, in1=yb_, op=ALU.mult), r=[wk + "hf", wk + "yb"], w=bkeys("HT", c0, c0 + n))
                    P.barrier()
                self.proj_residual(self.dram["lru_w_out"], 8, 16, ctx=True)
            P.barrier()

    def declare_ret(self):
        self.din("ret_w_in", [D, 6144])
        self.din("ret_wk_perm", [D, 1024])
        self.din("ret_wq_perm", [D, 1024])
        self.din("ret_w_out", [2048, D])
        self.din("ret_ng", [128, 2048])
        self.din("ret_rope", [2, 4, 128, 64])
        self.din("ret_DT", [2, 128, 4, 128])
        self.din("ret_XI", [2, 128, 4, 128])
        self.din("ret_ZETA", [2, 128, 4])
        self.din("ident", [128, 128])
        self.dscr("KT", [D, NT])
        self.dscr("QT", [D, NT])
        self.dscr("VTOK", [NT, 2048])
        self.dscr("GTOK", [NT, 2048])
        self.dscr("OF", [NT, 2048])

    def ret_mixer(self, li):
        P = self.P
        KT, QT, VTOK, GTOK, OF = (self.dram[k] for k in ("KT", "QT", "VTOK", "GTOK", "OF"))
        win = self.dram["ret_w_in"].rearrange("(kc p) n -> p kc n", p=128)
        ident = self.sb("ident", [128, 128], F32)
        if not hasattr(self, "_ident_loaded"):
            P.dma("sp", ident[:], self.dram["ident"], w=["ident"])
            self._ident_loaded = True
        self.ident = ident
        with ExitStack() as hst:
            self.alloc_ht(hst)
            HT = self.HT
            self.norm_phase(self.A1, 0, ctx=True)
            with ExitStack() as st:
                rope = self.sb("retrope", [128, 2, 4, 64], F32, st)
                for a in range(2):
                    for b4 in range(4):
                        P.dma("sp", rope[:, a, b4, :], self.dram["ret_rope"][a, b4], w=["retrope"])
                w1 = [self.sb(f"retw1{i}", [128, 8, 512], BF16, st) for i in range(2)]
                w2 = [self.sb(f"retw2{i}", [128, 8, 512], BF16, st) for i in range(2)]
                m1 = [self.sb(f"retm1{i}", [128, 512], F32, st) for i in range(2)]
                m2 = [self.sb(f"retm2{i}", [128, 512], F32, st) for i in range(2)]
                cnt = 0
                gcount = 0
                for a, (col0, permname, dst) in enumerate(((0, "ret_wk_perm", KT), (3072, "ret_wq_perm", QT))):
                    pv = self.dram[permname].rearrange("(kc p) n -> p kc n", p=128)
                    for g in range(2):
                        gb = gcount % 2
                        gcount += 1
                        P.dma("pool", w1[gb][:], win[:, :, col0 + g * 512:col0 + (g + 1) * 512], w=[f"retw1{gb}"])
                        P.dma("pool", w2[gb][:], pv[:, :, g * 512:(g + 1) * 512], w=[f"retw2{gb}"])
                        for ol in range(4):
                            oc = 4 * g + ol
                            typ = oc % 2
                            for (seg, t0, n) in self.tiles(ctx=True):
                                tok0 = t0 if seg == 0 else LC + t0
                                c0 = ht_col(tok0)
                                b = cnt % 2
                                cnt += 1
                                ps1, ps2 = self.psum[2 * b], self.psum[2 * b + 1]
                                k1, k2 = f"ps{2 * b}", f"ps{2 * b + 1}"

                                def mmA(e, ps, wt, ol=ol, c0=c0, n=n):
                                    ins = None
                                    for kc in range(8):
                                        ins = e.matmul(ps[:, :n], lhsT=wt[:, kc, ol * 128:(ol + 1) * 128], rhs=HT[:, kc, c0:c0 + n], start=(kc == 0), stop=(kc == 7))
                                    return ins

                                hk = bkeys("HT", c0, c0 + n)
                                P.op("pe", lambda e, ps1=ps1, gb=gb, ol=ol, c0=c0, n=n: mmA(e, ps1, w1[gb], ol, c0, n), r=[f"retw1{gb}"] + hk, w=[k1])
                                o_ = m1[b][:, :n]
                                if seg == 0:
                                    P.op("act", lambda e, o_=o_, ps1=ps1, n=n, a=a: e.activation(out=o_, in_=ps1[:, :n], func=AF.Copy, scale=(1.0 if a == 0 else 1.0 / 16)), r=[k1], w=[f"retm1{b}"])
                                else:
                                    P.op("pe", lambda e, ps2=ps2, gb=gb, ol=ol, c0=c0, n=n: mmA(e, ps2, w2[gb], ol, c0, n), r=[f"retw2{gb}"] + hk, w=[k2])
                                    r0 = t0 // 64
                                    nr = n // 64
                                    if typ == 0:
                                        ct = rope[:, a, 0, r0:r0 + nr].unsqueeze(2).to_broadcast([128, nr, 64])
                                        sn = rope[:, a, 1, r0:r0 + nr].unsqueeze(2).to_broadcast([128, nr, 64])
                                    else:
                                        ct = rope[:, a, 2, :].unsqueeze(1).to_broadcast([128, nr, 64])
                                        sn = rope[:, a, 3, :].unsqueeze(1).to_broadcast([128, nr, 64])
                                    o3 = m1[b][:, :n].rearrange("p (r c) -> p r c", c=64)
                                    t3 = m2[b][:, :n].rearrange("p (r c) -> p r c", c=64)
                                    P.op("dve", lambda e, o3=o3, ps1=ps1, ct=ct, n=n: e.tensor_tensor(out=o3, in0=ps1[:, :n].rearrange("p (r c) -> p r c", c=64), in1=ct, op=ALU.mult), r=[k1, "retrope"], w=[f"retm1{b}"])
                                    P.op("dve", lambda e, t3=t3, ps2=ps2, sn=sn, n=n: e.tensor_tensor(out=t3, in0=ps2[:, :n].rearrange("p (r c) -> p r c", c=64), in1=sn, op=ALU.mult), r=[k2, "retrope"], w=[f"retm2{b}"])
                                    P.op("pool", lambda e, o_=o_, b=b, n=n: e.tensor_tensor(out=o_, in0=o_, in1=m2[b][:, :n], op=ALU.add), r=[f"retm1{b}", f"retm2{b}"], w=[f"retm1{b}"])
                                P.dma("sp", dst[oc * 128:(oc + 1) * 128, tok0:tok0 + n], o_, r=[f"retm1{b}"], w=bkeys(dst.name if hasattr(dst, "name") else str(a), tok0, tok0 + n))
                P.barrier()
            with ExitStack() as st:
                wv = [self.sb(f"retwv{i}", [128, 8, 512], BF16, st) for i in range(2)]
                og = [self.sb(f"retog{i}", [128, 512], F32, st) for i in range(2)]
                cnt = 0
                for gi in range(8):
                    is_gate = gi >= 4
                    col0 = (1024 if not is_gate else 4096) + (gi % 4) * 512
                    dst = GTOK if is_gate else VTOK
                    dkey = "GTOK" if is_gate else "VTOK"
                    gb = gi % 2
                    P.dma("pool", wv[gb][:], win[:, :, col0:col0 + 512], w=[f"retwv{gb}"])
                    for blk in range(NT // 128):
                        tok0 = blk * 128
                        c0 = ht_col(tok0)
                        b = cnt % 2
                        cnt += 1
                        ps = self.psum[b]

                        def mm(e, ps=ps, gb=gb, c0=c0):
                            ins = None
                            for kc in range(8):
                                ins = e.matmul(ps[:, :512], lhsT=HT[:, kc, c0:c0 + 128], rhs=wv[gb][:, kc, :], start=(kc == 0), stop=(kc == 7))
                            return ins

                        P.op("pe", mm, r=[f"retwv{gb}"] + bkeys("HT", c0, c0 + 128), w=[f"ps{b}"])
                        P.op("act", lambda e, ps=ps, b=b, is_gate=is_gate: e.activation(out=og[b][:], in_=ps[:, :512], func=(AF.Silu if is_gate else AF.Copy)), r=[f"ps{b}"], w=[f"retog{b}"])
                        P.dma("sp", dst[tok0:tok0 + 128, (gi % 4) * 512:(gi % 4 + 1) * 512], og[b][:], r=[f"retog{b}"], w=bkeys(dkey, tok0, tok0 + 128))
                P.barrier()
        ktv = KT.rearrange("(c p) t -> p c t", p=128)
        qtv = QT.rearrange("(c p) t -> p c t", p=128)
        xlv = self.dram["XL"].rearrange("(kc p) t -> p kc t", p=128)
        xcv = self.dram["XC"].rearrange("(kc p) t -> p kc t", p=128)
        gam = [1.0 - 2.0 ** (-5 - h) for h in range(4)]
        with ExitStack() as st:
            wo = self.sb("retwo", [128, 16, D], BF16, st)
            wsrc = self.dram["ret_w_out"].rearrange("(ec p) n -> p ec n", p=128)
            for hh in range(2):
                P.dma("pool", wo[:, hh * 8:(hh + 1) * 8, :], wsrc[:, hh * 8:(hh + 1) * 8, :], w=[f"retwo{hh}"])
            ng = self.sb("retng", [128, 2048], F32, st)
            P.dma("sp", ng[:], self.dram["ret_ng"], w=["retng"])
            DT = self.sb("retDT", [128, 4, 128], F32, st)
            XI = self.sb("retXI", [128, 4, 128], F32, st)
            ZE = self.sb("retZE", [128, 4], F32, st)
            rst = [self.sb(f"retr{h}", [128, 2, 512], F32, st) for h in range(4)]
            kt = [self.sb(f"retkt{i}", [128, 8, 128], F32, st) for i in range(2)]
            qt = [self.sb(f"retqt{i}", [128, 8, 128], F32, st) for i in range(2)]
            qx = self.sb("retqx", [128, 8, 128], F32, st)
            vt = [self.sb(f"retvt{i}", [128, 2048], F32, st) for i in range(2)]
            kk = self.sb("retkk", [128, 1024], F32, st)
            sm = [self.sb(f"retsm{i}", [128, 128], F32, st) for i in range(2)]
            ost = [self.sb(f"retost{i}", [128, 2048], F32, st) for i in range(2)]
            oft = self.sb("retoft", [128, 2048], F32, st)
            gt = self.sb("retgt", [128, 2048], F32, st)
            ss = self.sb("retss", [128, 4], F32, st)
            junk = self.sb("retjunk", [128, 512], F32, st)
            mT = self.sb("retmT", [128, 16, 128], BF16, st)
            xt = self.sb("retxt", [128, 8, 128], F32, st)
            for d in range(2):
                P.dma("sp", DT[:], self.dram["ret_DT"][d], w=["retDT"])
                P.dma("sp", XI[:], self.dram["ret_XI"][d], w=["retXI"])
                P.dma("sp", ZE[:], self.dram["ret_ZETA"][d], w=["retZE"])
                for h in range(4):
                    P.op("dve", lambda e, h=h: e.memset(rst[h][:], 0.0), w=[f"retr{h}"])
                order = list(range(34)) if d == 0 else [1, 0] + list(range(33, 1, -1))
                hc = 0
                for ci, blk in enumerate(order):
                    tok0 = blk * 128
                    b = ci % 2
                    P.dma("sp", kt[b][:], ktv[:, :, tok0:tok0 + 128], r=bkeys("0", tok0, tok0 + 128), w=[f"retkt{b}"])
                    P.dma("sp", qt[b][:], qtv[:, :, tok0:tok0 + 128], r=bkeys("1", tok0, tok0 + 128), w=[f"retqt{b}"])
                    P.dma("sp", vt[b][:], VTOK[tok0:tok0 + 128, :], r=bkeys("VTOK", tok0, tok0 + 128), w=[f"retvt{b}"])
                    if d == 1:
                        P.dma("sp", oft[:], OF[tok0:tok0 + 128, :], r=bkeys("OF", tok0, tok0 + 128), w=["retoft"])
                        P.dma("sp", gt[:], GTOK[tok0:tok0 + 128, :], r=bkeys("GTOK", tok0, tok0 + 128), w=["retgt"])
                    psk = (self.psum[0], self.psum[1])

                    def trk(e, b=b):
                        ins = None
                        for c in range(8):
                            ins = e.transpose(out=psk[c // 4][:, (c % 4) * 128:(c % 4 + 1) * 128], in_=kt[b][:, c, :], identity=ident[:])
                        return ins

                    P.op("pe", trk, r=[f"retkt{b}", "ident"], w=["ps0", "ps1"])
                    for h in range(4):
                        P.op("act", lambda e, h=h: e.activation(out=kk[:, h * 256:(h + 1) * 256], in_=psk[h // 2][:, (h % 2) * 256:(h % 2 + 1) * 256], func=AF.Copy, scale=ZE[:, h:h + 1]), r=["ps0", "ps1", "retZE"], w=[f"retkk{h}"])
                    P.op("dve", lambda e, b=b: e.tensor_tensor(out=qx[:].rearrange("p (h c) t -> p h c t", c=2), in0=qt[b][:].rearrange("p (h c) t -> p h c t", c=2), in1=XI[:].unsqueeze(2).to_broadcast([128, 4, 2, 128]), op=ALU.mult), r=[f"retqt{b}", "retXI"], w=["retqx"])
                    for h in range(4):
                        sb_ = hc % 2
                        hc += 1
                        psS = self.psum[2 + sb_]
                        psO = self.psum[4 + sb_]
                        kS, kO = f"ps{2 + sb_}", f"ps{4 + sb_}"

                        def mmS(e, psS=psS, b=b, h=h):
                            ins = None
                            for dc in range(2):
                                ins = e.matmul(psS[:, 0:128], lhsT=kt[b][:, 2 * h + dc, :], rhs=qt[b][:, 2 * h + dc, :], start=(dc == 0), stop=(dc == 1))
                            return ins

                        P.op("pe", mmS, r=[f"retkt{b}", f"retqt{b}"], w=[kS])
                        P.op("dve", lambda e, psS=psS, sb_=sb_, h=h: e.tensor_tensor(out=sm[sb_][:], in0=psS[:, 0:128], in1=DT[:, h, :], op=ALU.mult), r=[kS, "retDT"], w=[f"retsm{sb_}"])

                        def mmO(e, psO=psO, sb_=sb_, b=b, h=h):
                            e.matmul(psO[:, :512], lhsT=sm[sb_][:], rhs=vt[b][:, h * 512:(h + 1) * 512], start=True, stop=False)
                            ins = None
                            for dc in range(2):
                                ins = e.matmul(psO[:, :512], lhsT=qx[:, 2 * h + dc, :], rhs=rst[h][:, dc, :], start=False, stop=(dc == 1))
                            return ins

                        P.op("pe", mmO, r=[f"retsm{sb_}", f"retvt{b}", "retqx", f"retr{h}"], w=[kO])
                        for dc in range(2):
                            psKV = self.psum[6 + dc]

                            def mmKV(e, psKV=psKV, b=b, h=h, dc=dc):
                                return e.matmul(psKV[:, :512], lhsT=kk[:, (2 * h + dc) * 128:(2 * h + dc + 1) * 128], rhs=vt[b][:, h * 512:(h + 1) * 512], start=True, stop=True)

                            P.op("pe", mmKV, r=[f"retkk{h}", f"retvt{b}"], w=[f"ps{6 + dc}"])
                            P.op("dve", lambda e, psKV=psKV, h=h, dc=dc: e.scalar_tensor_tensor(out=rst[h][:, dc, :], in0=rst[h][:, dc, :], scalar=float(gam[h] ** 128), in1=psKV[:, :512], op0=ALU.mult, op1=ALU.add), r=[f"ps{6 + dc}", f"retr{h}"], w=[f"retr{h}"])
                        if d == 0:
                            P.op("act", lambda e, psO=psO, b=b, h=h: e.activation(out=ost[b][:, h * 512:(h + 1) * 512], in_=psO[:, :512], func=AF.Copy), r=[kO], w=[f"retost{b}h{h}"])
                        else:
                            P.op("dve", lambda e, psO=psO, h=h: e.tensor_tensor(out=oft[:, h * 512:(h + 1) * 512], in0=oft[:, h * 512:(h + 1) * 512], in1=psO[:, :512], op=ALU.add), r=[kO, "retoft"], w=[f"retofs{h}"])
                            P.op("act", lambda e, h=h: e.activation(out=junk[:], in_=oft[:, h * 512:(h + 1) * 512], func=AF.Square, accum_out=ss[:, h:h + 1]), r=[f"retofs{h}"], w=["retjunk", f"retss{h}"])
                    if d == 0:
                        P.dma("sp", OF[tok0:tok0 + 128, :], ost[b][:], r=[f"retost{b}h{h}" for h in range(4)], w=bkeys("OF", tok0, tok0 + 128))
                        continue
                    ssk = [f"retss{h}" for h in range(4)]
                    P.op("act", lambda e: e.activation(out=ss[:], in_=ss[:], func=AF.Sqrt, scale=1.0 / 512, bias=self.epsc[:]), r=ssk + ["epsc"], w=["retss"])
                    P.op("dve", lambda e: e.reciprocal(out=ss[:], in_=ss[:]), r=["retss"], w=["retss"])
                    for h in range(4):
                        P.op("dve", lambda e, h=h: e.tensor_scalar(out=oft[:, h * 512:(h + 1) * 512], in0=oft[:, h * 512:(h + 1) * 512], scalar1=ss[:, h:h + 1], scalar2=None, op0=ALU.mult), r=[f"retofs{h}", "retss"], w=[f"retofs{h}"])
                    ofk = [f"retofs{h}" for h in range(4)]
                    P.op("pool", lambda e: e.tensor_tensor(out=gt[:], in0=gt[:], in1=ng[:], op=ALU.mult), r=["retgt", "retng"], w=["retgt"])
                    P.op("pool", lambda e: e.tensor_tensor(out=oft[:], in0=oft[:], in1=gt[:], op=ALU.mult), r=ofk + ["retgt"], w=["retoft"] + ofk)
                    for eg in range(4):
                        pst = self.psum[eg % 2]

                        def tro(e, pst=pst, eg=eg):
                            ins = None
                            for e4 in range(4):
                                ec = 4 * eg + e4
                                ins = e.transpose(out=pst[:, e4 * 128:(e4 + 1) * 128], in_=oft[:, ec * 128:(ec + 1) * 128], identity=ident[:])
                            return ins

                        P.op("pe", tro, r=["retoft", "ident"], w=[f"ps{eg % 2}"])
                        P.op("act", lambda e, pst=pst, eg=eg: e.activation(out=mT[:, 4 * eg:4 * eg + 4, :].rearrange("p a t -> p (a t)"), in_=pst[:, :512], func=AF.Copy), r=[f"ps{eg % 2}"], w=[f"retmT{eg}"])
                    seg = 0 if blk < 2 else 1
                    t0 = tok0 if seg == 0 else tok0 - LC
                    xv = xcv if seg == 0 else xlv
                    xname = "XC" if seg == 0 else "XL"
                    s_ = 1 if seg == 0 else 0
                    P.dma("sp", xt[:], xv[:, :, t0:t0 + 128], r=bkeys(xname, t0, t0 + 128), w=["retxt"])
                    for oc in range(8):
                        psP = self.psum[2 + oc % 2]

                        def mmP(e, psP=psP, oc=oc):
                            ins = None
                            for ec in range(16):
                                ins = e.matmul(psP[:, 0:128], lhsT=wo[:, ec, oc * 128:(oc + 1) * 128], rhs=mT[:, ec, :], start=(ec == 0), stop=(ec == 15))
                            return ins

                        P.op("pe", mmP, r=["retwo0", "retwo1"] + [f"retmT{eg}" for eg in range(4)], w=[f"ps{2 + oc % 2}"])
                        P.op("dve", lambda e, psP=psP, oc=oc, s_=s_: e.scalar_tensor_tensor(out=xt[:, oc, :], in0=psP[:, 0:128], scalar=self.modv[:, 16 + oc, s_:s_ + 1], in1=xt[:, oc, :], op0=ALU.mult, op1=ALU.add), r=[f"ps{2 + oc % 2}", "modv", "retxt"], w=["retxt"])
                    P.dma("sp", xv[:, :, t0:t0 + 128], xt[:], r=["retxt"], w=bkeys(xname, t0, t0 + 128))
            P.barrier()

    def declare_gdn(self):
        self.din("gdn_w_in", [D, 6208])
        self.din("gdn_w_out", [2048, D])
        self.din("i_gdn_cw", [128, 32, 4])
        self.din("gdn_ng", [128, 128])
        self.din("gdn_ab", [128, 2, 2, 16])
        self.din("gdn_c64", [64, 4, 64])
        self.din("gdn_ones", [64, 128])
        if "ident" not in self.dram:
            self.din("ident", [128, 128])
        self.dscr("GK", [NT, 1024])
        self.dscr("GV", [NT, 2048])
        self.dscr("GQ", [NT, 1024])
        self.dscr("GZ", [NT, 2048])
        self.dscr("GGB", [NT, 64])
        self.dscr("GOF", [NT, 2048])
        self.dscr("GNM", [2176, 64, 64])
        self.dscr("GTM", [2176, 64, 64], BF16)

    def gdn_mixer(self, li):
        P = self.P
        GK, GV, GQ, GZ, GGB, GOF, GNM, GTM = (self.dram[k] for k in ("GK", "GV", "GQ", "GZ", "GGB", "GOF", "GNM", "GTM"))
        win = self.dram["gdn_w_in"].rearrange("(kc p) n -> p kc n", p=128)
        ident = self.sb("identg", [128, 128], F32)
        P.dma("sp", ident[:], self.dram["ident"], w=["identg"])
        onec = self.sb("gdnone", [128, 1], F32)
        P.op("dve", lambda e: e.memset(onec[:], 1.0), w=["gdnone"])
        eps6 = self.sb("gdneps6", [128, 1], F32)
        P.op("dve", lambda e: e.memset(eps6[:], 1e-6), w=["gdneps6"])
        with ExitStack() as hst:
            self.alloc_ht(hst)
            HT = self.HT
            self.norm_phase(self.A1, 0, ctx=True)
            with ExitStack() as st:
                cw = self.sb("gdncw", [128, 32, 4], F32, st)
                P.dma("sp", cw[:], self.dram["i_gdn_cw"], w=["gdncw"])
                wb = [self.sb(f"gdnw{i}", [128, 8, 512], BF16, st) for i in range(2)]
                xbl = self.sb("gdnxbl", [128, L + 3], F32, st)
                xbc = self.sb("gdnxbc", [128, LC + 3], F32, st)
                xsl = self.sb("gdnxsl", [128, L], F32, st)
                xsc = self.sb("gdnxsc", [128, LC], F32, st)
                tst = [self.sb(f"gdntst{i}", [128, 4, 128], F32, st) for i in range(2)]
                ssq = self.sb("gdnssq", [128, 4], F32, st)
                P.op("dve", lambda e: e.memset(ssq[:], 1.0), w=["gdnssq"])
                junk = self.sb("gdnjunk", [128, 128], F32, st)
                for bufx, nn in ((xbl, L), (xbc, LC)):
                    P.op("dve", lambda e, bufx=bufx: e.memset(bufx[:, 0:2], 0.0), w=["gdnxb"])
                    P.op("dve", lambda e, bufx=bufx, nn=nn: e.memset(bufx[:, nn + 2:nn + 3], 0.0), w=["gdnxb"])
                cnt = 0
                tcnt = 0
                for g in range(8):
                    gb = g % 2
                    col0 = g * 512 if g < 6 else 3136 + (g - 6) * 512
                    P.dma("pool", wb[gb][:], win[:, :, col0:col0 + 512], w=[f"gdnw{gb}"])
                    for ol in range(4):
                        fc = 4 * g + ol
                        kind = "k" if fc < 8 else ("v" if fc < 24 else "q")
                        use_ctx = kind != "q"
                        dst, dcol = (GK, fc * 128) if kind == "k" else ((GV, (fc - 8) * 128) if kind == "v" else (GQ, (fc - 24) * 128))
                        dkey = {"k": "GK", "v": "GV", "q": "GQ"}[kind]
                        for (seg, t0, n) in self.tiles(ctx=use_ctx):
                            tok0 = t0 if seg == 0 else LC + t0
                            c0 = ht_col(tok0)
                            b = cnt % 2
                            cnt += 1
                            ps = self.psum[b]

                            def mm(e, ps=ps, gb=gb, ol=ol, c0=c0, n=n):
                                ins = None
                                for kc in range(8):
                                    ins = e.matmul(ps[:, :n], lhsT=wb[gb][:, kc, ol * 128:(ol + 1) * 128], rhs=HT[:, kc, c0:c0 + n], start=(kc == 0), stop=(kc == 7))
                                return ins

                            P.op("pe", mm, r=[f"gdnw{gb}"] + bkeys("HT", c0, c0 + n), w=[f"ps{b}"])
                            dstb = xbc[:, 2 + t0:2 + t0 + n] if seg == 0 else xbl[:, 2 + t0:2 + t0 + n]
                            P.op("act", lambda e, ps=ps, dstb=dstb, n=n: e.activation(out=dstb, in_=ps[:, :n], func=AF.Copy), r=[f"ps{b}"], w=["gdnxb"])
                        segs = ((xbc, xsc, LC, 0), (xbl, xsl, L, LC)) if use_ctx else ((xbl, xsl, L, LC),)
                        for (bufx, xs, nn, tokb) in segs:
                            o = xs[:, :nn]
                            P.op("dve", lambda e, o=o, bufx=bufx, nn=nn, fc=fc: e.tensor_scalar(out=o, in0=bufx[:, 0:nn], scalar1=cw[:, fc, 0:1], scalar2=None, op0=ALU.mult), r=["gdnxb", "gdncw"], w=["gdnxs"])
                            for kk in range(1, 4):
                                P.op("dve", lambda e, o=o, bufx=bufx, nn=nn, fc=fc, kk=kk: e.scalar_tensor_tensor(out=o, in0=bufx[:, kk:kk + nn], scalar=cw[:, fc, kk:kk + 1], in1=o, op0=ALU.mult, op1=ALU.add), r=["gdnxb", "gdncw", "gdnxs"], w=["gdnxs"])
                            P.op("act", lambda e, o=o: e.activation(out=o, in_=o, func=AF.Silu), r=["gdnxs"], w=["gdnxs"])
                            for q4 in range(nn // 512 if nn >= 512 else 1):
                                nb = 4 if nn >= 512 else nn // 128
                                tb = tcnt % 2
                                tcnt += 1
                                pst = self.psum[2 + tb]

                                def tr(e, pst=pst, xs=xs, q4=q4, nb=nb):
                                    ins = None
                                    for i4 in range(nb):
                                        ins = e.transpose(out=pst[:, i4 * 128:(i4 + 1) * 128], in_=xs[:, q4 * 512 + i4 * 128:q4 * 512 + (i4 + 1) * 128], identity=ident[:])
                                    return ins

                                P.op("pe", tr, r=["gdnxs", "identg"], w=[f"ps{2 + tb}"])
                                tk = f"gdntst{tb}"
                                if kind == "v":
                                    P.op("act", lambda e, pst=pst, tb=tb, nb=nb: e.activation(out=tst[tb][:, :nb, :].rearrange("p a d -> p (a d)"), in_=pst[:, :nb * 128], func=AF.Copy), r=[f"ps{2 + tb}"], w=[tk])
                                else:
                                    for i4 in range(nb):
                                        P.op("act", lambda e, pst=pst, i4=i4: e.activation(out=junk[:], in_=pst[:, i4 * 128:(i4 + 1) * 128], func=AF.Square, accum_out=ssq[:, i4:i4 + 1]), r=[f"ps{2 + tb}"], w=["gdnjunk", "gdnssq"])
                                    P.op("act", lambda e: e.activation(out=ssq[:], in_=ssq[:], func=AF.Sqrt, bias=eps6[:]), r=["gdnssq", "gdneps6"], w=["gdnssq"])
                                    P.op("dve", lambda e: e.reciprocal(out=ssq[:], in_=ssq[:]), r=["gdnssq"], w=["gdnssq"])
                                    sc2 = 1.0 if kind == "k" else 128.0 ** -0.5
                                    for i4 in range(nb):
                                        P.op("dve", lambda e, pst=pst, tb=tb, i4=i4, sc2=sc2: e.tensor_scalar(out=tst[tb][:, i4, :], in0=pst[:, i4 * 128:(i4 + 1) * 128], scalar1=ssq[:, i4:i4 + 1], scalar2=sc2, op0=ALU.mult, op1=ALU.mult), r=[f"ps{2 + tb}", "gdnssq"], w=[tk])
                                r0 = tokb + q4 * 512
                                P.dma("sp", dst[r0:r0 + nb * 128, dcol:dcol + 128].rearrange("(a p) c -> p a c", p=128), tst[tb][:, :nb, :], r=[tk], w=bkeys(dkey, r0, r0 + nb * 128))
                P.barrier()
            with ExitStack() as st:
                wz = [self.sb(f"gdnwz{i}", [128, 8, 512], BF16, st) for i in range(2)]
                wba = self.sb("gdnwba", [128, 8, 64], BF16, st)
                og = [self.sb(f"gdnog{i}", [128, 512], F32, st) for i in range(2)]
                ab = self.sb("gdnab", [128, 2, 2, 16], F32, st)
                negA = self.sb("gdnnegA", [128, 2, 16], F32, st)
                P.dma("sp", ab[:], self.dram["gdn_ab"], w=["gdnab"])
                P.op("act", lambda e: e.activation(out=negA[:], in_=ab[:, 0], func=AF.Exp), r=["gdnab"], w=["gdnnegA"])
                P.op("dve", lambda e: e.tensor_scalar(out=negA[:], in0=negA[:], scalar1=-1.0, scalar2=None, op0=ALU.mult), r=["gdnnegA"], w=["gdnnegA"])
                P.dma("pool", wba[:], win[:, :, 3072:3136], w=["gdnwba"])
                gt_ = {k: self.sb("gdng" + k, [128, 2, 16], F32, st) for k in ("x", "nx", "e", "o")}
                gbo = [self.sb(f"gdngbo{i}", [128, 64], F32, st) for i in range(2)]
                cnt = 0
                for blk in range(NT // 128):
                    tok0 = blk * 128
                    c0 = ht_col(tok0)
                    b = cnt % 2
                    cnt += 1
                    ps = self.psum[b]

                    def mmb(e, ps=ps, c0=c0):
                        ins = None
                        for kc in range(8):
                            ins = e.matmul(ps[:, :64], lhsT=HT[:, kc, c0:c0 + 128], rhs=wba[:, kc, :], start=(kc == 0), stop=(kc == 7))
                        return ins

                    P.op("pe", mmb, r=["gdnwba"] + bkeys("HT", c0, c0 + 128), w=[f"ps{b}"])
                    ps4 = ps[:, :64].rearrange("p (d s h) -> p d s h", d=2, s=2)
                    gb4 = gbo[b][:].rearrange("p (d s h) -> p d s h", d=2, s=2)
                    kq = "gdngt"
                    x_, nx_, e_, o_ = (gt_[k][:] for k in ("x", "nx", "e", "o"))
                    P.op("act", lambda e, ps4=ps4, gb4=gb4: e.activation(out=gb4[:, :, 0, :], in_=ps4[:, :, 0, :], func=AF.Sigmoid), r=[f"ps{b}"], w=[f"gdngbo{b}"])
                    P.op("dve", lambda e, ps4=ps4, x_=x_: e.tensor_tensor(out=x_, in0=ps4[:, :, 1, :], in1=ab[:, 1], op=ALU.add), r=[f"ps{b}", "gdnab", kq], w=[kq])
                    P.op("dve", lambda e, x_=x_, nx_=nx_: e.tensor_scalar(out=nx_, in0=x_, scalar1=-1.0, scalar2=None, op0=ALU.mult), r=[kq], w=[kq])
                    P.op("dve", lambda e, x_=x_, nx_=nx_: e.tensor_tensor(out=nx_, in0=nx_, in1=x_, op=ALU.max), r=[kq], w=[kq])
                    P.op("act", lambda e, nx_=nx_, e_=e_: e.activation(out=e_, in_=nx_, func=AF.Exp, scale=-1.0), r=[kq], w=[kq])
                    P.op("act", lambda e, e_=e_: e.activation(out=e_, in_=e_, func=AF.Ln, bias=onec[:]), r=[kq, "gdnone"], w=[kq])
                    P.op("dve", lambda e, x_=x_: e.tensor_scalar(out=x_, in0=x_, scalar1=0.0, scalar2=None, op0=ALU.max), r=[kq], w=[kq])
                    P.op("dve", lambda e, x_=x_, e_=e_: e.tensor_tensor(out=x_, in0=x_, in1=e_, op=ALU.add), r=[kq], w=[kq])
                    P.op("dve", lambda e, x_=x_, gb4=gb4: e.tensor_tensor(out=gb4[:, :, 1, :], in0=x_, in1=negA[:], op=ALU.mult), r=[kq, "gdnnegA", f"gdngbo{b}"], w=[f"gdngbo{b}"])
                    P.dma("sp", GGB[tok0:tok0 + 128, :], gbo[b][:], r=[f"gdngbo{b}"], w=bkeys("GGB", tok0, tok0 + 128))
                cnt = 0
                for gi in range(4):
                    gb = gi % 2
                    P.dma("pool", wz[gb][:], win[:, :, 4160 + gi * 512:4160 + (gi + 1) * 512], w=[f"gdnwz{gb}"])
                    for blk in range(2, NT // 128):
                        tok0 = blk * 128
                        c0 = ht_col(tok0)
                        b = cnt % 2
                        cnt += 1
                        ps = self.psum[2 + b]

                        def mmz(e, ps=ps, gb=gb, c0=c0):
                            ins = None
                            for kc in range(8):
                                ins = e.matmul(ps[:, :512], lhsT=HT[:, kc, c0:c0 + 128], rhs=wz[gb][:, kc, :], start=(kc == 0), stop=(kc == 7))
                            return ins

                        P.op("pe", mmz, r=[f"gdnwz{gb}"] + bkeys("HT", c0, c0 + 128), w=[f"ps{2 + b}"])
                        P.op("act", lambda e, ps=ps, b=b: e.activation(out=og[b][:], in_=ps[:, :512], func=AF.Silu), r=[f"ps{2 + b}"], w=[f"gdnog{b}"])
                        P.dma("sp", GZ[tok0:tok0 + 128, gi * 512:(gi + 1) * 512], og[b][:], r=[f"gdnog{b}"], w=bkeys("GZ", tok0, tok0 + 128))
                P.barrier()
        if self.cfg.get("gdn_stop") == 1:
            return
        B2 = ["ps2"] + [f"ps2wT{i}" for i in range(4)]
        B3 = ["ps3"] + [f"ps3S{i}" for i in range(4)]
        BK = {2: B2, 3: B3}

        def chunk_order(d):
            return list(range(68)) if d == 0 else [3, 2, 1, 0] + list(range(67, 3, -1))

        rltmp = self.sb("gdnrltmp", [64, 2048], F32)

        def rows_load(dst, src, tok0, ncols, d, rkeys, wkey):
            if d == 0:
                P.dma("sp", dst, src[tok0:tok0 + 64], r=rkeys, w=[wkey])
                return
            P.dma("sp", rltmp[:, :ncols], src[tok0:tok0 + 64], r=rkeys, w=["gdnrltmp"])
            for c0 in range(0, ncols, 512):
                n = min(512, ncols - c0)
                P.op("pe", lambda e, c0=c0, n=n: e.matmul(self.psum[0][0:64, :n], lhsT=c64[:, 3, :], rhs=rltmp[:, c0:c0 + n], start=True, stop=True), r=["gdnrltmp", "gdnc64"], w=["ps0"])
                P.op("act", lambda e, c0=c0, n=n, dst=dst: e.activation(out=dst[:, c0:c0 + n], in_=self.psum[0][0:64, :n], func=AF.Copy), r=["ps0"], w=[wkey])

        c64 = self.sb("gdnc64", [64, 4, 64], F32)
        P.dma("sp", c64[:], self.dram["gdn_c64"], w=["gdnc64"])
        ones64 = self.sb("gdnones", [64, 128], F32)
        P.dma("sp", ones64[:], self.dram["gdn_ones"], w=["gdnones"])
        LTm, mLs, mU = c64[:, 0, :], c64[:, 1, :], c64[:, 2, :]

        def gates(st_bufs, gbt, d, bkey, tag):
            G3, gcum = st_bufs
            graw = gbt[:, d * 32 + 16:d * 32 + 32]
            P.op("dve", lambda e: e.tensor_tensor(out=G3[:], in0=LTm.unsqueeze(1).to_broadcast([64, 16, 64]), in1=graw.unsqueeze(2).to_broadcast([64, 16, 64]), op=ALU.mult), r=[bkey, "gdnc64"], w=[tag + "G3"])
            psg = self.psum[1]
            P.op("pe", lambda e: e.matmul(psg[0:64, 0:16], lhsT=LTm, rhs=graw, start=True, stop=True), r=[bkey, "gdnc64"], w=["ps1"])
            P.op("act", lambda e: e.activation(out=gcum[:], in_=psg[0:64, 0:16], func=AF.Copy), r=["ps1"], w=[tag + "gcum"])
            G3f = G3[:].rearrange("p h i -> p (h i)")

            def mmD(e):
                e.matmul(self.psum[2][0:64, :512], lhsT=ones64[:, 0:64], rhs=G3f[:, 0:512], start=True, stop=True)
                return e.matmul(self.psum[3][0:64, :512], lhsT=ones64[:, 0:64], rhs=G3f[:, 512:1024], start=True, stop=True)

            P.op("pe", mmD, r=[tag + "G3", "gdnones"], w=B2 + B3)

        def decay_from(out3, gcum, transposed, mask, tag, wkey):
            for half in range(2):
                psd = self.psum[2 + half][0:64, :512].rearrange("p (h i) -> p h i", i=64)
                gb_ = gcum[:, half * 8:(half + 1) * 8].unsqueeze(2).to_broadcast([64, 8, 64])
                o = out3[:, half * 8:(half + 1) * 8, :]
                if transposed:
                    P.op("dve", lambda e, o=o, psd=psd, gb_=gb_: e.tensor_tensor(out=o, in0=psd, in1=gb_, op=ALU.subtract), r=BK[2 + half] + [tag + "gcum"], w=[wkey])
                else:
                    P.op("dve", lambda e, o=o, psd=psd, gb_=gb_: e.tensor_tensor(out=o, in0=gb_, in1=psd, op=ALU.subtract), r=BK[2 + half] + [tag + "gcum"], w=[wkey])
            P.op("dve", lambda e: e.tensor_scalar(out=out3, in0=out3, scalar1=0.0, scalar2=None, op0=ALU.min), r=[wkey], w=[wkey])
            P.op("act", lambda e: e.activation(out=out3, in_=out3, func=AF.Exp), r=[wkey], w=[wkey])
            P.op("pool", lambda e: e.tensor_tensor(out=out3, in0=out3, in1=mask.unsqueeze(1).to_broadcast([64, 16, 64]), op=ALU.mult), r=[wkey, "gdnc64"], w=[wkey])

        with ExitStack() as st:
            kA_ = [self.sb(f"gdnAk{i}", [64, 1024], F32, st) for i in range(2)]
            gbA_ = [self.sb(f"gdnAgb{i}", [64, 64], F32, st) for i in range(2)]
            G3A = self.sb("gdnAG3A", [64, 16, 64], F32, st)
            gcumA = self.sb("gdnAgcumA", [64, 16], F32, st)
            kTcA = self.sb("gdnAkT", [128, 8, 64], F32, st)
            KKs = self.sb("gdnAKK", [64, 8, 64], F32, st)
            Nm = [self.sb(f"gdnAN{i}", [64, 16, 64], F32, st) for i in range(2)]
            for d in range(2):
                for ci, ch in enumerate(chunk_order(d)):
                    tok0 = ch * 64
                    b = ci % 2
                    rows_load(kA_[b][:], GK, tok0, 1024, d, bkeys("GK", tok0, tok0 + 64), f"gdnAk{b}")
                    rows_load(gbA_[b][:], GGB, tok0, 64, d, bkeys("GGB", tok0, tok0 + 64), f"gdnAgb{b}")
                    gates((G3A, gcumA), gbA_[b], d, f"gdnAgb{b}", "A")
                    decay_from(Nm[b][:], gcumA, False, mLs, "A", f"gdnAN{b}")

                    def trk(e, b=b):
                        ins = None
                        for kh in range(8):
                            ins = e.transpose(out=self.psum[0][:, kh * 64:(kh + 1) * 64], in_=kA_[b][:, kh * 128:(kh + 1) * 128], identity=ident[0:64, 0:64])
                        return ins

                    P.op("pe", trk, r=[f"gdnAk{b}", "identg"], w=["ps0"])
                    P.op("act", lambda e: e.activation(out=kTcA[:].rearrange("p a t -> p (a t)"), in_=self.psum[0][:, :512], func=AF.Copy), r=["ps0"], w=["gdnAkT"])

                    def mmKK(e):
                        ins = None
                        for kh in range(8):
                            ins = e.matmul(self.psum[4][0:64, kh * 64:(kh + 1) * 64], lhsT=kTcA[:, kh, :], rhs=kTcA[:, kh, :], start=True, stop=True)
                        return ins

                    P.op("pe", mmKK, r=["gdnAkT"], w=["ps4"])
                    P.op("act", lambda e: e.activation(out=KKs[:].rearrange("p a t -> p (a t)"), in_=self.psum[4][0:64, :512], func=AF.Copy), r=["ps4"], w=["gdnAKK"])
                    N4 = Nm[b][:].rearrange("p (a r) j -> p a r j", r=2)
                    P.op("dve", lambda e, N4=N4: e.tensor_tensor(out=N4, in0=N4, in1=KKs[:].unsqueeze(2).to_broadcast([64, 8, 2, 64]), op=ALU.mult), r=[f"gdnAN{b}", "gdnAKK"], w=[f"gdnAN{b}"])
                    beta = gbA_[b][:, d * 32:d * 32 + 16]
                    P.op("dve", lambda e, b=b, beta=beta: e.tensor_tensor(out=Nm[b][:], in0=Nm[b][:], in1=beta.unsqueeze(2).to_broadcast([64, 16, 64]), op=ALU.mult), r=[f"gdnAN{b}", f"gdnAgb{b}"], w=[f"gdnAN{b}"])
                    pid0 = (d * 68 + ci) * 16
                    P.dma("sp", GNM[pid0:pid0 + 16].rearrange("h i j -> i h j"), Nm[b][:], r=[f"gdnAN{b}"], w=bkeys("GNM", pid0, pid0 + 16))
            P.barrier()
        if self.cfg.get("gdn_stop") == 2:
            return
        with ExitStack() as st:
            Nb = self.sb("gdnBN", [128, 64, 64], F32, st)
            X = self.sb("gdnBX", [128, 64, 64], F32, st)
            pr = self.sb("gdnBpr", [128, 64, 64], F32, st)
            XT = self.sb("gdnBXT", [128, 64, 64], BF16, st)
            for bt in range(17):
                P.dma("sp", Nb[:], GNM[bt * 128:(bt + 1) * 128], r=bkeys("GNM", bt * 128, (bt + 1) * 128), w=["gdnBN"])
                P.op("pool", lambda e: e.memset(X[:], 0.0), w=["gdnBX"])
                P.op("pool", lambda e: e.memset(X[:].rearrange("p a b -> p (a b)")[:, 0:4096:65], 1.0), w=["gdnBX"])
                for i in range(1, 64):
                    P.op("dve", lambda e, i=i: e.tensor_tensor(out=pr[:, 0:i, 0:i], in0=X[:, 0:i, 0:i].rearrange("p m j -> p j m"), in1=Nb[:, i, 0:i].unsqueeze(1).to_broadcast([128, i, i]), op=ALU.mult), r=["gdnBN", "gdnBX"], w=["gdnBpr"])
                    P.op("dve", lambda e, i=i: e.tensor_reduce(out=X[:, i, 0:i], in_=pr[:, 0:i, 0:i], axis=AX.X, op=ALU.add, negate=True), r=["gdnBpr"], w=["gdnBX"])
                P.op("pool", lambda e: e.tensor_copy(out=XT[:], in_=X[:].rearrange("p i j -> p j i")), r=["gdnBX"], w=["gdnBXT"])
                P.dma("sp", GTM[bt * 128:(bt + 1) * 128], XT[:], r=["gdnBXT"], w=bkeys("GTM", bt * 128, (bt + 1) * 128))
            P.barrier()
        if self.cfg.get("gdn_stop") == 3:
            return
        xlv = self.dram["XL"].rearrange("(kc p) t -> p kc t", p=128)
        with ExitStack() as st:
            wo = self.sb("gdnwo", [128, 16, D], BF16, st)
            wsrc = self.dram["gdn_w_out"].rearrange("(ec p) n -> p ec n", p=128)
            for hh in range(2):
                P.dma("pool", wo[:, hh * 8:(hh + 1) * 8, :], wsrc[:, hh * 8:(hh + 1) * 8, :], w=[f"gdnwo{hh}"])
            ngr = self.sb("gdnng", [64, 128], F32, st)
            P.dma("sp", ngr[:], self.dram["gdn_ng"][0:64, :], w=["gdnng"])
            S = self.sb("gdnS", [128, 16, 128], F32, st)
            kt_ = [self.sb(f"gdnCk{i}", [64, 1024], F32, st) for i in range(2)]
            vt_ = [self.sb(f"gdnCv{i}", [64, 2048], F32, st) for i in range(2)]
            qt_ = [self.sb(f"gdnCq{i}", [64, 1024], F32, st) for i in range(2)]
            gbt_ = [self.sb(f"gdnCgb{i}", [64, 64], F32, st) for i in range(2)]
            TT = [self.sb(f"gdnCT{i}", [64, 16, 64], BF16, st) for i in range(2)]
            zt_ = self.sb("gdnCz", [64, 2048], F32, st)
            oft = self.sb("gdnCof", [64, 2048], F32, st)
            G3 = self.sb("gdnCG3", [64, 16, 64], F32, st)
            gcum = self.sb("gdnCgcum", [64, 16], F32, st)
            dT = self.sb("gdnCdT", [64, 16, 64], F32, st)
            eg = self.sb("gdnCeg", [64, 16], F32, st)
            cf = self.sb("gdnCcf", [64, 16], F32, st)
            kd = self.sb("gdnCkd", [64, 16], F32, st)
            egl = self.sb("gdnCegl", [128, 16], F32, st)
            kbg = self.sb("gdnCkbg", [64, 16, 128], BF16, st)
            vb = self.sb("gdnCvb", [64, 16, 128], BF16, st)
            kdc = self.sb("gdnCkdc", [64, 16, 128], BF16, st)
            qeg = self.sb("gdnCqeg", [64, 16, 128], F32, st)
            kTc = self.sb("gdnCkT", [128, 8, 64], BF16, st)
            qTc = self.sb("gdnCqT", [128, 8, 64], BF16, st)
            qegT = self.sb("gdnCqegT", [128, 16, 64], BF16, st)
            S16 = self.sb("gdnS16", [128, 16, 128], BF16, st)
            usb = [self.sb(f"gdnCu{i}", [64, 128], F32, st) for i in range(4)]
            wTs = [self.sb(f"gdnCwT{i}", [128, 64], BF16, st) for i in range(4)]
            vn = [self.sb(f"gdnCvn{i}", [64, 128], BF16, st) for i in range(4)]
            am = [self.sb(f"gdnCam{i}", [64, 64], BF16, st) for i in range(4)]
            ost = [self.sb(f"gdnCo{i}", [64, 16, 128], F32, st) for i in range(2)]
            ssn = self.sb("gdnCss", [64, 16], F32, st)
            sq = self.sb("gdnCsq", [64, 16, 128], F32, st)
            mT = self.sb("gdnCmT", [128, 16, 64], BF16, st)
            xt = self.sb("gdnCxt", [128, 8, 64], F32, st)
            for d in range(2):
                P.op("dve", lambda e: e.memset(S[:], 0.0), w=[f"gdnS{hv}" for hv in range(16)])
                P.op("pool", lambda e: e.memset(S16[:], 0.0), w=[f"gdnS{hv}b" for hv in range(16)])
                hc = 0
                for ci, ch in enumerate(chunk_order(d)):
                    tok0 = ch * 64
                    is_lat = ch >= 4
                    b = ci % 2
                    rows_load(kt_[b][:], GK, tok0, 1024, d, bkeys("GK", tok0, tok0 + 64), f"gdnCk{b}")
                    rows_load(vt_[b][:], GV, tok0, 2048, d, bkeys("GV", tok0, tok0 + 64), f"gdnCv{b}")
                    rows_load(gbt_[b][:], GGB, tok0, 64, d, bkeys("GGB", tok0, tok0 + 64), f"gdnCgb{b}")
                    pid0 = (d * 68 + ci) * 16
                    P.dma("sp", TT[b][:], GTM[pid0:pid0 + 16].rearrange("h j i -> j h i"), r=bkeys("GTM", pid0, pid0 + 16), w=[f"gdnCT{b}"])
                    if is_lat:
                        rows_load(qt_[b][:], GQ, tok0, 1024, d, bkeys("GQ", tok0, tok0 + 64), f"gdnCq{b}")
                        if d == 1:
                            rows_load(zt_[:], GZ, tok0, 2048, d, bkeys("GZ", tok0, tok0 + 64), "gdnCz")
                            rows_load(oft[:], GOF, tok0, 2048, d, bkeys("GOF", tok0, tok0 + 64), "gdnCof")
                    gbk = f"gdnCgb{b}"
                    gates((G3, gcum), gbt_[b], d, gbk, "C")
                    if is_lat:
                        decay_from(dT[:], gcum, True, mU, "C", "gdnCdT")
                    graw = gbt_[b][:, d * 32 + 16:d * 32 + 32]
                    beta = gbt_[b][:, d * 32:d * 32 + 16]
                    P.op("pe", lambda e, graw=graw: e.matmul(self.psum[1][:, 16:32], lhsT=ones64[:, :], rhs=graw, start=True, stop=True), r=[gbk, "gdnones"], w=["ps1"])
                    P.op("act", lambda e: e.activation(out=egl[:], in_=self.psum[1][:, 16:32], func=AF.Exp), r=["ps1"], w=["gdnCegl"])
                    P.op("dve", lambda e: e.tensor_tensor(out=kd[:], in0=self.psum[1][0:64, 16:32], in1=gcum[:], op=ALU.subtract), r=["ps1", "Cgcum"], w=["gdnCkd"])
                    P.op("act", lambda e: e.activation(out=kd[:], in_=kd[:], func=AF.Exp), r=["gdnCkd"], w=["gdnCkd"])
                    P.op("act", lambda e: e.activation(out=eg[:], in_=gcum[:], func=AF.Exp), r=["Cgcum"], w=["gdnCeg"])
                    P.op("dve", lambda e, beta=beta: e.tensor_tensor(out=cf[:], in0=eg[:], in1=beta, op=ALU.mult), r=["gdnCeg", gbk], w=["gdnCcf"])
                    k4 = kt_[b][:].rearrange("p (a c) -> p a c", c=128).unsqueeze(2).to_broadcast([64, 8, 2, 128])

                    def sc4(t16):
                        return t16.rearrange("p (a r) -> p a r", r=2).unsqueeze(3).to_broadcast([64, 8, 2, 128])

                    P.op("pool", lambda e, k4=k4: e.tensor_tensor(out=kbg[:].rearrange("p (a r) c -> p a r c", r=2), in0=k4, in1=sc4(cf[:]), op=ALU.mult), r=[f"gdnCk{b}", "gdnCcf"], w=["gdnCkbg"])
                    P.op("pool", lambda e, k4=k4: e.tensor_tensor(out=kdc[:].rearrange("p (a r) c -> p a r c", r=2), in0=k4, in1=sc4(kd[:]), op=ALU.mult), r=[f"gdnCk{b}", "gdnCkd"], w=["gdnCkdc"])
                    P.op("dve", lambda e, b=b, beta=beta: e.tensor_tensor(out=vb[:], in0=vt_[b][:].rearrange("p (h c) -> p h c", c=128), in1=beta.unsqueeze(2).to_broadcast([64, 16, 128]), op=ALU.mult), r=[f"gdnCv{b}", gbk], w=["gdnCvb"])
                    if is_lat:
                        def trkq(e, b=b):
                            ins = None
                            for kh in range(8):
                                ins = e.transpose(out=self.psum[0][:, kh * 64:(kh + 1) * 64], in_=kt_[b][:, kh * 128:(kh + 1) * 128], identity=ident[0:64, 0:64])
                            return ins

                        P.op("pe", trkq, r=[f"gdnCk{b}", "identg"], w=["ps0"])
                        P.op("act", lambda e: e.activation(out=kTc[:].rearrange("p a t -> p (a t)"), in_=self.psum[0][:, :512], func=AF.Copy), r=["ps0"], w=["gdnCkT"])

                        def trq(e, b=b):
                            ins = None
                            for kh in range(8):
                                ins = e.transpose(out=self.psum[0][:, kh * 64:(kh + 1) * 64], in_=qt_[b][:, kh * 128:(kh + 1) * 128], identity=ident[0:64, 0:64])
                            return ins

                        P.op("pe", trq, r=[f"gdnCq{b}", "identg"], w=["ps0"])
                        P.op("act", lambda e: e.activation(out=qTc[:].rearrange("p a t -> p (a t)"), in_=self.psum[0][:, :512], func=AF.Copy), r=["ps0"], w=["gdnCqT"])
                        q4 = qt_[b][:].rearrange("p (a c) -> p a c", c=128).unsqueeze(2).to_broadcast([64, 8, 2, 128])
                        P.op("pool", lambda e, q4=q4: e.tensor_tensor(out=qeg[:].rearrange("p (a r) c -> p a r c", r=2), in0=q4, in1=sc4(eg[:]), op=ALU.mult), r=[f"gdnCq{b}", "gdnCeg"], w=["gdnCqeg"])
                        for half in range(2):
                            def trqe(e, half=half):
                                ins = None
                                for h8 in range(8):
                                    hv = half * 8 + h8
                                    ins = e.transpose(out=self.psum[2 + half][:, h8 * 64:(h8 + 1) * 64], in_=qeg[:, hv, :], identity=ident[0:64, 0:64])
                                return ins

                            P.op("pe", trqe, r=["gdnCqeg", "identg", "gdnCdT"], w=BK[2 + half])
                            P.op("act", lambda e, half=half: e.activation(out=qegT[:, half * 8:(half + 1) * 8, :].rearrange("p a t -> p (a t)"), in_=self.psum[2 + half][:, :512], func=AF.Copy), r=BK[2 + half], w=[f"gdnCqegT{half}"])
                    GG = Stager(P, enabled=self.cfg.get("stage_gdn", False))
                    for hv in range(16):
                        if hv % 4 == 0 and hv:
                            GG.flush()
                        kh = hv // 2
                        pb = hc % 4
                        hc += 1
                        psA = self.psum[4 + pb]
                        psWT = self.psum[2][:, pb * 64:(pb + 1) * 64]
                        psSN = self.psum[3][:, pb * 128:(pb + 1) * 128]
                        kU, kW, kO, kAt, kWT, kSn = (f"ps{4 + pb}u", f"ps{4 + pb}w", f"ps{4 + pb}o", f"ps{4 + pb}a", f"ps2wT{pb}", f"ps3S{pb}")
                        Sk = f"gdnS{hv}"
                        GG.op(0, "pe", lambda e, psA=psA, b=b, hv=hv: e.matmul(psA[0:64, 0:128], lhsT=TT[b][:, hv, :], rhs=vb[:, hv, :], start=True, stop=True), r=[f"gdnCT{b}", "gdnCvb"], w=[kU])
                        GG.op(0, "pe", lambda e, psWT=psWT, b=b, hv=hv: e.matmul(psWT, lhsT=kbg[:, hv, :], rhs=TT[b][:, hv, :], start=True, stop=True), r=[f"gdnCT{b}", "gdnCkbg", "ps2"], w=[kWT])
                        GG.op(1, "act", lambda e, psA=psA, pb=pb: e.activation(out=usb[pb][:], in_=psA[0:64, 0:128], func=AF.Copy), r=[kU], w=[f"gdnCu{pb}"])
                        GG.op(1, "act", lambda e, psWT=psWT, pb=pb: e.activation(out=wTs[pb][:], in_=psWT, func=AF.Copy), r=[kWT], w=[f"gdnCwT{pb}"])
                        GG.op(2, "pe", lambda e, psA=psA, pb=pb, hv=hv: e.matmul(psA[0:64, 128:256], lhsT=wTs[pb][:], rhs=S16[:, hv, :], start=True, stop=True), r=[f"gdnCwT{pb}", Sk + "b"], w=[kW])
                        GG.op(3, "dve", lambda e, psA=psA, pb=pb: e.tensor_tensor(out=vn[pb][:], in0=usb[pb][:], in1=psA[0:64, 128:256], op=ALU.subtract), r=[kW, f"gdnCu{pb}"], w=[f"gdnCvn{pb}"])
                        if is_lat:
                            GG.op(0, "pe", lambda e, psA=psA, kh=kh: e.matmul(psA[0:64, 384:448], lhsT=kTc[:, kh, :], rhs=qTc[:, kh, :], start=True, stop=True), r=["gdnCkT", "gdnCqT"], w=[kAt])
                            GG.op(1, "dve", lambda e, psA=psA, pb=pb, hv=hv: e.tensor_tensor(out=am[pb][:], in0=psA[0:64, 384:448], in1=dT[:, hv, :], op=ALU.mult), r=[kAt, "gdnCdT"], w=[f"gdnCam{pb}"])

                            def mmo(e, psA=psA, pb=pb, hv=hv):
                                e.matmul(psA[0:64, 256:384], lhsT=am[pb][:], rhs=vn[pb][:], start=True, stop=False)
                                return e.matmul(psA[0:64, 256:384], lhsT=qegT[:, hv, :], rhs=S16[:, hv, :], start=False, stop=True)

                            GG.op(4, "pe", mmo, r=[f"gdnCam{pb}", f"gdnCvn{pb}", f"gdnCqegT{hv // 8}", Sk + "b"], w=[kO])
                            if d == 0:
                                GG.op(5, "act", lambda e, psA=psA, b=b, hv=hv: e.activation(out=ost[b][:, hv, :], in_=psA[0:64, 256:384], func=AF.Copy), r=[kO], w=[f"gdnCo{b}h{hv}"])
                            else:
                                GG.op(5, "dve", lambda e, psA=psA, hv=hv: e.tensor_tensor(out=oft[:, hv * 128:(hv + 1) * 128], in0=oft[:, hv * 128:(hv + 1) * 128], in1=psA[0:64, 256:384], op=ALU.add), r=[kO, "gdnCof"], w=[f"gdnCofh{hv}"])
                        GG.op(4, "pe", lambda e, psSN=psSN, pb=pb, hv=hv: e.matmul(psSN, lhsT=kdc[:, hv, :], rhs=vn[pb][:], start=True, stop=True), r=["gdnCkdc", f"gdnCvn{pb}", "ps3"], w=[kSn])
                        GG.op(5, "dve", lambda e, psSN=psSN, hv=hv: e.scalar_tensor_tensor(out=S[:, hv, :], in0=S[:, hv, :], scalar=egl[:, hv:hv + 1], in1=psSN, op0=ALU.mult, op1=ALU.add), r=[kSn, "gdnCegl", Sk], w=[Sk])
                        GG.op(6, "act", lambda e, hv=hv: e.activation(out=S16[:, hv, :], in_=S[:, hv, :], func=AF.Copy), r=[Sk], w=[Sk + "b"])
                    GG.flush()
                    if not is_lat:
                        continue
                    if d == 0:
                        P.dma("sp", GOF[tok0:tok0 + 64, :], ost[b][:].rearrange("p h c -> p (h c)"), r=[f"gdnCo{b}h{hv}" for hv in range(16)], w=bkeys("GOF", tok0, tok0 + 64))
                        continue
                    ofk = [f"gdnCofh{hv}" for hv in range(16)]
                    o3 = oft[:].rearrange("p (h c) -> p h c", c=128)
                    P.op("pool", lambda e, o3=o3: e.tensor_tensor(out=sq[:], in0=o3, in1=o3, op=ALU.mult), r=ofk, w=["gdnCsq"])
                    P.op("dve", lambda e: e.tensor_reduce(out=ssn[:], in_=sq[:], axis=AX.X, op=ALU.add), r=["gdnCsq"], w=["gdnCss"])
                    P.op("act", lambda e: e.activation(out=ssn[:], in_=ssn[:], func=AF.Sqrt, scale=1.0 / 128, bias=self.epsc[0:64, :]), r=["gdnCss", "epsc"], w=["gdnCss"])
                    P.op("dve", lambda e: e.reciprocal(out=ssn[:], in_=ssn[:]), r=["gdnCss"], w=["gdnCss"])
                    P.op("dve", lambda e, o3=o3: e.tensor_tensor(out=o3, in0=o3, in1=ssn[:].unsqueeze(2).to_broadcast([64, 16, 128]), op=ALU.mult), r=ofk + ["gdnCss"], w=["gdnCof"] + ofk)
                    P.op("pool", lambda e, o3=o3: e.tensor_tensor(out=o3, in0=o3, in1=ngr[:].unsqueeze(1).to_broadcast([64, 16, 128]), op=ALU.mult), r=["gdnCof", "gdnng"], w=["gdnCof"] + ofk)
                    P.op("pool", lambda e: e.tensor_tensor(out=oft[:], in0=oft[:], in1=zt_[:], op=ALU.mult), r=["gdnCof", "gdnCz"], w=["gdnCof"] + ofk)
                    for half in range(2):
                        def tro(e, half=half):
                            ins = None
                            for h8 in range(8):
                                hv = half * 8 + h8
                                ins = e.transpose(out=self.psum[half][:, h8 * 64:(h8 + 1) * 64], in_=oft[:, hv * 128:(hv + 1) * 128], identity=ident[0:64, 0:64])
                            return ins

                        P.op("pe", tro, r=["gdnCof", "identg"], w=[f"ps{half}"])
                        P.op("act", lambda e, half=half: e.activation(out=mT[:, half * 8:(half + 1) * 8, :].rearrange("p a t -> p (a t)"), in_=self.psum[half][:, :512], func=AF.Copy), r=[f"ps{half}"], w=[f"gdnCmT{half}"])
                    t0 = tok0 - LC
                    P.dma("sp", xt[:], xlv[:, :, t0:t0 + 64], r=bkeys("XL", t0, t0 + 64), w=["gdnCxt"])
                    for oc in range(8):
                        psP = self.psum[2 + oc % 2]

                        def mmP(e, psP=psP, oc=oc):
                            ins = None
                            for hv in range(16):
                                ins = e.matmul(psP[:, 0:64], lhsT=wo[:, hv, oc * 128:(oc + 1) * 128], rhs=mT[:, hv, :], start=(hv == 0), stop=(hv == 15))
                            return ins

                        P.op("pe", mmP, r=["gdnwo0", "gdnwo1", "gdnCmT0", "gdnCmT1"], w=BK[2 + oc % 2])
                        xo = xt[:, oc, :] if d == 0 else xt[:, oc, ::-1]
                        P.op("dve", lambda e, psP=psP, oc=oc, xo=xo: e.scalar_tensor_tensor(out=xo, in0=psP[:, 0:64], scalar=self.modv[:, 16 + oc, 0:1], in1=xo, op0=ALU.mult, op1=ALU.add), r=BK[2 + oc % 2] + ["modv", "gdnCxt"], w=["gdnCxt"])
                    P.dma("sp", xlv[:, :, t0:t0 + 64], xt[:], r=["gdnCxt"], w=bkeys("XL", t0, t0 + 64))
            P.barrier()

    def load_inputs(self):
        P = self.P
        xin, cin = self.dram["xT"], self.dram["cT"]
        XL, XC = self.dram["XL"], self.dram["XC"]
        for t0 in range(0, L, 512):
            P.dma("sp", XL[:, t0:t0 + 512], xin[:, t0:t0 + 512], w=bkeys("XL", t0, t0 + 512))
        P.dma("sp", XC[:, :], cin[:, :], w=bkeys("XC", 0, LC))

    def final_phase(self, do_norm=True):
        P = self.P
        outv = self.dram["outT"].rearrange("(kc p) t -> p kc t", p=128)
        XL, XC = self.dram["XL"], self.dram["XC"]
        P.dma("sp", self.dram["outC"][:, :], XC[:, :], r=bkeys("XC", 0, LC), w=["outC"])
        if not do_norm:
            for t0 in range(0, L, 512):
                P.dma("sp", self.dram["outT"][:, t0:t0 + 512], XL[:, t0:t0 + 512], r=bkeys("XL", t0, t0 + 512),
                      w=bkeys("outT", t0, t0 + 512))
            return
        with ExitStack() as st:
            xts = [self.sb(f"nx{i}", [128, 8, 512], F32, st) for i in range(2)]
            sq = self.sb("nsq", [128, 8, 512], F32, st)
            rs = self.sb("nrs", [128, 512], F32, st)
            rinv = self.sb("nrinv", [128, 512], F32, st)
            tmp = [self.sb(f"ntmp{i}", [128, 8, 512], F32, st) for i in range(2)]
            for it, (seg, t0, n) in enumerate(self.tiles(ctx=False)):
                xt = xts[it % 2]
                xk = f"nx{it % 2}"
                tb = tmp[it % 2]
                tk = f"ntmp{it % 2}"
                xv, xname = self.xview(seg)
                P.dma("sp", xt[:, :, :n], xv[:, :, t0:t0 + n], r=bkeys(xname, t0, t0 + n), w=[xk])
                self.rstd_tile((sq, rs, rinv), xt, n, xk)
                P.op("dve", lambda e, xt=xt, n=n, tb=tb: e.tensor_tensor(
                    out=tb[:, :, :n], in0=xt[:, :, :n], in1=rinv[:, :n].unsqueeze(1).to_broadcast([128, 8, n]), op=ALU.mult),
                    r=[xk, "nrinv"], w=[tk])
                P.op("dve", lambda e, n=n, tb=tb: e.tensor_tensor(
                    out=tb[:, :, :n], in0=tb[:, :, :n], in1=self.fng[:].unsqueeze(2).to_broadcast([128, 8, n]), op=ALU.mult),
                    r=[tk, "fng"], w=[tk])
                P.dma("sp", outv[:, :, t0:t0 + n], tb[:, :, :n], r=[tk], w=bkeys("outT", t0, t0 + n))
            P.barrier()

    def declare_io(self):
        self.din("xT", [D, L])
        self.din("cT", [D, LC])
        self.din("mod_w", [DEPTH, D, 6 * D])
        self.din("ffn_w_in", [DEPTH, D, 2 * DFF])
        self.din("ffn_w_out", [DEPTH, DFF, D])
        self.din("i_ffn_cw", [DEPTH, 128, NFC, 3])
        self.din("i_ffn_cb", [DEPTH, 128, NFC])
        self.dout("outT", [D, L])
        self.dout("outC", [D, LC])
        self.dscr("XL", [D, L])
        self.dscr("XC", [D, LC])
        self.dscr("AT", [DFF, NT], BF16)
        kinds = {li % 4 for (li, m, f) in self.cfg["layers"] if m}
        if 0 in kinds:
            self.declare_s5()
        if 1 in kinds:
            self.declare_lru()
        if 2 in kinds:
            self.declare_ret()
        if 3 in kinds:
            self.declare_gdn()


def build(cfg):
    nc = bass.Bass("TRN2", target_bir_lowering=False)
    with ExitStack() as stack:
        k = K(nc, stack, cfg)
        k.declare_io()
        k.setup_common()
        k.load_inputs()
        for (li, do_mixer, do_ffn) in cfg["layers"]:
            k.mod_phase(li)
            if do_mixer:
                k.mixer(li)
            if do_ffn:
                k.ffn_phase(li, ctx=(li < DEPTH - 1))
        for nm in cfg.get("dump", []):
            src = k.dram[nm]
            dst = k.dout("dbg_" + nm, list(src.shape), src.dtype)
            k.P.barrier()
            nrow = src.shape[0]
            step = max(1, nrow // 8)
            for r0 in range(0, nrow, step):
                r1 = min(nrow, r0 + step)
                k.P.dma("sp", dst[r0:r1], src[r0:r1], r=[], w=[("dbg", nm, r0)])
        k.final_phase(do_norm=cfg.get("final", True))
        k.P.finish()
        k.P.emit()
        k.n_ops = k.P.n_ops
    return nc


def col_layout(v, nchunk):
    return np.ascontiguousarray(v.reshape(nchunk, 128).T)


def prep_shared(inp, cfg):
    sh = {}
    kinds = {li % 4 for (li, m, f) in cfg["layers"] if m}
    want = {k: (k in kinds) for k in range(4)}
    sh["mod_w"] = np.ascontiguousarray(inp["mod_w"])
    sh["ffn_w_in"] = np.ascontiguousarray(inp["ffn_w_in"])
    sh["ffn_w_out"] = np.ascontiguousarray(inp["ffn_w_out"])
    sh["i_n1g"] = np.ascontiguousarray(np.stack([col_layout(inp["norm1_g"][i], 8) for i in range(DEPTH)], axis=1))
    sh["i_n2g"] = np.ascontiguousarray(np.stack([col_layout(inp["norm2_g"][i], 8) for i in range(DEPTH)], axis=1))
    sh["i_fng"] = col_layout(inp["final_norm_g"], 8)
    sh["i_modb"] = np.ascontiguousarray(np.stack([col_layout(inp["mod_b"][i], 48) for i in range(DEPTH)], axis=1))
    cw = inp["ffn_conv_w"]
    sh["i_ffn_cw"] = np.ascontiguousarray(cw.reshape(DEPTH, 3, NFC, 128).transpose(0, 3, 2, 1))
    sh["i_ffn_cb"] = np.ascontiguousarray(inp["ffn_conv_b"].reshape(DEPTH, NFC, 128).transpose(0, 2, 1))
    if "s5_lam_re" in inp and want.get(0, True):
        lre, lim, lst = inp["s5_lam_re"][0], inp["s5_lam_im"][0], inp["s5_log_step"][0]
        col = np.zeros((2, 128, 3, 32), np.float32)
        row = np.zeros((2, 3, 4096), np.float32)
        for d in range(2):
            for i, X in enumerate((lre[d], lim[d], np.repeat(lst[d][:, None], 64, axis=1))):
                col[d, :, i, :] = X.reshape(32, 2, 64).transpose(1, 2, 0).reshape(128, 32)
                row[d, i, :] = X.reshape(4096)
        sh["s5_col"], sh["s5_row"] = col, row
        Bp = np.zeros((2, 2, 128, 32, 128), np.float32)
        Cp = np.zeros((2, 2, 128, 32, 128), np.float32)
        Bs = (inp["s5_b_re"][0], inp["s5_b_im"][0])
        Cs = (inp["s5_c_re"][0], inp["s5_c_im"][0])
        for d in range(2):
            for ri in range(2):
                for gp in range(32):
                    for g2 in range(2):
                        g = 2 * gp + g2
                        k0 = (2 * (gp % 4) + g2) * 16
                        Bp[d, ri, k0:k0 + 16, gp, g2 * 64:(g2 + 1) * 64] = Bs[ri][d, g].T
                        Cp[d, ri, g2 * 64:(g2 + 1) * 64, gp, k0:k0 + 16] = Cs[ri][d, g].T
        sh["s5_B"], sh["s5_C"] = Bp, Cp
        sh["s5_tau"] = np.ascontiguousarray(np.broadcast_to(np.arange(1, 129, dtype=np.float32)[None, :], (128, 128)))
        sh["i_s5d"] = col_layout(inp["s5_d"][0], 8)
        sh["i_s5bglu"] = col_layout(inp["s5_b_glu"][0], 16)
        sh["s5_w_glu"] = np.ascontiguousarray(inp["s5_w_glu"][0])
    if want.get(1, False):
        sh["lru_w_in"] = np.ascontiguousarray(inp["lru_w_in"][0])
        sh["lru_w_a"] = np.ascontiguousarray(inp["lru_w_a"][0])
        sh["lru_w_x"] = np.ascontiguousarray(inp["lru_w_x"][0])
        sh["lru_w_out"] = np.ascontiguousarray(inp["lru_w_out"][0])
        sh["i_lru_cw"] = np.ascontiguousarray(inp["lru_conv_w"][0].reshape(4, 8, 128).transpose(2, 1, 0))
        sh["i_lru_cb"] = col_layout(inp["lru_conv_b"][0], 8)
        v = np.stack([inp["lru_b_a"][0], inp["lru_b_x"][0], inp["lru_lam"][0]], axis=0)
        sh["i_lru_vec"] = np.ascontiguousarray(v.reshape(3, 2, 8, 128).transpose(3, 0, 1, 2))
    if want.get(2, False):
        w = inp["ret_w_in"][0]
        sh["ret_w_in"] = np.ascontiguousarray(w)
        perm = np.arange(1024).reshape(8, 128)
        perm = ((perm % 128 + 64) % 128 + (perm // 128) * 128).reshape(-1)
        sh["ret_wk_perm"] = np.ascontiguousarray(w[:, 0:1024][:, perm])
        sh["ret_wq_perm"] = np.ascontiguousarray(w[:, 3072:4096][:, perm])
        sh["ret_w_out"] = np.ascontiguousarray(inp["ret_w_out"][0])
        sh["ret_ng"] = np.ascontiguousarray(np.broadcast_to(inp["ret_norm_g"][0][None, :], (128, 2048)))
        sh.update(ret_consts())
    if want.get(3, False):
        sh["gdn_w_in"] = np.ascontiguousarray(inp["gdn_w_in"][0])
        sh["gdn_w_out"] = np.ascontiguousarray(inp["gdn_w_out"][0])
        sh["i_gdn_cw"] = np.ascontiguousarray(inp["gdn_conv_w"][0].reshape(4, 32, 128).transpose(2, 1, 0))
        sh["gdn_ng"] = np.ascontiguousarray(np.broadcast_to(inp["gdn_norm_g"][0][None, :], (128, 128)))
        ab = np.stack([inp["gdn_a_log"][0], inp["gdn_dt_bias"][0]], axis=0)
        sh["gdn_ab"] = np.ascontiguousarray(np.broadcast_to(ab[None], (128, 2, 2, 16))).astype(np.float32)
        i = np.arange(64)
        c64 = np.zeros((64, 4, 64), np.float32)
        c64[:, 0, :] = (i[:, None] <= i[None, :])
        c64[:, 1, :] = (i[:, None] > i[None, :])
        c64[:, 2, :] = (i[:, None] <= i[None, :])
        c64[:, 3, :] = np.eye(64, dtype=np.float32)[::-1]
        sh["gdn_c64"] = c64
        sh["gdn_ones"] = np.ones((64, 128), np.float32)
        sh["ident"] = np.eye(128, dtype=np.float32)
    return sh


def ret_consts():
    c = {}
    p = np.arange(128)
    inv = 10000.0 ** (-(p % 64).astype(np.float64) / 64.0)
    pos = np.arange(64, dtype=np.float64) - 31.5
    ang = inv[:, None] * pos[None, :]
    sign = np.where(p < 64, -1.0, 1.0)[:, None]
    base = np.stack([np.cos(ang), sign * np.sin(ang), np.cos(ang), sign * np.sin(ang)], axis=0)
    c["ret_rope"] = np.stack([base, base / 16.0], axis=0).astype(np.float32)
    gam = np.array([1.0 - 2.0 ** (-5 - h) for h in range(4)], np.float64)
    lg = np.log(gam)
    idx = np.arange(128, dtype=np.float64)
    DT = np.zeros((2, 128, 4, 128)); XI = np.zeros((2, 128, 4, 128)); ZE = np.zeros((2, 128, 4))
    for h in range(4):
        dif = idx[None, :] - idx[:, None]
        DT[0, :, h, :] = np.where(dif >= 0, np.exp(np.maximum(dif, 0) * lg[h]), 0.0)
        DT[1, :, h, :] = np.where(dif < 0, np.exp(np.maximum(-dif, 0) * lg[h]), 0.0)
        XI[0, :, h, :] = np.exp((idx + 1.0) * lg[h])[None, :]
        XI[1, :, h, :] = np.exp((128.0 - idx) * lg[h])[None, :]
        ZE[0, :, h] = np.exp((127.0 - idx) * lg[h])
        ZE[1, :, h] = np.exp(idx * lg[h])
    c["ret_DT"], c["ret_XI"], c["ret_ZETA"] = DT.astype(np.float32), XI.astype(np.float32), ZE.astype(np.float32)
    c["ident"] = np.eye(128, dtype=np.float32)
    return c


def prep_core(inp, b, x=None, ctx=None):
    x = inp["x"][b] if x is None else x
    ctx = inp["ctx"][b] if ctx is None else ctx
    m = {}
    m["xT"] = np.ascontiguousarray(x.T)
    m["cT"] = np.ascontiguousarray(ctx.T)
    cc = np.stack([col_layout(inp["c"][b], 8), col_layout(inp["c_ctx"], 8)], axis=2)
    m["i_ccol"] = np.ascontiguousarray(cc.astype(np.float32))
    return m


_CACHE = {}


def run(inp, cfg, xs=None, ctxs=None, ncores=8):
    key = repr(cfg)
    if key not in _CACHE:
        _CACHE[key] = build(cfg)
    nc = _CACHE[key]
    sh = prep_shared(inp, cfg)
    in_maps = []
    for b in range(ncores):
        m = dict(sh)
        m.update(prep_core(inp, b, None if xs is None else xs[b], None if ctxs is None else ctxs[b]))
        in_maps.append(m)
    res = run_bass_kernel_spmd(nc, in_maps, core_ids=list(range(ncores)))
    global LAST_RESULTS
    LAST_RESULTS = res.results
    outs = [np.ascontiguousarray(r["outT"].T) for r in res.results]
    outc = [np.ascontiguousarray(r["outC"].T) for r in res.results]
    return np.stack(outs), np.stack(outc)


FULL_CFG = {"layers": [(0, True, True), (1, True, True), (2, True, True), (3, True, True)], "final": True}


def kernel(**inputs):
    inp = {k: np.asarray(v) for k, v in inputs.items()}
    out, _ = run(inp, FULL_CFG)
    return out.astype(np.float32)
```
